# Optimizing a Trainium2 kernel written in Bass

```python
import math
import jax, jax.numpy as jnp
from jax import lax
import numpy as np

D_MODEL = 1024
BATCH = 4
SEQ = 8192
DEPTH = 1

HEAD_DIM = 64
SB_HEADS = 6
MOBA_HEADS = 6
MEM_HEADS = 4
MEM_LEN = 256
SB_W = SB_HEADS * HEAD_DIM
MOBA_W = MOBA_HEADS * HEAD_DIM
MEM_W = MEM_HEADS * HEAD_DIM
SB_QBLOCK = 128
MOBA_BLOCK = 256
MOBA_TOPK = 3
MOBA_QCHUNK = 32
ROPE_THETA = 10000.0
D_FF = -(-8 * D_MODEL // (3 * 256)) * 256
N_BRANCH = 3
EPS = 1e-6
IN_SPLIT_SIZES = (SB_W, SB_W, SB_W, MOBA_W, MOBA_W, MOBA_W, MEM_W, D_MODEL, D_MODEL, D_MODEL)
IN_WIDTH = sum(IN_SPLIT_SIZES)

kernel_name = "hybrid_stickbreak_moba_memory_swiglu"


def rms_norm(x, g):
    xf = x.astype(jnp.float32)
    y = xf * lax.rsqrt(jnp.mean(xf * xf, axis=-1, keepdims=True) + EPS)
    return (y * g.astype(jnp.float32)).astype(x.dtype)


def head_rms(t, g):
    return t * lax.rsqrt(jnp.mean(t * t, axis=-1, keepdims=True) + EPS) * g.astype(jnp.float32)


def split_heads(t, n):
    b, s, _ = t.shape
    return t.reshape(b, s, n, HEAD_DIM).transpose(0, 2, 1, 3).astype(jnp.float32)


def merge_heads(t):
    b, h, s, d = t.shape
    return t.transpose(0, 2, 1, 3).reshape(b, s, h * d)


def rope(t, pos):
    half = HEAD_DIM // 2
    inv_freq = ROPE_THETA ** (-jnp.arange(half, dtype=jnp.float32) * 2.0 / HEAD_DIM)
    ang = pos[:, None] * inv_freq[None, :]
    cos, sin = jnp.cos(ang), jnp.sin(ang)
    t1, t2 = t[..., :half], t[..., half:]
    return jnp.concatenate([t1 * cos - t2 * sin, t2 * cos + t1 * sin], axis=-1)


def stick_breaking_attention(q, k, v):
    b, h, s, d = q.shape
    nqb = s // SB_QBLOCK
    scale = 1.0 / math.sqrt(d)
    qb = q.reshape(b, h, nqb, SB_QBLOCK, d).transpose(2, 0, 1, 3, 4)
    kpos = jnp.arange(s)

    def block(args):
        qi, i = args
        qpos = i * SB_QBLOCK + jnp.arange(SB_QBLOCK)
        z = jnp.einsum('bhqd,bhkd->bhqk', qi, k) * scale
        mask = kpos[None, :] < qpos[:, None]
        log_beta = jax.nn.log_sigmoid(z)
        log_1m = jnp.where(mask, jax.nn.log_sigmoid(-z), 0.0)
        tail = lax.cumsum(log_1m, axis=3, reverse=True) - log_1m
        a = jnp.where(mask, jnp.exp(log_beta + tail), 0.0)
        return jnp.einsum('bhqk,bhkd->bhqd', a, v)

    out = lax.map(block, (qb, jnp.arange(nqb)))
    return out.transpose(1, 2, 0, 3, 4).reshape(b, h, s, d)


def moba_attention(q, k, v):
    b, h, s, d = q.shape
    scale = 1.0 / math.sqrt(d)
    sp = -(-s // MOBA_BLOCK) * MOBA_BLOCK
    pad = ((0, 0), (0, 0), (0, sp - s), (0, 0))
    q, k, v = jnp.pad(q, pad), jnp.pad(k, pad), jnp.pad(v, pad)
    nb = sp // MOBA_BLOCK
    kb = k.reshape(b, h, nb, MOBA_BLOCK, d)
    vb = v.reshape(b, h, nb, MOBA_BLOCK, d)
    kmean = jnp.mean(kb, axis=3)
    pos = jnp.arange(sp)
    qblk = pos // MOBA_BLOCK
    gate = jnp.einsum('bhtd,bhnd->bhtn', q, kmean)
    past = jnp.arange(nb)[None, :] < qblk[:, None]
    gate = jnp.where(past, gate, -jnp.inf)
    topk = min(MOBA_TOPK, nb)
    _, sel = lax.top_k(gate, topk)
    sel_ok = jnp.arange(topk)[None, :] < qblk[:, None]

    nc = sp // MOBA_QCHUNK
    qc = q.reshape(b, h, nc, MOBA_QCHUNK, d).transpose(2, 0, 1, 3, 4)
    selc = sel.reshape(b, h, nc, MOBA_QCHUNK, topk).transpose(2, 0, 1, 3, 4)
    okc = sel_ok.reshape(nc, MOBA_QCHUNK, topk)
    gather = jax.vmap(jax.vmap(lambda blocks, idx: blocks[idx]))

    def chunk(args):
        qi, si, oki, c = args
        qpos = c * MOBA_QCHUNK + jnp.arange(MOBA_QCHUNK)
        own = (c * MOBA_QCHUNK) // MOBA_BLOCK
        k_own = lax.dynamic_index_in_dim(kb, own, axis=2, keepdims=False)
        v_own = lax.dynamic_index_in_dim(vb, own, axis=2, keepdims=False)
        k_sel = gather(kb, si)
        v_sel = gather(vb, si)
        s_sel = jnp.einsum('bhqd,bhqnkd->bhqnk', qi, k_sel) * scale
        s_sel = jnp.where(oki[None, None, :, :, None], s_sel, -jnp.inf)
        s_sel = s_sel.reshape(b, h, MOBA_QCHUNK, topk * MOBA_BLOCK)
        s_own = jnp.einsum('bhqd,bhkd->bhqk', qi, k_own) * scale
        kpos_own = own * MOBA_BLOCK + jnp.arange(MOBA_BLOCK)
        s_own = jnp.where(kpos_own[None, :] <= qpos[:, None], s_own, -jnp.inf)
        p = jax.nn.softmax(jnp.concatenate([s_sel, s_own], axis=-1), axis=-1)
        p_sel = p[..., :topk * MOBA_BLOCK].reshape(b, h, MOBA_QCHUNK, topk, MOBA_BLOCK)
        p_own = p[..., topk * MOBA_BLOCK:]
        return (jnp.einsum('bhqnk,bhqnkd->bhqd', p_sel, v_sel)
                + jnp.einsum('bhqk,bhkd->bhqd', p_own, v_own))

    out = lax.map(chunk, (qc, selc, okc, jnp.arange(nc)))
    out = out.transpose(1, 2, 0, 3, 4).reshape(b, h, sp, d)
    return out[:, :, :s]


def memory_cross_attention(q, k, v):
    scale = 1.0 / math.sqrt(q.shape[-1])
    p = jax.nn.softmax(jnp.einsum('bhsd,bhmd->bhsm', q, k) * scale, axis=-1)
    return jnp.einsum('bhsm,bhmd->bhsd', p, v)


def hybrid_layer(x, mem, mix_norm_g, mem_norm_g, ffn_norm_g, w_in, w_mem_kv,
                 moba_q_norm_g, moba_k_norm_g, mem_q_norm_g, mem_k_norm_g,
                 w_up_sb, w_up_moba, w_up_mem, w_out, w_ffn_in, w_ffn_down):
    s = x.shape[1]
    pos = jnp.arange(s, dtype=jnp.float32)
    h = rms_norm(x, mix_norm_g)
    proj = h @ w_in
    offsets = list(np.cumsum(IN_SPLIT_SIZES)[:-1].tolist())
    (sb_q, sb_k, sb_v, mo_q, mo_k, mo_v, me_q, g_sb, g_mo, g_me) = jnp.split(proj, offsets, axis=-1)

    o_sb = stick_breaking_attention(split_heads(sb_q, SB_HEADS), split_heads(sb_k, SB_HEADS),
                                    split_heads(sb_v, SB_HEADS))
    mq = rope(head_rms(split_heads(mo_q, MOBA_HEADS), moba_q_norm_g), pos)
    mk = rope(head_rms(split_heads(mo_k, MOBA_HEADS), moba_k_norm_g), pos)
    o_mo = moba_attention(mq, mk, split_heads(mo_v, MOBA_HEADS))
    mem_h = rms_norm(mem, mem_norm_g)
    mem_k, mem_v = jnp.split(mem_h @ w_mem_kv, 2, axis=-1)
    cq = head_rms(split_heads(me_q, MEM_HEADS), mem_q_norm_g)
    ck = head_rms(split_heads(mem_k, MEM_HEADS), mem_k_norm_g)
    o_me = memory_cross_attention(cq, ck, split_heads(mem_v, MEM_HEADS))

    dt = x.dtype
    up_sb = (merge_heads(o_sb).astype(dt) @ w_up_sb).astype(jnp.float32)
    up_mo = (merge_heads(o_mo).astype(dt) @ w_up_moba).astype(jnp.float32)
    up_me = (merge_heads(o_me).astype(dt) @ w_up_mem).astype(jnp.float32)
    mix = (jax.nn.sigmoid(g_sb.astype(jnp.float32)) * up_sb
           + jax.nn.sigmoid(g_mo.astype(jnp.float32)) * up_mo
           + jax.nn.sigmoid(g_me.astype(jnp.float32)) * up_me)
    x = x + (mix.astype(dt) @ w_out).astype(dt)

    h2 = rms_norm(x, ffn_norm_g)
    gate, up = jnp.split(h2 @ w_ffn_in, 2, axis=-1)
    ff = (jax.nn.silu(gate.astype(jnp.float32)) * up.astype(jnp.float32)).astype(dt)
    return x + (ff @ w_ffn_down).astype(dt)


def setup_inputs(seed: int = 0) -> dict:
    key = jax.random.key(seed)
    ks = jax.random.split(key, 20)
    f32 = jnp.float32

    def w(k, shape, fan_in):
        return jax.random.normal(k, shape, f32) * (fan_in ** -0.5)

    def gain(k, shape):
        return 1.0 + 0.02 * jax.random.normal(k, shape, f32)

    L = DEPTH
    return {
        "x": jax.random.normal(ks[0], (BATCH, SEQ, D_MODEL), f32),
        "mem": jax.random.normal(ks[1], (BATCH, MEM_LEN, D_MODEL), f32),
        "mix_norm_g": gain(ks[2], (L, D_MODEL)),
        "mem_norm_g": gain(ks[3], (L, D_MODEL)),
        "ffn_norm_g": gain(ks[4], (L, D_MODEL)),
        "w_in": w(ks[5], (L, D_MODEL, IN_WIDTH), D_MODEL),
        "w_mem_kv": w(ks[6], (L, D_MODEL, 2 * MEM_W), D_MODEL),
        "moba_q_norm_g": gain(ks[7], (L, HEAD_DIM)),
        "moba_k_norm_g": gain(ks[8], (L, HEAD_DIM)),
        "mem_q_norm_g": gain(ks[9], (L, HEAD_DIM)),
        "mem_k_norm_g": gain(ks[10], (L, HEAD_DIM)),
        "w_up_sb": w(ks[11], (L, SB_W, D_MODEL), SB_W),
        "w_up_moba": w(ks[12], (L, MOBA_W, D_MODEL), MOBA_W),
        "w_up_mem": w(ks[13], (L, MEM_W, D_MODEL), MEM_W),
        "w_out": w(ks[14], (L, D_MODEL, D_MODEL), D_MODEL),
        "w_ffn_in": w(ks[15], (L, D_MODEL, 2 * D_FF), D_MODEL),
        "w_ffn_down": w(ks[16], (L, D_FF, D_MODEL), D_FF),
    }


def reference(x, mem, mix_norm_g, mem_norm_g, ffn_norm_g, w_in, w_mem_kv,
              moba_q_norm_g, moba_k_norm_g, mem_q_norm_g, mem_k_norm_g,
              w_up_sb, w_up_moba, w_up_mem, w_out, w_ffn_in, w_ffn_down):
    for l in range(DEPTH):
        x = hybrid_layer(x, mem, mix_norm_g[l], mem_norm_g[l], ffn_norm_g[l], w_in[l], w_mem_kv[l],
                         moba_q_norm_g[l], moba_k_norm_g[l], mem_q_norm_g[l], mem_k_norm_g[l],
                         w_up_sb[l], w_up_moba[l], w_up_mem[l], w_out[l], w_ffn_in[l], w_ffn_down[l])
    return x
```

```python
import math
from contextlib import ExitStack

import numpy as np
import ml_dtypes

import concourse.bass as bass
import concourse.mybir as mybir
from concourse.bass_utils import run_bass_kernel_spmd

F32 = mybir.dt.float32
BF16 = mybir.dt.bfloat16
AF = mybir.ActivationFunctionType
ALU = mybir.AluOpType
AX = mybir.AxisListType

D = 1024
HD = 64
DFF = 2816
MEMLEN = 256
INW = 5632
NEG = -30000.0
EPS = 1e-6
N_CORES = 8


class Buf:
    __slots__ = ("name", "w", "r", "psum")

    def __init__(self, name, psum=False):
        self.name = name
        self.w = None
        self.r = []
        self.psum = psum


class Ev:
    __slots__ = ("eng", "seq", "sem", "val", "clock")

    def __init__(self, eng, seq):
        self.eng = eng
        self.seq = seq
        self.sem = None
        self.val = None
        self.clock = None


class Sched:
    def __init__(self, nc, engsems, dmasems):
        self.nc = nc
        self.eng = {"pe": nc.tensor, "act": nc.scalar, "dve": nc.vector,
                    "pool": nc.gpsimd, "sp": nc.sync}
        self.sem = engsems
        self.clock = {k: {} for k in self.eng}
        self.count = {k: 0 for k in self.eng}
        self.seq = {k: 0 for k in self.eng}
        self.pending = {k: [] for k in self.eng}
        self.last = {k: None for k in self.eng}
        self.dma_sems = list(dmasems)
        self.dma_cnt = [0] * len(dmasems)
        self.dma_last = [None] * len(dmasems)
        self.dma_rr = 0
        self.nwaits = 0
        self.nops = 0
        self.defer = None

    def _deps(self, reads, writes):
        deps = []
        for b in reads:
            if b.w is not None:
                deps.append(b.w)
            if b.psum:
                deps.extend(b.r)
        for b in writes:
            if b.w is not None:
                deps.append(b.w)
            deps.extend(b.r)
        return deps

    def _waits(self, e, deps):
        ck = self.clock[e]
        need = {}
        for d in deps:
            if d.eng == "pe" and e == "pe":
                continue
            if ck.get(d.eng, 0) >= d.seq:
                continue
            assert d.val is not None, "dependency on unresolved milestone (%s)" % d.eng
            cur = need.get(d.eng)
            if cur is None or cur.seq < d.seq:
                need[d.eng] = d
        waits = []
        for d in need.values():
            if ck.get(d.eng, 0) >= d.seq:
                continue
            waits.append((d.sem, d.val))
            for k, v in d.clock.items():
                if ck.get(k, 0) < v:
                    ck[k] = v
            if ck.get(d.eng, 0) < d.seq:
                ck[d.eng] = d.seq
        return waits

    def _apply(self, e, waits, make):
        eng = self.eng[e]
        self.nwaits += len(waits)
        for (s, v) in waits[1:]:
            eng.wait_ge(s, v)
        ins = make()
        if waits:
            ins._wait_ge(waits[0][0], waits[0][1])
        return ins

    def _record(self, ev, reads, writes):
        for b in reads:
            b.r.append(ev)
        for b in writes:
            b.w = ev
            b.r = []

    def start_defer(self):
        self.defer = []

    def end_defer(self):
        d, self.defer = self.defer, None
        return d

    def ub(self):
        if self.defer is not None and self.defer and self.defer[-1] is not None:
            self.defer.append(None)

    def emit_deferred(self, q, nunits):
        while nunits > 0 and q:
            it = q.popleft()
            if it is None:
                nunits -= 1
                continue
            kind = it[0]
            if kind == "op":
                self.op(*it[1:])
            else:
                self.dma(*it[1:])

    def op(self, e, make, reads=(), writes=(), inc=True):
        if self.defer is not None:
            self.defer.append(("op", e, make, tuple(reads), tuple(writes), inc))
            return None
        self.nops += 1
        waits = self._waits(e, self._deps(reads, writes))
        ins = self._apply(e, waits, make)
        self.seq[e] += 1
        ev = Ev(e, self.seq[e])
        ev.sem = self.sem[e]
        ev.clock = dict(self.clock[e])
        if inc:
            self.count[e] += 1
            ins.then_inc(self.sem[e], 1)
            ev.val = self.count[e]
            for p in self.pending[e]:
                p.val = ev.val
            self.pending[e] = []
        else:
            self.pending[e].append(ev)
        self.last[e] = ev
        self._record(ev, reads, writes)
        return ev

    def dma(self, make, reads=(), writes=(), q="sp"):
        if self.defer is not None:
            self.defer.append(("dma", make, tuple(reads), tuple(writes), q))
            return None
        self.nops += 1
        j = self.dma_rr
        self.dma_rr = (self.dma_rr + 1) % len(self.dma_sems)
        deps = self._deps(reads, writes)
        if self.dma_last[j] is not None:
            deps.append(self.dma_last[j])
        waits = self._waits(q, deps)
        ins = self._apply(q, waits, make)
        self.dma_cnt[j] += 1
        ins.then_inc(self.dma_sems[j], 16)
        ev = Ev("dma%d" % j, self.dma_cnt[j])
        ev.sem = self.dma_sems[j]
        ev.val = 16 * self.dma_cnt[j]
        ev.clock = dict(self.clock[q])
        self.dma_last[j] = ev
        self._record(ev, reads, writes)
        return ev

    def barrier(self):
        evs = []
        for k in self.eng:
            if self.pending[k]:
                self.op(k, lambda k=k: self.eng[k].drain(), inc=True)
            if self.last[k] is not None and self.last[k].val is not None:
                evs.append(self.last[k])
        for d in self.dma_last:
            if d is not None:
                evs.append(d)
        for e in self.eng:
            waits = self._waits(e, evs)
            self.nwaits += len(waits)
            for (s, v) in waits:
                self.eng[e].wait_ge(s, v)


class Feeder:
    def __init__(self, S, entries, steps):
        from collections import deque
        self.S = S
        self.q = deque(entries)
        self.units = sum(1 for x in entries if x is None) + 1
        self.steps = max(steps, 1)

    def step(self):
        if not self.q:
            return
        n = -(-self.units // self.steps)
        self.steps = max(self.steps - 1, 1)
        self.units = max(self.units - n, 0)
        self.S.emit_deferred(self.q, n)

    def drain(self):
        self.S.emit_deferred(self.q, 1 << 30)


class SbAlloc:
    def __init__(self, nc, limit):
        self.nc = nc
        self.off = 16512
        self.limit = limit
        self.n = 0
        self.peak = 0

    def alloc(self, name, cols, dt):
        isz = 4 if dt == F32 else 2
        nbytes = (cols * isz + 63) // 64 * 64
        off = self.off
        assert off + nbytes <= self.limit, "SBUF overflow at %s: %d + %d > %d" % (name, off, nbytes, self.limit)
        self.off += nbytes
        self.peak = max(self.peak, self.off)
        self.n += 1
        return self.nc.alloc_sbuf_tensor_at("%s_%d" % (name, self.n), [128, cols], dt, offset=off)

    def mark(self):
        return self.off

    def release(self, m):
        self.off = m


class Builder:
    def __init__(self, NT, debug=False, stop_after=4):
        self.stop_after = stop_after
        assert NT % 4 == 0
        self.NT = NT
        self.NS = NT // 2
        self.SEQ = NT * 512
        self.NKB = self.SEQ // 128
        self.OWN = self.NS * 512
        self.debug = debug
        self.nc = bass.Bass("TRN2", target_bir_lowering=False)
        self.bank_rr = 0

    def din(self, name, shape, dt=F32):
        return self.nc.dram_tensor(name, list(shape), dt, kind="ExternalInput").ap()

    def dout(self, name, shape, dt=F32):
        return self.nc.dram_tensor(name, list(shape), dt, kind="ExternalOutput").ap()

    def build(self):
        nc = self.nc
        SEQ, OWN, NS = self.SEQ, self.OWN, self.NS
        self.xseq = self.din("xseq", [SEQ, D])
        self.xown = self.din("xown", [OWN, D])
        self.mem = self.din("mem", [MEMLEN, D])
        self.w_in = self.din("w_in", [D, INW])
        self.w_mem_kv = self.din("w_mem_kv", [D, 512])
        self.w_up_sb = self.din("w_up_sb", [384, D])
        self.w_up_moba = self.din("w_up_moba", [384, D])
        self.w_up_mem = self.din("w_up_mem", [256, D])
        self.w_out = self.din("w_out", [D, D])
        self.w_ffn_in = self.din("w_ffn_in", [D, 2 * DFF])
        self.w_ffn_down = self.din("w_ffn_down", [DFF, D])
        self.g_mix = self.din("mix_norm_g", [D])
        self.g_memn = self.din("mem_norm_g", [D])
        self.g_ffn = self.din("ffn_norm_g", [D])
        self.g_moq = self.din("moba_q_norm_g", [HD])
        self.g_mok = self.din("moba_k_norm_g", [HD])
        self.g_meq = self.din("mem_q_norm_g", [HD])
        self.g_mek = self.din("mem_k_norm_g", [HD])
        self.cosK = self.din("cosK", [128, SEQ])
        self.sinK = self.din("sinK", [128, SEQ])
        self.cosQ = self.din("cosQ", [128, OWN])
        self.sinQ = self.din("sinQ", [128, OWN])
        self.pastb = self.din("pastb", [128, NS * 4 * 192])
        self.sbmask = self.din("sbmask", [2, 128, 8 * 512], BF16)
        self.momask = self.din("momask", [2, 128, 8 * 512], BF16)
        self.out = self.dout("out", [OWN, D])
        if self.debug:
            self.oTs = self.dout("oTs", [8, 128, OWN], BF16)
            self.x1s = self.dout("x1s", [NS * 8, 128, 512], F32)
        else:
            self.oTs = nc.dram_tensor("oTs", [8, 128, OWN], BF16).ap()
            self.x1s = nc.dram_tensor("x1s", [NS * 8, 128, 512], F32).ap()
        self.BoTs = [Buf("oTs%d" % j) for j in range(NS)]
        self.Bx1s = [Buf("x1s%d" % j) for j in range(NS)]

        with ExitStack() as es:
            engsems = {k: es.enter_context(nc.semaphore("s_" + k)) for k in ["pe", "act", "dve", "pool", "sp"]}
            dmasems = [es.enter_context(nc.semaphore("d%d" % i)) for i in range(24)]
            self.S = Sched(nc, engsems, dmasems)
            self.A = SbAlloc(nc, 229376)
            self.banks = [nc.alloc_psum_tensor("bank%d" % i, [128, 512], F32) for i in range(8)]
            self.Bbank = [Buf("bank%d" % i, psum=True) for i in range(8)]
            self.consts()
            m0 = self.A.mark()
            for i, ph in enumerate((self.phase_sb, self.phase_mo, self.phase_f1)):
                if self.stop_after < i + 1:
                    break
                if STRESS and i == 0:
                    continue
                ph()
                self.S.barrier()
                self.A.release(m0)
            if self.stop_after >= 4:
                self.A.release(self.m_consts)
                self.phase_f2()
            self.S.barrier()
        return nc

    def consts(self):
        nc, S, A = self.nc, self.S, self.A
        self.Bconst = Bc = Buf("consts")

        def diag_select(ap, ncols, base, op=ALU.is_equal):
            S.op("pool", lambda: nc.gpsimd.affine_select(out=ap, in_=ap, pattern=[[-1, ncols]], compare_op=op,
                                                         fill=0.0, base=base, channel_multiplier=1),
                 reads=[Bc], writes=[Bc])

        self.ident = A.alloc("ident", 128, F32)
        S.op("pool", lambda: nc.gpsimd.memset(self.ident[:], 1.0), writes=[Bc])
        diag_select(self.ident[:], 128, 0)
        self.identb = A.alloc("identb", 128, BF16)
        S.op("pool", lambda: nc.gpsimd.memset(self.identb[:], 1.0), writes=[Bc])
        diag_select(self.identb[:], 128, 0)
        self.trineg = A.alloc("trineg", 128, BF16)
        S.op("pool", lambda: nc.gpsimd.memset(self.trineg[:], -1.0), writes=[Bc])
        diag_select(self.trineg[:], 128, 0, ALU.is_ge)
        self.negones = A.alloc("negones", 128, BF16)
        S.op("pool", lambda: nc.gpsimd.memset(self.negones[:], -1.0), writes=[Bc])
        self.onesb = A.alloc("onesb", 128, BF16)
        S.op("pool", lambda: nc.gpsimd.memset(self.onesb[:], 1.0), writes=[Bc])
        self.onesf = A.alloc("onesf", 128, F32)
        S.op("pool", lambda: nc.gpsimd.memset(self.onesf[:], 1.0), writes=[Bc])
        self.blk64 = A.alloc("blk64", 128, F32)
        S.op("pool", lambda: nc.gpsimd.memset(self.blk64[:], 0.0), writes=[Bc])
        S.op("pool", lambda: nc.gpsimd.memset(self.blk64[0:64, 0:64], 1.0 / 64), reads=[Bc], writes=[Bc])
        S.op("pool", lambda: nc.gpsimd.memset(self.blk64[64:128, 64:128], 1.0 / 64), reads=[Bc], writes=[Bc])
        self.rot = A.alloc("rot", 128, F32)
        for (c0, val, base) in ((0, -1.0, -32), (32, 1.0, 0), (64, -1.0, -96), (96, 1.0, -64)):
            S.op("pool", lambda c0=c0, val=val: nc.gpsimd.memset(self.rot[:, c0:c0 + 32], val), reads=[Bc], writes=[Bc])
            diag_select(self.rot[:, c0:c0 + 32], 32, base)
        self.cst = A.alloc("cst", 8, F32)
        S.op("pool", lambda: nc.gpsimd.memset(self.cst[:, 0:1], EPS), writes=[Bc])
        S.op("pool", lambda: nc.gpsimd.memset(self.cst[:, 1:2], 1.0), reads=[Bc], writes=[Bc])
        self.esel = A.alloc("esel", 32 * 128, BF16)
        mtmp = A.mark()
        etmp = A.alloc("etmp", 32 * 128, BF16)
        for g in range(3):
            dstt = self.esel if g == 0 else etmp
            v = dstt[:].rearrange("p (n m) -> p n m", n=32)
            S.op("pool", lambda dstt=dstt: nc.gpsimd.memset(dstt[:], 1.0), reads=[Bc], writes=[Bc])
            S.op("pool", lambda v=v, g=g: nc.gpsimd.affine_select(
                out=v, in_=v, pattern=[[-1, 32], [0, 128]], compare_op=ALU.is_equal, fill=0.0, base=-32 * g,
                channel_multiplier=1), reads=[Bc], writes=[Bc])
            if g > 0:
                S.op("pool", lambda: nc.gpsimd.tensor_tensor(out=self.esel[:], in0=self.esel[:], in1=etmp[:], op=ALU.add),
                     reads=[Bc], writes=[Bc])
        S.barrier()
        A.release(mtmp)
        self.gcol = A.alloc("gcol", 24, F32)
        self.Bg = Buf("gains")
        for i, g in enumerate([self.g_mix, self.g_memn, self.g_ffn]):
            S.dma(lambda i=i, g=g: nc.sync.dma_start(out=self.gcol[:, 8 * i:8 * i + 8],
                                                     in_=g.rearrange("(c p) -> p c", p=128),
                                                     allow_slow_non_contiguous=True), writes=[self.Bg])
        self.gh = A.alloc("gh", 8, F32)
        for i, g in enumerate([self.g_moq, self.g_mok, self.g_meq, self.g_mek]):
            g2 = g.rearrange("(p o) -> p o", o=1)
            for half in range(2):
                S.dma(lambda i=i, g2=g2, half=half: nc.sync.dma_start(
                    out=self.gh[64 * half:64 * half + 64, 2 * i:2 * i + 1], in_=g2[0:64, :]), writes=[self.Bg])
                for q in range(2):
                    S.dma(lambda i=i, g2=g2, half=half, q=q: nc.sync.dma_start(
                        out=self.gh[64 * half + 32 * q:64 * half + 32 * q + 32, 2 * i + 1:2 * i + 2],
                        in_=g2[32 * (1 - q):32 * (1 - q) + 32, :]), writes=[self.Bg])
        self.m_consts = A.mark()
        self.xs = [A.alloc("xs%d" % i, D, BF16) for i in range(2)]
        self.Bxs = [Buf("xs%d" % i) for i in range(2)]
        self.stat = A.alloc("stat", 16, F32)
        self.Bstat = [Buf("stat%d" % i) for i in range(2)]
        self.hT = A.alloc("hT", 8 * 512, BF16)
        self.BhT = Buf("hT")
        self.oTslot = A.alloc("oTslot", 8 * 512, BF16)
        self.BoTslot = Buf("oTslot")
        self.sub_rr = 0

    def alloc_x(self, n):
        self.xt = self.A.alloc("xt", n * D, F32)
        self.Bxt = [Buf("xt%d" % i) for i in range(n)]

    def with_stage(self, fn, keep=False):
        m = self.A.mark()
        self.stage = [self.A.alloc("stage%d" % i, 2048, F32) for i in range(2)]
        self.Bstage = [Buf("stage%d" % i) for i in range(2)]
        self.stage_rr = 0
        fn()
        self.S.barrier()
        if not keep:
            self.A.release(m)
        self._stage_mark = m

    def load_w(self, wd, k0, nk, c0, ncols, dst, dstw, dst_c0, Bdst, gcol_off=None):
        nc, S = self.nc, self.S
        for kc in range(nk):
            for cc in range(0, ncols, 2048):
                n = min(2048, ncols - cc)
                si = self.stage_rr
                self.stage_rr ^= 1
                st = self.stage[si]
                Bst = self.Bstage[si]
                S.dma(lambda kc=kc, cc=cc, n=n, st=st: nc.sync.dma_start(
                    out=st[:, 0:n], in_=wd[k0 + kc * 128:k0 + kc * 128 + 128, c0 + cc:c0 + cc + n]), writes=[Bst])
                o0 = kc * dstw + dst_c0 + cc
                if gcol_off is None:
                    S.op("dve", lambda st=st, n=n, o0=o0: nc.vector.tensor_copy(out=dst[:, o0:o0 + n], in_=st[:, 0:n]),
                         reads=[Bst], writes=[Bdst])
                else:
                    S.op("dve", lambda st=st, n=n, o0=o0, kc=kc: nc.vector.tensor_scalar(
                        out=dst[:, o0:o0 + n], in0=st[:, 0:n],
                        scalar1=self.gcol[:, gcol_off + kc:gcol_off + kc + 1], scalar2=None, op0=ALU.mult),
                        reads=[Bst, self.Bg], writes=[Bdst])

    def make_hT(self, xd, row0, nsub, tpbank, keep_x=False):
        nc, S = self.nc, self.S
        Btp = self.Bbank[tpbank]
        tpb = self.banks[tpbank][:].bitcast(BF16)
        hTv = self.hT[:].rearrange("p (c t) -> p c t", c=8)
        for j in range(nsub):
            xi = j if keep_x else (self.sub_rr % 2)
            si = self.sub_rr % 2
            self.sub_rr += 1
            xt = self.xt[:, xi * D:(xi + 1) * D]
            Bx = self.Bxt[xi]
            S.dma(lambda xt=xt, j=j: nc.sync.dma_start(out=xt, in_=xd[row0 + j * 128:row0 + j * 128 + 128, :]), writes=[Bx])
            ss = self.stat[:, 4 * si:4 * si + 1]
            lnv = self.stat[:, 4 * si + 1:4 * si + 2]
            rstd = self.stat[:, 4 * si + 2:4 * si + 3]
            Bs = self.Bstat[si]
            S.op("act", lambda xt=xt, ss=ss, si=si: nc.scalar.activation(out=self.xs[si][:], in_=xt, func=AF.Square,
                                                                        accum_out=ss),
                 reads=[Bx], writes=[self.Bxs[si], Bs])
            S.op("act", lambda ss=ss, lnv=lnv: nc.scalar.activation(out=lnv, in_=ss, func=AF.Ln, bias=self.cst[:, 0:1],
                                                                   scale=1.0 / D), reads=[Bs, self.Bconst], writes=[Bs])
            S.op("act", lambda lnv=lnv, rstd=rstd: nc.scalar.activation(out=rstd, in_=lnv, func=AF.Exp, scale=-0.5),
                 reads=[Bs], writes=[Bs])
            xs = self.xs[si]
            Bxs = self.Bxs[si]
            S.op("dve", lambda xs=xs, xt=xt, rstd=rstd: nc.vector.tensor_scalar(
                out=xs[:], in0=xt, scalar1=rstd, scalar2=None, op0=ALU.mult), reads=[Bx, Bs], writes=[Bxs])
            S.ub()
            for c in range(8):
                S.op("pe", lambda c=c, xs=xs: nc.tensor.transpose(
                    out=tpb[:, c * 128:(c + 1) * 128], in_=xs[:, c * 128:(c + 1) * 128], identity=self.identb[:]),
                    reads=[Bxs, self.Bconst], writes=[Btp], inc=(c == 7))
            S.ub()
            S.op("dve", lambda j=j: nc.vector.tensor_copy(
                out=hTv[:, :, j * 128:(j + 1) * 128], in_=tpb.rearrange("p (c t) -> p c t", c=8)),
                reads=[Btp], writes=[self.BhT])
            S.ub()

    def mm_group(self, bank, ncols, pairs, reads, extra=None, inc=True):
        nc, S = self.nc, self.S
        n = len(pairs) + (len(extra) if extra else 0)
        out = self.banks[bank][:, 0:ncols]
        k = 0
        for (l, r) in pairs:
            k += 1
            S.op("pe", lambda l=l, r=r, k=k: nc.tensor.matmul(out, lhsT=l, rhs=r, start=(k == 1), stop=(k == n)),
                 reads=reads, writes=[self.Bbank[bank]], inc=(inc and k == n))
        for (o, l, r) in (extra or []):
            k += 1
            S.op("pe", lambda o=o, l=l, r=r, k=k: nc.tensor.matmul(o, lhsT=l, rhs=r, start=False, stop=(k == n)),
                 reads=reads, writes=[self.Bbank[bank]], inc=(inc and k == n))
        S.ub()

    def store_oT(self, j, q0, nq):
        nc, S = self.nc, self.S
        src = self.oTslot[:, q0 * 512:(q0 + nq) * 512].rearrange("p (q t) -> p q t", q=nq)
        dst = self.oTs[q0:q0 + nq, :, j * 512:(j + 1) * 512].rearrange("q p t -> p q t")
        S.dma(lambda: nc.sync.dma_start(out=dst, in_=src), reads=[self.BoTslot], writes=[self.BoTs[j]])

    def phase_sb(self):
        nc, S, A = self.nc, self.S, self.A
        SEQ, NKB, NS, NT = self.SEQ, self.NKB, self.NS, self.NT
        self.alloc_x(2)
        wsb = A.alloc("wsb", 8 * 1152, BF16)
        Bw = Buf("wsb")
        KT = A.alloc("KT", 3 * SEQ, BF16)
        V = A.alloc("V", NKB * 384, BF16)
        BKT = [Buf("KT%d" % t) for t in range(NT)]
        BV = [Buf("V%d" % t) for t in range(NT)]
        self.with_stage(lambda: self.load_w(self.w_in, 0, 8, 0, 1152, wsb, 1152, 0, Bw, gcol_off=0))
        mask = A.alloc("mask", 8 * 512, BF16)
        Bmask = Buf("mask")
        QTs = [A.alloc("QT%d" % i, 3 * 512, BF16) for i in range(2)]
        BQTs = [Buf("QT%d" % i) for i in range(2)]
        NE, NL, NA = 3, 3, 2
        eb = [A.alloc("e%d" % i, 512, F32) for i in range(NE)]
        Be = [Buf("e%d" % i) for i in range(NE)]
        Lb = [A.alloc("L%d" % i, 512, BF16) for i in range(NL)]
        BL = [Buf("L%d" % i) for i in range(NL)]
        wb = [A.alloc("w%d" % i, 512, F32) for i in range(2)]
        Bwb = [Buf("w%d" % i) for i in range(2)]
        Ab = [A.alloc("A%d" % i, 512, BF16) for i in range(NA)]
        BA = [Buf("A%d" % i) for i in range(NA)]
        Rb = [A.alloc("R%d" % i, 512, BF16) for i in range(2)]
        BR = [Buf("R%d" % i) for i in range(2)]
        hT = self.hT
        TP, ACC = 6, 7

        def evac(bank, ncols, dst, Bdst, eng="dve"):
            if eng == "dve":
                S.op("dve", lambda: nc.vector.tensor_copy(out=dst, in_=self.banks[bank][:, 0:ncols]),
                     reads=[self.Bbank[bank]], writes=[Bdst])
            else:
                S.op("act", lambda: nc.scalar.copy(out=dst, in_=self.banks[bank][:, 0:ncols]),
                     reads=[self.Bbank[bank]], writes=[Bdst])
            S.ub()

        def proj_kv(t):
            self.make_hT(self.xseq, t * 512, 4, TP)
            for pair in range(3):
                self.mm_group(ACC, 512, [(wsb[:, kc * 1152 + 384 + pair * 128:kc * 1152 + 384 + pair * 128 + 128],
                                          hT[:, kc * 512:(kc + 1) * 512]) for kc in range(8)], [Bw, self.BhT])
                evac(ACC, 512, KT[:, pair * SEQ + t * 512:pair * SEQ + (t + 1) * 512], BKT[t])
            for sub in range(4):
                self.mm_group(ACC, 384, [(hT[:, kc * 512 + sub * 128:kc * 512 + sub * 128 + 128],
                                          wsb[:, kc * 1152 + 768:kc * 1152 + 1152]) for kc in range(8)], [Bw, self.BhT])
                kb = t * 4 + sub
                evac(ACC, 384, V[:, kb * 384:(kb + 1) * 384], BV[t])

        def proj_q(j):
            QT, BQT = QTs[j % 2], BQTs[j % 2]
            self.make_hT(self.xown, j * 512, 4, TP)
            for pair in range(3):
                self.mm_group(ACC, 512, [(wsb[:, kc * 1152 + pair * 128:kc * 1152 + pair * 128 + 128],
                                          hT[:, kc * 512:(kc + 1) * 512]) for kc in range(8)], [Bw, self.BhT])
                evac(ACC, 512, QT[:, pair * 512:(pair + 1) * 512], BQT)

        def attn(j, feeder=None):
            QT, BQT = QTs[j % 2], BQTs[j % 2]
            nj = 8 * (j + 1)
            its = [(h, i) for h in range(6) for i in range(nj)]
            N = len(its)
            zb, Tb, Ob = [0, 1], [2, 3], [4, 5]

            def emit_z(s):
                h, i = its[s]
                kb = nj - 1 - i
                pair, hp = h // 2, h % 2
                bank = zb[s % 2]
                inwin = i < 8
                extra = None
                if inwin:
                    wi = 7 - i
                    extra = [(self.banks[bank][:, :], self.identb[:], mask[:, wi * 512:(wi + 1) * 512])]
                self.mm_group(bank, 512,
                              [(KT[hp * 64:hp * 64 + 64, pair * SEQ + kb * 128:pair * SEQ + kb * 128 + 128],
                                QT[hp * 64:hp * 64 + 64, pair * 512:(pair + 1) * 512])],
                              [BKT[kb // 4], BQT, Bmask, self.Bconst], extra=extra)

            def emit_eL(s):
                bank = zb[s % 2]
                e, L = eb[s % NE], Lb[s % NL]
                S.op("act", lambda: nc.scalar.activation(out=e[:], in_=self.banks[bank][:, :], func=AF.Exp, scale=0.125),
                     reads=[self.Bbank[bank]], writes=[Be[s % NE]])
                S.op("act", lambda: nc.scalar.activation(out=L[:], in_=e[:], func=AF.Ln, bias=self.cst[:, 1:2]),
                     reads=[Be[s % NE], self.Bconst], writes=[BL[s % NL]])

            def emit_T(s):
                h, i = its[s]
                bank = Tb[s % 2]
                pairs = [(self.trineg[:], Lb[s % NL][:])]
                rd = [BL[s % NL], self.Bconst]
                if i > 0:
                    pairs.append((self.negones[:], Rb[(i - 1) % 2][:]))
                    rd.append(BR[(i - 1) % 2])
                self.mm_group(bank, 512, pairs, rd)
                w = wb[s % 2]
                S.op("act", lambda: nc.scalar.activation(out=w[:], in_=self.banks[bank][:, :], func=AF.Exp),
                     reads=[self.Bbank[bank]], writes=[Bwb[s % 2]])
                Aa = Ab[s % NA]
                S.op("dve", lambda: nc.vector.tensor_tensor(out=Aa[:], in0=eb[s % NE][:], in1=w[:], op=ALU.mult),
                     reads=[Be[s % NE], Bwb[s % 2]], writes=[BA[s % NA]])
                if i < nj - 1:
                    if i == 0:
                        S.op("pool", lambda: nc.gpsimd.tensor_copy(out=Rb[0][:], in_=Lb[s % NL][:]),
                             reads=[BL[s % NL]], writes=[BR[0]])
                    else:
                        S.op("pool", lambda: nc.gpsimd.tensor_tensor(out=Rb[i % 2][:], in0=Rb[(i - 1) % 2][:],
                                                                     in1=Lb[s % NL][:], op=ALU.add),
                             reads=[BL[s % NL], BR[(i - 1) % 2]], writes=[BR[i % 2]])

            def emit_AV(s):
                h, i = its[s]
                kb = nj - 1 - i
                pair, hp = h // 2, h % 2
                bank = Ob[h % 2]
                S.op("pe", lambda: nc.tensor.matmul(self.banks[bank][:, :],
                                                    lhsT=V[:, kb * 384 + pair * 128:kb * 384 + pair * 128 + 128],
                                                    rhs=Ab[s % NA][:], start=(i == 0), stop=(i == nj - 1)),
                     reads=[BV[kb // 4], BA[s % NA]], writes=[self.Bbank[bank]], inc=True)
                if i == nj - 1:
                    S.op("dve", lambda: nc.vector.tensor_copy(
                        out=self.oTslot[hp * 64:hp * 64 + 64, pair * 512:(pair + 1) * 512],
                        in_=self.banks[bank][hp * 64:hp * 64 + 64, :]),
                        reads=[self.Bbank[bank]], writes=[self.BoTslot])

            emit_z(0)
            for s in range(N + 2):
                if s + 1 < N:
                    emit_z(s + 1)
                if s < N:
                    emit_eL(s)
                if 1 <= s <= N:
                    emit_T(s - 1)
                if s >= 2:
                    emit_AV(s - 2)
                if feeder is not None:
                    feeder.step()

        proj_kv(0)
        proj_kv(1)
        proj_q(0)
        for j in range(NS):
            S.dma(lambda j=j: nc.sync.dma_start(out=mask[:], in_=self.sbmask[j % 2]), writes=[Bmask])
            feeder = None
            if j + 1 < NS:
                S.start_defer()
                proj_kv(2 * j + 2)
                proj_kv(2 * j + 3)
                proj_q(j + 1)
                feeder = Feeder(S, S.end_defer(), 48 * (j + 1))
            attn(j, feeder)
            if feeder is not None:
                feeder.drain()
            self.store_oT(j, 0, 3)

    def phase_mo(self):
        nc, S, A = self.nc, self.S, self.A
        SEQ, NKB, NS, NT = self.SEQ, self.NKB, self.NS, self.NT
        self.alloc_x(2)
        WW = 1408
        wmo = A.alloc("wmo", 8 * WW, BF16)
        Bw = Buf("wmo")
        KT = A.alloc("KT", 3 * SEQ, BF16)
        V = A.alloc("V", NKB * 384, BF16)
        BKT = [Buf("KT%d" % t) for t in range(NT)]
        BV = [Buf("V%d" % t) for t in range(NT)]
        ksum = A.alloc("ksum", 6 * 32, F32)
        Bks = Buf("ksum")
        S.op("pool", lambda: nc.gpsimd.memset(ksum[:], 0.0), writes=[Bks])
        memKT = A.alloc("memKT", 2 * 256, BF16)
        memV = A.alloc("memV", 2 * 256, BF16)
        Bmem = Buf("memkv")

        wmem = V
        Bwm = Buf("wmem")

        def loadw():
            self.load_w(self.w_in, 0, 8, 1152, 1408, wmo, WW, 0, Bw, gcol_off=0)
            self.load_w(self.w_mem_kv, 0, 8, 0, 512, wmem, 512, 0, Bwm, gcol_off=8)
        self.with_stage(loadw)

        mask = A.alloc("mask", 8 * 512, BF16)
        Bmask = Buf("mask")
        QTz = A.alloc("QTz", 6 * 512, BF16)
        Qf = A.alloc("Qf", 3 * 512, F32)
        QTmz = A.alloc("QTmz", 4 * 512, BF16)
        BQTz, BQf, BQTmz = Buf("QTz"), Buf("Qf"), Buf("QTmz")
        S.op("pool", lambda: nc.gpsimd.memset(QTz[:], 0.0), writes=[BQTz])
        S.op("pool", lambda: nc.gpsimd.memset(QTmz[:], 0.0), writes=[BQTmz])
        pb = A.alloc("pb", 4 * 192, F32)
        Bpb = Buf("pb")
        MBT = A.alloc("MBT", 1024, BF16)
        BMBT = Buf("MBT")
        NP = 3
        pbuf = [A.alloc("p%d" % i, 512, BF16) for i in range(NP)]
        Bp = [Buf("p%d" % i) for i in range(NP)]
        cs = A.alloc("cs", 1024, F32)
        Bcs = Buf("cs")
        rawsb = A.alloc("rawsb", 512, F32)
        rstd = A.alloc("rstd", 512, F32)
        ta = A.alloc("ta", 512, F32)
        tb = A.alloc("tb", 512, F32)
        sq = tb
        Braw, Brstd, Bta, Btb = Buf("rawsb"), Buf("rstd"), Buf("ta"), Buf("tb")
        Bsq = Btb
        Gp = A.alloc("Gp", 192, F32)
        sel = Gp
        MBb = A.alloc("MBb", 192, BF16)
        mx = A.alloc("mx", 56, F32)
        BGp, BMBb, Bmx = Buf("Gp"), Buf("MBb"), Buf("mx")
        Bsel = BGp
        rec = rstd
        Brec = Brstd
        acc = A.alloc("acc", 512, F32)
        Bacc = Buf("acc")
        hT = self.hT
        TP, ACC, MSB, ROTB = 6, 7, 5, 4

        def norm_qk(ncols, gi, rope, dst_bf, Bdst, dst_f=None, Bdstf=None):
            acc = self.banks[ACC][:, 0:ncols]
            gA = self.gh[:, 2 * gi:2 * gi + 1]
            gB = self.gh[:, 2 * gi + 1:2 * gi + 2]
            S.op("act", lambda: nc.scalar.copy(out=rawsb[:, 0:ncols], in_=acc), reads=[self.Bbank[ACC]], writes=[Braw])
            S.op("act", lambda: nc.scalar.activation(out=sq[:, 0:ncols], in_=acc, func=AF.Square),
                 reads=[self.Bbank[ACC]], writes=[Bsq])
            S.ub()
            self.mm_group(MSB, ncols, [(self.blk64[:], sq[:, 0:ncols])], [Bsq, self.Bconst])
            if rope and "norot" not in VAR:
                self.mm_group(ROTB, ncols, [(self.rot[:], rawsb[:, 0:ncols])], [Braw, self.Bconst])
            S.op("act", lambda: nc.scalar.activation(out=rstd[:, 0:ncols], in_=self.banks[MSB][:, 0:ncols], func=AF.Ln,
                                                     bias=self.cst[:, 0:1]), reads=[self.Bbank[MSB], self.Bconst],
                 writes=[Brstd])
            S.op("act", lambda: nc.scalar.activation(out=rstd[:, 0:ncols], in_=rstd[:, 0:ncols], func=AF.Exp, scale=-0.5),
                 reads=[Brstd], writes=[Brstd])
            S.ub()
            if rope:
                S.op("dve", lambda: nc.vector.scalar_tensor_tensor(out=ta[:, 0:ncols], in0=rawsb[:, 0:ncols], scalar=gA,
                                                                   in1=cs[:, 0:ncols], op0=ALU.mult, op1=ALU.mult),
                     reads=[Braw, Bcs, self.Bg], writes=[Bta])
                S.op("dve", lambda: nc.vector.scalar_tensor_tensor(out=tb[:, 0:ncols], in0=self.banks[ROTB][:, 0:ncols],
                                                                   scalar=gB, in1=cs[:, 512:512 + ncols],
                                                                   op0=ALU.mult, op1=ALU.mult),
                     reads=[self.Bbank[ROTB], Bcs, self.Bg], writes=[Btb])
                S.ub()
                pe_ = "dve" if "nopool" in VAR else "pool"
                pen_ = nc.vector if "nopool" in VAR else nc.gpsimd
                S.op(pe_, lambda: pen_.tensor_tensor(out=ta[:, 0:ncols], in0=ta[:, 0:ncols], in1=tb[:, 0:ncols],
                                                     op=ALU.add), reads=[Bta, Btb], writes=[Bta])
                S.ub()
                fin = dst_f if dst_f is not None else tb[:, 0:ncols]
                Bfin = Bdstf if dst_f is not None else Btb
                S.op("dve", lambda: nc.vector.tensor_tensor(out=fin, in0=ta[:, 0:ncols], in1=rstd[:, 0:ncols], op=ALU.mult),
                     reads=[Bta, Brstd], writes=[Bfin])
            else:
                fin, Bfin = ta[:, 0:ncols], Bta
                S.op("dve", lambda: nc.vector.scalar_tensor_tensor(out=fin, in0=rawsb[:, 0:ncols], scalar=gA,
                                                                   in1=rstd[:, 0:ncols], op0=ALU.mult, op1=ALU.mult),
                     reads=[Braw, Brstd, self.Bg], writes=[Bfin])
            S.ub()
            if callable(dst_bf):
                for hp in range(2):
                    S.op("pool", lambda hp=hp: nc.gpsimd.tensor_copy(out=dst_bf(hp), in_=fin[hp * 64:hp * 64 + 64, :]),
                         reads=[Bfin], writes=[Bdst])
            else:
                S.op("pool", lambda: nc.gpsimd.tensor_copy(out=dst_bf, in_=fin), reads=[Bfin], writes=[Bdst])
            S.ub()
            return fin, Bfin

        def evac(bank, ncols, dst, Bdst):
            S.op("dve", lambda: nc.vector.tensor_copy(out=dst, in_=self.banks[bank][:, 0:ncols]),
                 reads=[self.Bbank[bank]], writes=[Bdst])
            S.ub()

        self.make_hT(self.mem, 0, 2, TP)
        for mp in range(2):
            self.mm_group(ACC, 256, [(wmem[:, kc * 512 + mp * 128:kc * 512 + mp * 128 + 128],
                                      hT[:, kc * 512:kc * 512 + 256]) for kc in range(8)], [Bwm, self.BhT])
            norm_qk(256, 3, False, memKT[:, mp * 256:(mp + 1) * 256], Bmem)
        for sub in range(2):
            self.mm_group(ACC, 256, [(hT[:, kc * 512 + sub * 128:kc * 512 + sub * 128 + 128],
                                      wmem[:, kc * 512 + 256:kc * 512 + 512]) for kc in range(8)], [Bwm, self.BhT])
            evac(ACC, 256, memV[:, sub * 256:(sub + 1) * 256], Bmem)

        S.barrier()

        def proj_kv(t):
            self.make_hT(self.xseq, t * 512, 4, TP)
            S.dma(lambda: nc.sync.dma_start(out=cs[:, 0:512], in_=self.cosK[:, t * 512:(t + 1) * 512]), writes=[Bcs])
            S.dma(lambda: nc.sync.dma_start(out=cs[:, 512:1024], in_=self.sinK[:, t * 512:(t + 1) * 512]), writes=[Bcs])
            VB = 3

            def vproj(sub):
                self.mm_group(VB, 384, [(hT[:, kc * 512 + sub * 128:kc * 512 + sub * 128 + 128],
                                         wmo[:, kc * WW + 768:kc * WW + 1152]) for kc in range(8)], [Bw, self.BhT])

            def vevac(sub):
                kb = t * 4 + sub
                evac(VB, 384, V[:, kb * 384:(kb + 1) * 384], BV[t])

            for pair in range(3):
                self.mm_group(ACC, 512, [(wmo[:, kc * WW + 384 + pair * 128:kc * WW + 384 + pair * 128 + 128],
                                          hT[:, kc * 512:(kc + 1) * 512]) for kc in range(8)], [Bw, self.BhT])
                vproj(pair)
                fin, Bfin = norm_qk(512, 1, True, KT[:, pair * SEQ + t * 512:pair * SEQ + (t + 1) * 512], BKT[t])
                vevac(pair)
                for hp in range(2):
                    h = 2 * pair + hp
                    S.op("dve", lambda h=h, hp=hp, fin=fin: nc.vector.tensor_reduce(
                        out=ksum[hp * 64:hp * 64 + 64, h * 32 + 2 * t:h * 32 + 2 * t + 2],
                        in_=fin[hp * 64:hp * 64 + 64, :].rearrange("p (b k) -> p b k", b=2), axis=AX.X, op=ALU.add),
                        reads=[Bfin], writes=[Bks])
                S.ub()
            vproj(3)
            vevac(3)

        def proj_q(j):
            self.make_hT(self.xown, j * 512, 4, TP)
            S.dma(lambda: nc.sync.dma_start(out=cs[:, 0:512], in_=self.cosQ[:, j * 512:(j + 1) * 512]), writes=[Bcs])
            S.dma(lambda: nc.sync.dma_start(out=cs[:, 512:1024], in_=self.sinQ[:, j * 512:(j + 1) * 512]), writes=[Bcs])
            for pair in range(3):
                self.mm_group(ACC, 512, [(wmo[:, kc * WW + pair * 128:kc * WW + pair * 128 + 128],
                                          hT[:, kc * 512:(kc + 1) * 512]) for kc in range(8)], [Bw, self.BhT])
                norm_qk(512, 0, True,
                        lambda hp, pair=pair: QTz[hp * 64:hp * 64 + 64, (2 * pair + hp) * 512:(2 * pair + hp + 1) * 512],
                        BQTz, dst_f=Qf[:, pair * 512:(pair + 1) * 512], Bdstf=BQf)
            for mp in range(2):
                self.mm_group(ACC, 512, [(wmo[:, kc * WW + 1152 + mp * 128:kc * WW + 1152 + mp * 128 + 128],
                                          hT[:, kc * 512:(kc + 1) * 512]) for kc in range(8)], [Bw, self.BhT])
                norm_qk(512, 2, False,
                        lambda hp, mp=mp: QTmz[hp * 64:hp * 64 + 64, (2 * mp + hp) * 512:(2 * mp + hp + 1) * 512], BQTmz)

        def gate_select(j):
            GB = 4
            tpb = self.banks[TP][:].bitcast(BF16)
            S.dma(lambda: nc.sync.dma_start(out=pb[:], in_=self.pastb[:, j * 768:(j + 1) * 768]), writes=[Bpb])
            for sub in range(4):
                for h in range(6):
                    pair, hp = h // 2, h % 2
                    S.op("pe", lambda h=h, pair=pair, hp=hp: nc.tensor.matmul(
                        self.banks[GB][:, h * 32:(h + 1) * 32],
                        lhsT=Qf[:, pair * 512 + sub * 128:pair * 512 + sub * 128 + 128],
                        rhs=ksum[:, h * 32:(h + 1) * 32], start=True, stop=True),
                        reads=[BQf, Bks], writes=[self.Bbank[GB]], inc=(h == 5))
                S.op("dve", lambda: nc.vector.tensor_tensor(out=Gp[:], in0=self.banks[GB][:, 0:192],
                                                            in1=pb[:, sub * 192:(sub + 1) * 192], op=ALU.add),
                     reads=[self.Bbank[GB], Bpb], writes=[BGp])
                for h in range(6):
                    S.op("dve", lambda h=h: nc.vector.max(out=mx[:, h * 8:(h + 1) * 8], in_=Gp[:, h * 32:(h + 1) * 32]),
                         reads=[BGp], writes=[Bmx])
                S.op("dve", lambda: nc.vector.tensor_scalar(
                    out=mx[:, 48:54], in0=mx[:, 0:48].rearrange("p (h e) -> p h e", e=8)[:, :, 3],
                    scalar1=-1e29, scalar2=None, op0=ALU.max), reads=[Bmx], writes=[Bmx])
                for h in range(6):
                    S.op("dve", lambda h=h: nc.vector.tensor_scalar(
                        out=sel[:, h * 32:(h + 1) * 32], in0=Gp[:, h * 32:(h + 1) * 32],
                        scalar1=mx[:, 48 + h:49 + h], scalar2=None, op0=ALU.is_ge), reads=[BGp, Bmx], writes=[Bsel])
                S.op("dve", lambda: nc.vector.tensor_scalar(out=MBb[:], in0=sel[:], scalar1=-1.0, scalar2=-NEG,
                                                            op0=ALU.add, op1=ALU.mult), reads=[Bsel], writes=[BMBb])
                for g in range(2):
                    S.op("pe", lambda g=g: nc.tensor.transpose(
                        out=tpb[0:96, g * 512 + sub * 128:g * 512 + sub * 128 + 128],
                        in_=MBb[:, g * 96:(g + 1) * 96], identity=self.identb[:]),
                        reads=[BMBb, self.Bconst], writes=[self.Bbank[TP]], inc=(g == 1))
            S.op("dve", lambda: nc.vector.tensor_copy(out=MBT[0:96, :], in_=tpb[0:96, :]),
                 reads=[self.Bbank[TP]], writes=[BMBT])

        def attn(j, kind, feeder=None):
            if kind == "mo":
                nj = 8 * (j + 1)
                nh = 6
            else:
                nj = 2
                nh = 4
            its = [(h, kb) for h in range(nh) for kb in range(nj)]
            N = len(its)
            zb, Ob, Lbk = [0, 1, 2], [3, 4], [5, 5]
            NZ = 3

            def emit_z(s):
                h, kb = its[s]
                pair, hp = h // 2, h % 2
                bank = zb[s % NZ]
                out = self.banks[bank][:, :]
                if kind == "mo":
                    pairs = [(KT[:, pair * SEQ + kb * 128:pair * SEQ + kb * 128 + 128],
                              QTz[:, h * 512:(h + 1) * 512])]
                    r3 = 32 * (h % 3)
                    extra = [(out, self.esel[r3:r3 + 32, (kb // 2) * 128:(kb // 2) * 128 + 128],
                              MBT[r3:r3 + 32, (h // 3) * 512:(h // 3) * 512 + 512])]
                    if kb >= nj - 8:
                        wi = kb - (nj - 8)
                        extra.append((out, self.identb[:], mask[:, wi * 512:(wi + 1) * 512]))
                    self.mm_group(bank, 512, pairs, [BKT[kb // 4], BQTz, BMBT, Bmask, self.Bconst], extra=extra)
                else:
                    pairs = [(memKT[:, pair * 256 + kb * 128:pair * 256 + kb * 128 + 128],
                              QTmz[:, h * 512:(h + 1) * 512])]
                    self.mm_group(bank, 512, pairs, [Bmem, BQTmz])

            def emit_p(s):
                bank = zb[s % NZ]
                p = pbuf[s % NP]
                S.op("act", lambda: nc.scalar.activation(out=p[:], in_=self.banks[bank][:, :], func=AF.Exp, scale=0.125),
                     reads=[self.Bbank[bank]], writes=[Bp[s % NP]])

            def emit_AV(s):
                h, kb = its[s]
                pair, hp = h // 2, h % 2
                ob, lb = Ob[h % 2], Lbk[h % 2]
                p = pbuf[s % NP]
                if kind == "mo":
                    vap = V[:, kb * 384 + pair * 128:kb * 384 + pair * 128 + 128]
                    rd = [BV[kb // 4], Bp[s % NP]]
                    q = 3 + pair
                else:
                    vap = memV[:, kb * 256 + pair * 128:kb * 256 + pair * 128 + 128]
                    rd = [Bmem, Bp[s % NP]]
                    q = 6 + pair
                last = (kb == nj - 1)
                S.op("pe", lambda: nc.tensor.matmul(self.banks[ob][:, :], lhsT=vap, rhs=p[:], start=(kb == 0), stop=last),
                     reads=rd, writes=[self.Bbank[ob]], inc=True)
                if kb == 0:
                    S.op("dve", lambda: nc.vector.tensor_copy(out=acc[:], in_=p[:]), reads=[Bp[s % NP]], writes=[Bacc])
                else:
                    S.op("dve", lambda: nc.vector.tensor_tensor(out=acc[:], in0=acc[:], in1=p[:], op=ALU.add),
                         reads=[Bp[s % NP], Bacc], writes=[Bacc])
                if last:
                    S.op("pe", lambda: nc.tensor.matmul(self.banks[lb][:, :], lhsT=self.onesf[:], rhs=acc[:], start=True,
                                                        stop=True),
                         reads=[Bacc, self.Bconst], writes=[self.Bbank[lb]], inc=True)
                    r0 = hp * 64
                    S.op("dve", lambda: nc.vector.reciprocal(out=rec[r0:r0 + 64, :], in_=self.banks[lb][r0:r0 + 64, :]),
                         reads=[self.Bbank[lb]], writes=[Brec])
                    S.op("dve", lambda: nc.vector.tensor_tensor(out=self.oTslot[r0:r0 + 64, q * 512:(q + 1) * 512],
                                                                in0=self.banks[ob][r0:r0 + 64, :], in1=rec[r0:r0 + 64, :],
                                                                op=ALU.mult),
                         reads=[self.Bbank[ob], Brec], writes=[self.BoTslot])

            emit_z(0)
            if N > 1:
                emit_z(1)
            for s in range(N + 1):
                if s + 2 < N:
                    emit_z(s + 2)
                if s < N:
                    emit_p(s)
                if s >= 1:
                    emit_AV(s - 1)
                if feeder is not None:
                    feeder.step()

        if STRESS:
            for rep in range(STRESS):
                proj_kv(rep % NT)
            return
        for j in range(NS):
            S.dma(lambda j=j: nc.sync.dma_start(out=mask[:], in_=self.momask[j % 2]), writes=[Bmask])
            proj_kv(2 * j)
            proj_kv(2 * j + 1)
            proj_q(j)
            gate_select(j)
            attn(j, "mo")
            attn(j, "me")
            self.store_oT(j, 3, 5)

    def phase_f1(self):
        nc, S, A = self.nc, self.S, self.A
        NS = self.NS
        self.alloc_x(4)
        wg = A.alloc("wg", 8 * 3072, BF16)
        wup = A.alloc("wup", 8 * 1024, BF16)
        wout = A.alloc("wout", 8 * 1024, BF16)
        Bwg, Bwup, Bwout = Buf("wg"), Buf("wup"), Buf("wout")

        def loadw():
            self.load_w(self.w_in, 0, 8, 2560, 3072, wg, 3072, 0, Bwg, gcol_off=0)
            self.load_w(self.w_up_sb, 0, 3, 0, 1024, wup, 1024, 0, Bwup)
            self.load_w(self.w_up_moba, 0, 3, 0, 1024, wup, 1024, 3 * 1024, Bwup)
            self.load_w(self.w_up_mem, 0, 2, 0, 1024, wup, 1024, 6 * 1024, Bwup)
            self.load_w(self.w_out, 0, 8, 0, 1024, wout, 1024, 0, Bwout)
        self.with_stage(loadw)
        oT = A.alloc("oT", 8 * 512, BF16)
        BoT = Buf("oT")
        sg = [A.alloc("sg%d" % i, 512, F32) for i in range(3)]
        tt = [A.alloc("tt%d" % i, 512, F32) for i in range(3)]
        Bsg = [Buf("sg%d" % i) for i in range(3)]
        Btt = [Buf("tt%d" % i) for i in range(3)]
        mixb = A.alloc("mixb", 8 * 512, BF16)
        Bmix = [Buf("mix%d" % i) for i in range(8)]
        x1c = [A.alloc("x1c%d" % i, 512, F32) for i in range(2)]
        Bx1c = [Buf("x1c%d" % i) for i in range(2)]
        hT = self.hT
        TP = 7
        rr = [0]

        def nb():
            b = rr[0] % 7
            rr[0] += 1
            return b

        branches = [(0, [0, 1, 2]), (1, [3, 4, 5]), (2, [6, 7])]
        for j in range(NS):
            src = self.oTs[:, :, j * 512:(j + 1) * 512].rearrange("q p t -> p q t")
            S.dma(lambda src=src: nc.sync.dma_start(out=oT[:].rearrange("p (q t) -> p q t", q=8), in_=src),
                  reads=[self.BoTs[j]], writes=[BoT])
            self.make_hT(self.xown, j * 512, 4, TP, keep_x=True)
            for c in range(8):
                for (b, qs) in branches:
                    gbk = nb()
                    self.mm_group(gbk, 512, [(wg[:, kc * 3072 + b * 1024 + c * 128:kc * 3072 + b * 1024 + c * 128 + 128],
                                              hT[:, kc * 512:(kc + 1) * 512]) for kc in range(8)], [Bwg, self.BhT])
                    ubk = nb()
                    self.mm_group(ubk, 512, [(wup[:, q * 1024 + c * 128:q * 1024 + c * 128 + 128],
                                              oT[:, q * 512:(q + 1) * 512]) for q in qs], [Bwup, BoT])
                    S.op("act", lambda b=b, gbk=gbk: nc.scalar.activation(out=sg[b][:], in_=self.banks[gbk][:, :],
                                                                          func=AF.Sigmoid),
                         reads=[self.Bbank[gbk]], writes=[Bsg[b]])
                    S.op("dve", lambda b=b, ubk=ubk: nc.vector.tensor_tensor(out=tt[b][:], in0=sg[b][:],
                                                                             in1=self.banks[ubk][:, :], op=ALU.mult),
                         reads=[Bsg[b], self.Bbank[ubk]], writes=[Btt[b]])
                S.op("pool", lambda: nc.gpsimd.tensor_tensor(out=tt[0][:], in0=tt[0][:], in1=tt[1][:], op=ALU.add),
                     reads=[Btt[0], Btt[1]], writes=[Btt[0]])
                S.op("pool", lambda c=c: nc.gpsimd.tensor_tensor(out=mixb[:, c * 512:(c + 1) * 512], in0=tt[0][:],
                                                                 in1=tt[2][:], op=ALU.add),
                     reads=[Btt[0], Btt[2]], writes=[Bmix[c]])
            for c in range(8):
                bk = nb()
                extra = [(self.banks[bk][:, sub * 128:(sub + 1) * 128],
                          self.xt[:, sub * D + c * 128:sub * D + c * 128 + 128], self.ident[:]) for sub in range(4)]
                self.mm_group(bk, 512, [(wout[:, k * 1024 + c * 128:k * 1024 + c * 128 + 128],
                                         mixb[:, k * 512:(k + 1) * 512]) for k in range(8)],
                              [Bwout, self.Bconst] + Bmix + self.Bxt, extra=extra)
                xc = x1c[c % 2]
                S.op("act", lambda bk=bk, xc=xc: nc.scalar.copy(out=xc[:], in_=self.banks[bk][:, :]),
                     reads=[self.Bbank[bk]], writes=[Bx1c[c % 2]])
                S.dma(lambda xc=xc, c=c, j=j: nc.sync.dma_start(out=self.x1s[j * 8 + c], in_=xc[:]),
                      reads=[Bx1c[c % 2]], writes=[self.Bx1s[j]])

    def phase_f2(self):
        nc, S, A = self.nc, self.S, self.A
        NS = self.NS
        NF = DFF // 128
        wfi = A.alloc("wfi", 8 * 2 * DFF, BF16)
        wfd = A.alloc("wfd", NF * 1024, BF16)
        Bwfi, Bwfd = Buf("wfi"), Buf("wfd")

        def loadw():
            self.load_w(self.w_ffn_in, 0, 8, 0, 2 * DFF, wfi, 2 * DFF, 0, Bwfi, gcol_off=16)
            self.load_w(self.w_ffn_down, 0, NF, 0, 1024, wfd, 1024, 0, Bwfd)
        self.with_stage(loadw)
        x1T = A.alloc("x1T", 8 * 512, F32)
        Bx1 = [Buf("x1T%d" % c) for c in range(8)]
        sq = [A.alloc("sq%d" % i, 512, F32) for i in range(2)]
        Bsq = [Buf("sq%d" % i) for i in range(2)]
        rstd = A.alloc("rstd", 512, F32)
        Brstd = Buf("rstd")
        h2T = A.alloc("h2T", 8 * 512, BF16)
        Bh2 = Buf("h2T")
        ffT = A.alloc("ffT", NF * 512, BF16)
        Bff = [Buf("ff%d" % f) for f in range(NF)]
        sl = [A.alloc("sl%d" % i, 512, F32) for i in range(2)]
        Bsl = [Buf("sl%d" % i) for i in range(2)]
        ot = A.alloc("ot", 1024, F32)
        Bot = Buf("ot")
        SSB = 7
        rr = [0]

        def nb():
            b = rr[0] % 7
            rr[0] += 1
            return b

        for j in range(NS):
            for c in range(8):
                S.dma(lambda c=c, j=j: nc.sync.dma_start(out=x1T[:, c * 512:(c + 1) * 512], in_=self.x1s[j * 8 + c]),
                      reads=[self.Bx1s[j]], writes=[Bx1[c]])
            for c in range(8):
                S.op("act", lambda c=c: nc.scalar.activation(out=sq[c % 2][:], in_=x1T[:, c * 512:(c + 1) * 512],
                                                             func=AF.Square), reads=[Bx1[c]], writes=[Bsq[c % 2]])
                S.op("pe", lambda c=c: nc.tensor.matmul(self.banks[SSB][:, :], lhsT=self.onesf[:], rhs=sq[c % 2][:],
                                                        start=(c == 0), stop=(c == 7)),
                     reads=[Bsq[c % 2], self.Bconst], writes=[self.Bbank[SSB]], inc=True)
            S.op("act", lambda: nc.scalar.activation(out=rstd[:], in_=self.banks[SSB][:, :], func=AF.Ln,
                                                     bias=self.cst[:, 0:1], scale=1.0 / D),
                 reads=[self.Bbank[SSB], self.Bconst], writes=[Brstd])
            S.op("act", lambda: nc.scalar.activation(out=rstd[:], in_=rstd[:], func=AF.Exp, scale=-0.5),
                 reads=[Brstd], writes=[Brstd])
            for c in range(8):
                S.op("dve", lambda c=c: nc.vector.tensor_tensor(out=h2T[:, c * 512:(c + 1) * 512],
                                                                in0=x1T[:, c * 512:(c + 1) * 512], in1=rstd[:],
                                                                op=ALU.mult), reads=[Bx1[c], Brstd], writes=[Bh2])
            for f in range(NF):
                gbk = nb()
                self.mm_group(gbk, 512, [(wfi[:, kc * 2 * DFF + f * 128:kc * 2 * DFF + f * 128 + 128],
                                          h2T[:, kc * 512:(kc + 1) * 512]) for kc in range(8)], [Bwfi, Bh2])
                ubk = nb()
                self.mm_group(ubk, 512, [(wfi[:, kc * 2 * DFF + DFF + f * 128:kc * 2 * DFF + DFF + f * 128 + 128],
                                          h2T[:, kc * 512:(kc + 1) * 512]) for kc in range(8)], [Bwfi, Bh2])
                S.op("act", lambda f=f, gbk=gbk: nc.scalar.activation(out=sl[f % 2][:], in_=self.banks[gbk][:, :],
                                                                      func=AF.Silu),
                     reads=[self.Bbank[gbk]], writes=[Bsl[f % 2]])
                S.op("dve", lambda f=f, ubk=ubk: nc.vector.tensor_tensor(out=ffT[:, f * 512:(f + 1) * 512], in0=sl[f % 2][:],
                                                                         in1=self.banks[ubk][:, :], op=ALU.mult),
                     reads=[Bsl[f % 2], self.Bbank[ubk]], writes=[Bff[f]])
            for sub in range(4):
                for half in range(2):
                    bk = nb()
                    extra = [(self.banks[bk][:, (c % 4) * 128:(c % 4) * 128 + 128],
                              x1T[:, c * 512 + sub * 128:c * 512 + sub * 128 + 128], self.ident[:])
                             for c in range(half * 4, half * 4 + 4)]
                    self.mm_group(bk, 512, [(ffT[:, f * 512 + sub * 128:f * 512 + sub * 128 + 128],
                                             wfd[:, f * 1024 + half * 512:f * 1024 + half * 512 + 512]) for f in range(NF)],
                                  [Bwfd, self.Bconst] + Bff + Bx1, extra=extra)
                    if half == 0:
                        S.op("act", lambda bk=bk: nc.scalar.copy(out=ot[:, 0:512], in_=self.banks[bk][:, :]),
                             reads=[self.Bbank[bk]], writes=[Bot])
                    else:
                        S.op("dve", lambda bk=bk: nc.vector.tensor_copy(out=ot[:, 512:1024], in_=self.banks[bk][:, :]),
                             reads=[self.Bbank[bk]], writes=[Bot])
                r0 = j * 512 + sub * 128
                S.dma(lambda r0=r0: nc.sync.dma_start(out=self.out[r0:r0 + 128, :], in_=ot[:]), reads=[Bot])


def own_tiles(role, NS):
    tiles = []
    for j in range(NS):
        early = (j % 2 == 0) if role == 0 else (j % 2 == 1)
        tiles.append(2 * j if early else 2 * j + 1)
    return tiles


def host_constants(role, NT):
    NS = NT // 2
    SEQ = NT * 512
    OWN = NS * 512
    tiles = own_tiles(role, NS)
    half = HD // 2
    inv_freq = (np.float32(10000.0) ** (-(np.arange(half, dtype=np.float32) * np.float32(2.0) / np.float32(HD)))).astype(np.float32)
    pos = np.arange(SEQ, dtype=np.float32)
    ang = (pos[:, None] * inv_freq[None, :]).astype(np.float32)
    cos = np.cos(ang).astype(np.float32)
    sin = np.sin(ang).astype(np.float32)
    fidx = np.arange(128) % 32
    cosK = np.ascontiguousarray(cos[:, fidx].T)
    sinK = np.ascontiguousarray(sin[:, fidx].T)
    own_pos = np.concatenate([np.arange(t * 512, (t + 1) * 512) for t in tiles])
    cosQ = np.ascontiguousarray(cosK[:, own_pos])
    sinQ = np.ascontiguousarray(sinK[:, own_pos])
    qblk = own_pos // 256
    n = np.arange(32)
    pb = np.where(n[None, :] < qblk[:, None], 0.0,
                  np.where(n[None, :] == qblk[:, None], 1e30, -1e30)).astype(np.float32)
    def lay(a):
        a = np.tile(a[:, None, :], (1, 6, 1)).reshape(OWN // 128, 128, 192)
        return np.ascontiguousarray(a.transpose(1, 0, 2).reshape(128, -1))
    pastb = lay(pb)
    sbmask = np.zeros((2, 128, 8, 512), np.float32)
    momask = np.zeros((2, 128, 8, 512), np.float32)
    s_idx = np.arange(128)[:, None]
    q_idx = np.arange(512)[None, :]
    for par in range(2):
        early = (par == 0) if role == 0 else (par == 1)
        own_off = 0 if early else 512
        for wi in range(8):
            kpos = wi * 128 + s_idx
            qpos = own_off + q_idx
            sbmask[par, :, wi, :] = np.where(kpos < qpos, 0.0, NEG)
            same_blk = (kpos // 256) == (qpos // 256)
            momask[par, :, wi, :] = np.where(same_blk & (kpos > qpos), NEG, 0.0)
    bf = ml_dtypes.bfloat16
    return dict(cosK=cosK, sinK=sinK, cosQ=cosQ, sinQ=sinQ, pastb=pastb,
                sbmask=sbmask.reshape(2, 128, 4096).astype(bf), momask=momask.reshape(2, 128, 4096).astype(bf))


_CACHE = {}


STOP_AFTER = 4
MO_STOP = 5
VAR = set()
STRESS = 0


def _program(NT, debug=False):
    key = (NT, debug, STOP_AFTER, MO_STOP, tuple(sorted(VAR)), STRESS)
    if key not in _CACHE:
        b = Builder(NT, debug, STOP_AFTER)
        nc = b.build()
        _CACHE[key] = nc
    return _CACHE[key]


WNAMES = ["w_in", "w_mem_kv", "w_up_sb", "w_up_moba", "w_up_mem", "w_out", "w_ffn_in", "w_ffn_down",
          "mix_norm_g", "mem_norm_g", "ffn_norm_g", "moba_q_norm_g", "moba_k_norm_g", "mem_q_norm_g", "mem_k_norm_g"]


def run(inputs, debug=False, trace=False):
    x = np.asarray(inputs["x"], dtype=np.float32)
    mem = np.asarray(inputs["mem"], dtype=np.float32)
    B, SEQ, _ = x.shape
    NT = SEQ // 512
    NS = NT // 2
    assert B * 2 == N_CORES
    nc = _program(NT, debug)
    wmaps = {k: np.ascontiguousarray(np.asarray(inputs[k], dtype=np.float32)[0]) for k in WNAMES}
    consts = [host_constants(r, NT) for r in range(2)]
    in_maps = []
    for core in range(N_CORES):
        b, role = core // 2, core % 2
        tiles = own_tiles(role, NS)
        xown = np.concatenate([x[b, t * 512:(t + 1) * 512] for t in tiles], axis=0)
        m = dict(wmaps)
        m.update(consts[role])
        m["xseq"] = np.ascontiguousarray(x[b])
        m["xown"] = np.ascontiguousarray(xown)
        m["mem"] = np.ascontiguousarray(mem[b])
        in_maps.append(m)
    res = run_bass_kernel_spmd(nc, in_maps, core_ids=list(range(N_CORES)), **({"trace": True} if trace else {}))
    out = np.empty((B, SEQ, D), np.float32)
    for core in range(N_CORES):
        b, role = core // 2, core % 2
        tiles = own_tiles(role, NS)
        o = np.asarray(res.results[core]["out"])
        for j, t in enumerate(tiles):
            out[b, t * 512:(t + 1) * 512] = o[j * 512:(j + 1) * 512]
    return out, res


def kernel(**inputs):
    out, _ = run(inputs)
    return out
```

```python
import math
from contextlib import ExitStack

import numpy as np
import ml_dtypes

import concourse.bass as bass
import concourse.mybir as mybir
from concourse.bass_utils import run_bass_kernel_spmd

F32 = mybir.dt.float32
BF16 = mybir.dt.bfloat16
AF = mybir.ActivationFunctionType
ALU = mybir.AluOpType
AX = mybir.AxisListType

D = 1024
HD = 64
DFF = 2816
MEMLEN = 256
INW = 5632
NEG = -30000.0
EPS = 1e-6
N_CORES = 8


class Buf:
    __slots__ = ("name", "w", "r", "psum")

    def __init__(self, name, psum=False):
        self.name = name
        self.w = None
        self.r = []
        self.psum = psum


class Ev:
    __slots__ = ("eng", "seq", "sem", "val", "clock")

    def __init__(self, eng, seq):
        self.eng = eng
        self.seq = seq
        self.sem = None
        self.val = None
        self.clock = None


class Sched:
    def __init__(self, nc, engsems, dmasems):
        self.nc = nc
        self.eng = {"pe": nc.tensor, "act": nc.scalar, "dve": nc.vector,
                    "pool": nc.gpsimd, "sp": nc.sync}
        self.sem = engsems
        self.clock = {k: {} for k in self.eng}
        self.count = {k: 0 for k in self.eng}
        self.seq = {k: 0 for k in self.eng}
        self.pending = {k: [] for k in self.eng}
        self.last = {k: None for k in self.eng}
        self.dma_sems = list(dmasems)
        self.dma_cnt = [0] * len(dmasems)
        self.dma_last = [None] * len(dmasems)
        self.dma_rr = 0
        self.nwaits = 0
        self.nops = 0
        self.defer = None

    def _deps(self, reads, writes):
        deps = []
        for b in reads:
            if b.w is not None:
                deps.append(b.w)
            if b.psum:
                deps.extend(b.r)
        for b in writes:
            if b.w is not None:
                deps.append(b.w)
            deps.extend(b.r)
        return deps

    def _waits(self, e, deps):
        ck = self.clock[e]
        need = {}
        for d in deps:
            if d.eng == "pe" and e == "pe":
                continue
            if ck.get(d.eng, 0) >= d.seq:
                continue
            assert d.val is not None, "dependency on unresolved milestone (%s)" % d.eng
            cur = need.get(d.eng)
            if cur is None or cur.seq < d.seq:
                need[d.eng] = d
        waits = []
        for d in need.values():
            if ck.get(d.eng, 0) >= d.seq:
                continue
            waits.append((d.sem, d.val))
            for k, v in d.clock.items():
                if ck.get(k, 0) < v:
                    ck[k] = v
            if ck.get(d.eng, 0) < d.seq:
                ck[d.eng] = d.seq
        return waits

    def _apply(self, e, waits, make):
        eng = self.eng[e]
        self.nwaits += len(waits)
        for (s, v) in waits[1:]:
            eng.wait_ge(s, v)
        ins = make()
        if waits:
            ins._wait_ge(waits[0][0], waits[0][1])
        return ins

    def _record(self, ev, reads, writes):
        for b in reads:
            b.r.append(ev)
        for b in writes:
            b.w = ev
            b.r = []

    def start_defer(self):
        self.defer = []

    def end_defer(self):
        d, self.defer = self.defer, None
        return d

    def ub(self):
        if self.defer is not None and self.defer and self.defer[-1] is not None:
            self.defer.append(None)

    def emit_deferred(self, q, nunits):
        while nunits > 0 and q:
            it = q.popleft()
            if it is None:
                nunits -= 1
                continue
            kind = it[0]
            if kind == "op":
                self.op(*it[1:])
            else:
                self.dma(*it[1:])

    def op(self, e, make, reads=(), writes=(), inc=True):
        if self.defer is not None:
            self.defer.append(("op", e, make, tuple(reads), tuple(writes), inc))
            return None
        self.nops += 1
        waits = self._waits(e, self._deps(reads, writes))
        ins = self._apply(e, waits, make)
        self.seq[e] += 1
        ev = Ev(e, self.seq[e])
        ev.sem = self.sem[e]
        ev.clock = dict(self.clock[e])
        if inc:
            self.count[e] += 1
            ins.then_inc(self.sem[e], 1)
            ev.val = self.count[e]
            for p in self.pending[e]:
                p.val = ev.val
            self.pending[e] = []
        else:
            self.pending[e].append(ev)
        self.last[e] = ev
        self._record(ev, reads, writes)
        return ev

    def dma(self, make, reads=(), writes=(), q="sp"):
        if self.defer is not None:
            self.defer.append(("dma", make, tuple(reads), tuple(writes), q))
            return None
        self.nops += 1
        j = self.dma_rr
        self.dma_rr = (self.dma_rr + 1) % len(self.dma_sems)
        deps = self._deps(reads, writes)
        if self.dma_last[j] is not None:
            deps.append(self.dma_last[j])
        waits = self._waits(q, deps)
        ins = self._apply(q, waits, make)
        self.dma_cnt[j] += 1
        ins.then_inc(self.dma_sems[j], 16)
        ev = Ev("dma%d" % j, self.dma_cnt[j])
        ev.sem = self.dma_sems[j]
        ev.val = 16 * self.dma_cnt[j]
        ev.clock = dict(self.clock[q])
        self.dma_last[j] = ev
        self._record(ev, reads, writes)
        return ev

    def barrier(self):
        evs = []
        for k in self.eng:
            if self.pending[k]:
                self.op(k, lambda k=k: self.eng[k].drain(), inc=True)
            if self.last[k] is not None and self.last[k].val is not None:
                evs.append(self.last[k])
        for d in self.dma_last:
            if d is not None:
                evs.append(d)
        for e in self.eng:
            waits = self._waits(e, evs)
            self.nwaits += len(waits)
            for (s, v) in waits:
                self.eng[e].wait_ge(s, v)


class Feeder:
    def __init__(self, S, entries, steps):
        from collections import deque
        self.S = S
        self.q = deque(entries)
        self.units = sum(1 for x in entries if x is None) + 1
        self.steps = max(steps, 1)

    def step(self):
        if not self.q:
            return
        n = -(-self.units // self.steps)
        self.steps = max(self.steps - 1, 1)
        self.units = max(self.units - n, 0)
        self.S.emit_deferred(self.q, n)

    def drain(self):
        self.S.emit_deferred(self.q, 1 << 30)


class SbAlloc:
    def __init__(self, nc, limit):
        self.nc = nc
        self.off = 16512
        self.limit = limit
        self.n = 0
        self.peak = 0

    def alloc(self, name, cols, dt):
        isz = 4 if dt == F32 else 2
        nbytes = (cols * isz + 63) // 64 * 64
        off = self.off
        assert off + nbytes <= self.limit, "SBUF overflow at %s: %d + %d > %d" % (name, off, nbytes, self.limit)
        self.off += nbytes
        self.peak = max(self.peak, self.off)
        self.n += 1
        return self.nc.alloc_sbuf_tensor_at("%s_%d" % (name, self.n), [128, cols], dt, offset=off)

    def mark(self):
        return self.off

    def release(self, m):
        self.off = m


class Builder:
    def __init__(self, NT, debug=False, stop_after=4):
        self.stop_after = stop_after
        assert NT % 4 == 0
        self.NT = NT
        self.NS = NT // 2
        self.SEQ = NT * 512
        self.NKB = self.SEQ // 128
        self.OWN = self.NS * 512
        self.debug = debug
        self.nc = bass.Bass("TRN2", target_bir_lowering=False)
        self.bank_rr = 0

    def din(self, name, shape, dt=F32):
        return self.nc.dram_tensor(name, list(shape), dt, kind="ExternalInput").ap()

    def dout(self, name, shape, dt=F32):
        return self.nc.dram_tensor(name, list(shape), dt, kind="ExternalOutput").ap()

    def build(self):
        nc = self.nc
        SEQ, OWN, NS = self.SEQ, self.OWN, self.NS
        self.xseq = self.din("xseq", [SEQ, D])
        self.xown = self.din("xown", [OWN, D])
        self.mem = self.din("mem", [MEMLEN, D])
        self.w_in = self.din("w_in", [D, INW])
        self.w_mem_kv = self.din("w_mem_kv", [D, 512])
        self.w_up_sb = self.din("w_up_sb", [384, D])
        self.w_up_moba = self.din("w_up_moba", [384, D])
        self.w_up_mem = self.din("w_up_mem", [256, D])
        self.w_out = self.din("w_out", [D, D])
        self.w_ffn_in = self.din("w_ffn_in", [D, 2 * DFF])
        self.w_ffn_down = self.din("w_ffn_down", [DFF, D])
        self.g_mix = self.din("mix_norm_g", [D])
        self.g_memn = self.din("mem_norm_g", [D])
        self.g_ffn = self.din("ffn_norm_g", [D])
        self.g_moq = self.din("moba_q_norm_g", [HD])
        self.g_mok = self.din("moba_k_norm_g", [HD])
        self.g_meq = self.din("mem_q_norm_g", [HD])
        self.g_mek = self.din("mem_k_norm_g", [HD])
        self.cosK = self.din("cosK", [128, SEQ])
        self.sinK = self.din("sinK", [128, SEQ])
        self.cosQ = self.din("cosQ", [128, OWN])
        self.sinQ = self.din("sinQ", [128, OWN])
        self.pastb = self.din("pastb", [128, NS * 4 * 192])
        self.sbmask = self.din("sbmask", [2, 128, 8 * 512], BF16)
        self.momask = self.din("momask", [2, 128, 8 * 512], BF16)
        self.out = self.dout("out", [OWN, D])
        if self.debug:
            self.oTs = self.dout("oTs", [8, 128, OWN], BF16)
            self.x1s = self.dout("x1s", [NS * 8, 128, 512], F32)
        else:
            self.oTs = nc.dram_tensor("oTs", [8, 128, OWN], BF16).ap()
            self.x1s = nc.dram_tensor("x1s", [NS * 8, 128, 512], F32).ap()
        self.BoTs = [Buf("oTs%d" % j) for j in range(NS)]
        self.Bx1s = [Buf("x1s%d" % j) for j in range(NS)]

        with ExitStack() as es:
            engsems = {k: es.enter_context(nc.semaphore("s_" + k)) for k in ["pe", "act", "dve", "pool", "sp"]}
            dmasems = [es.enter_context(nc.semaphore("d%d" % i)) for i in range(24)]
            self.S = Sched(nc, engsems, dmasems)
            self.A = SbAlloc(nc, 229376)
            self.banks = [nc.alloc_psum_tensor("bank%d" % i, [128, 512], F32) for i in range(8)]
            self.Bbank = [Buf("bank%d" % i, psum=True) for i in range(8)]
            self.consts()
            m0 = self.A.mark()
            for i, ph in enumerate((self.phase_sb, self.phase_mo, self.phase_f1)):
                if self.stop_after < i + 1:
                    break
                if STRESS and i == 0:
                    continue
                ph()
                self.S.barrier()
                self.A.release(m0)
            if self.stop_after >= 4:
                self.A.release(self.m_consts)
                self.phase_f2()
            self.S.barrier()
        return nc

    def consts(self):
        nc, S, A = self.nc, self.S, self.A
        self.Bconst = Bc = Buf("consts")

        def diag_select(ap, ncols, base, op=ALU.is_equal):
            S.op("pool", lambda: nc.gpsimd.affine_select(out=ap, in_=ap, pattern=[[-1, ncols]], compare_op=op,
                                                         fill=0.0, base=base, channel_multiplier=1),
                 reads=[Bc], writes=[Bc])

        self.ident = A.alloc("ident", 128, F32)
        S.op("pool", lambda: nc.gpsimd.memset(self.ident[:], 1.0), writes=[Bc])
        diag_select(self.ident[:], 128, 0)
        self.identb = A.alloc("identb", 128, BF16)
        S.op("pool", lambda: nc.gpsimd.memset(self.identb[:], 1.0), writes=[Bc])
        diag_select(self.identb[:], 128, 0)
        self.trineg = A.alloc("trineg", 128, BF16)
        S.op("pool", lambda: nc.gpsimd.memset(self.trineg[:], -1.0), writes=[Bc])
        diag_select(self.trineg[:], 128, 0, ALU.is_ge)
        self.negones = A.alloc("negones", 128, BF16)
        S.op("pool", lambda: nc.gpsimd.memset(self.negones[:], -1.0), writes=[Bc])
        self.onesb = A.alloc("onesb", 128, BF16)
        S.op("pool", lambda: nc.gpsimd.memset(self.onesb[:], 1.0), writes=[Bc])
        self.onesf = A.alloc("onesf", 128, F32)
        S.op("pool", lambda: nc.gpsimd.memset(self.onesf[:], 1.0), writes=[Bc])
        self.blk64 = A.alloc("blk64", 128, F32)
        S.op("pool", lambda: nc.gpsimd.memset(self.blk64[:], 0.0), writes=[Bc])
        S.op("pool", lambda: nc.gpsimd.memset(self.blk64[0:64, 0:64], 1.0 / 64), reads=[Bc], writes=[Bc])
        S.op("pool", lambda: nc.gpsimd.memset(self.blk64[64:128, 64:128], 1.0 / 64), reads=[Bc], writes=[Bc])
        self.rot = A.alloc("rot", 128, F32)
        for (c0, val, base) in ((0, -1.0, -32), (32, 1.0, 0), (64, -1.0, -96), (96, 1.0, -64)):
            S.op("pool", lambda c0=c0, val=val: nc.gpsimd.memset(self.rot[:, c0:c0 + 32], val), reads=[Bc], writes=[Bc])
            diag_select(self.rot[:, c0:c0 + 32], 32, base)
        self.cst = A.alloc("cst", 8, F32)
        S.op("pool", lambda: nc.gpsimd.memset(self.cst[:, 0:1], EPS), writes=[Bc])
        S.op("pool", lambda: nc.gpsimd.memset(self.cst[:, 1:2], 1.0), reads=[Bc], writes=[Bc])
        self.esel = A.alloc("esel", 32 * 128, BF16)
        mtmp = A.mark()
        etmp = A.alloc("etmp", 32 * 128, BF16)
        for g in range(3):
            dstt = self.esel if g == 0 else etmp
            v = dstt[:].rearrange("p (n m) -> p n m", n=32)
            S.op("pool", lambda dstt=dstt: nc.gpsimd.memset(dstt[:], 1.0), reads=[Bc], writes=[Bc])
            S.op("pool", lambda v=v, g=g: nc.gpsimd.affine_select(
                out=v, in_=v, pattern=[[-1, 32], [0, 128]], compare_op=ALU.is_equal, fill=0.0, base=-32 * g,
                channel_multiplier=1), reads=[Bc], writes=[Bc])
            if g > 0:
                S.op("pool", lambda: nc.gpsimd.tensor_tensor(out=self.esel[:], in0=self.esel[:], in1=etmp[:], op=ALU.add),
                     reads=[Bc], writes=[Bc])
        S.barrier()
        A.release(mtmp)
        self.gcol = A.alloc("gcol", 24, F32)
        self.Bg = Buf("gains")
        for i, g in enumerate([self.g_mix, self.g_memn, self.g_ffn]):
            S.dma(lambda i=i, g=g: nc.sync.dma_start(out=self.gcol[:, 8 * i:8 * i + 8],
                                                     in_=g.rearrange("(c p) -> p c", p=128),
                                                     allow_slow_non_contiguous=True), writes=[self.Bg])
        self.gh = A.alloc("gh", 8, F32)
        for i, g in enumerate([self.g_moq, self.g_mok, self.g_meq, self.g_mek]):
            g2 = g.rearrange("(p o) -> p o", o=1)
            for half in range(2):
                S.dma(lambda i=i, g2=g2, half=half: nc.sync.dma_start(
                    out=self.gh[64 * half:64 * half + 64, 2 * i:2 * i + 1], in_=g2[0:64, :]), writes=[self.Bg])
                for q in range(2):
                    S.dma(lambda i=i, g2=g2, half=half, q=q: nc.sync.dma_start(
                        out=self.gh[64 * half + 32 * q:64 * half + 32 * q + 32, 2 * i + 1:2 * i + 2],
                        in_=g2[32 * (1 - q):32 * (1 - q) + 32, :]), writes=[self.Bg])
        self.m_consts = A.mark()
        self.xs = [A.alloc("xs%d" % i, D, BF16) for i in range(2)]
        self.Bxs = [Buf("xs%d" % i) for i in range(2)]
        self.stat = A.alloc("stat", 16, F32)
        self.Bstat = [Buf("stat%d" % i) for i in range(2)]
        self.hT = A.alloc("hT", 8 * 512, BF16)
        self.BhT = Buf("hT")
        self.oTslot = A.alloc("oTslot", 8 * 512, BF16)
        self.BoTslot = Buf("oTslot")
        self.sub_rr = 0

    def alloc_x(self, n):
        self.xt = self.A.alloc("xt", n * D, F32)
        self.Bxt = [Buf("xt%d" % i) for i in range(n)]

    def with_stage(self, fn, keep=False):
        m = self.A.mark()
        self.stage = [self.A.alloc("stage%d" % i, 2048, F32) for i in range(NSTAGE)]
        self.Bstage = [Buf("stage%d" % i) for i in range(NSTAGE)]
        self.stage_rr = 0
        fn()
        self.S.barrier()
        if not keep:
            self.A.release(m)
        self._stage_mark = m

    def load_w(self, wd, k0, nk, c0, ncols, dst, dstw, dst_c0, Bdst, gcol_off=None):
        nc, S = self.nc, self.S
        for kc in range(nk):
            for cc in range(0, ncols, 2048):
                n = min(2048, ncols - cc)
                si = self.stage_rr
                self.stage_rr = (self.stage_rr + 1) % NSTAGE
                st = self.stage[si]
                Bst = self.Bstage[si]
                S.dma(lambda kc=kc, cc=cc, n=n, st=st: nc.sync.dma_start(
                    out=st[:, 0:n], in_=wd[k0 + kc * 128:k0 + kc * 128 + 128, c0 + cc:c0 + cc + n]), writes=[Bst])
                o0 = kc * dstw + dst_c0 + cc
                if gcol_off is None:
                    S.op("dve", lambda st=st, n=n, o0=o0: nc.vector.tensor_copy(out=dst[:, o0:o0 + n], in_=st[:, 0:n]),
                         reads=[Bst], writes=[Bdst])
                else:
                    S.op("dve", lambda st=st, n=n, o0=o0, kc=kc: nc.vector.tensor_scalar(
                        out=dst[:, o0:o0 + n], in0=st[:, 0:n],
                        scalar1=self.gcol[:, gcol_off + kc:gcol_off + kc + 1], scalar2=None, op0=ALU.mult),
                        reads=[Bst, self.Bg], writes=[Bdst])

    def make_hT(self, xd, row0, nsub, tpbank, keep_x=False):
        nc, S = self.nc, self.S
        Btp = self.Bbank[tpbank]
        tpb = self.banks[tpbank][:].bitcast(BF16)
        hTv = self.hT[:].rearrange("p (c t) -> p c t", c=8)
        for j in range(nsub):
            xi = j if keep_x else (self.sub_rr % 2)
            si = self.sub_rr % 2
            self.sub_rr += 1
            xt = self.xt[:, xi * D:(xi + 1) * D]
            Bx = self.Bxt[xi]
            S.dma(lambda xt=xt, j=j: nc.sync.dma_start(out=xt, in_=xd[row0 + j * 128:row0 + j * 128 + 128, :]), writes=[Bx])
            ss = self.stat[:, 4 * si:4 * si + 1]
            lnv = self.stat[:, 4 * si + 1:4 * si + 2]
            rstd = self.stat[:, 4 * si + 2:4 * si + 3]
            Bs = self.Bstat[si]
            S.op("act", lambda xt=xt, ss=ss, si=si: nc.scalar.activation(out=self.xs[si][:], in_=xt, func=AF.Square,
                                                                        accum_out=ss),
                 reads=[Bx], writes=[self.Bxs[si], Bs])
            S.op("act", lambda ss=ss, lnv=lnv: nc.scalar.activation(out=lnv, in_=ss, func=AF.Ln, bias=self.cst[:, 0:1],
                                                                   scale=1.0 / D), reads=[Bs, self.Bconst], writes=[Bs])
            S.op("act", lambda lnv=lnv, rstd=rstd: nc.scalar.activation(out=rstd, in_=lnv, func=AF.Exp, scale=-0.5),
                 reads=[Bs], writes=[Bs])
            xs = self.xs[si]
            Bxs = self.Bxs[si]
            S.op("dve", lambda xs=xs, xt=xt, rstd=rstd: nc.vector.tensor_scalar(
                out=xs[:], in0=xt, scalar1=rstd, scalar2=None, op0=ALU.mult), reads=[Bx, Bs], writes=[Bxs])
            S.ub()
            for c in range(8):
                S.op("pe", lambda c=c, xs=xs: nc.tensor.transpose(
                    out=tpb[:, c * 128:(c + 1) * 128], in_=xs[:, c * 128:(c + 1) * 128], identity=self.identb[:]),
                    reads=[Bxs, self.Bconst], writes=[Btp], inc=(c == 7))
            S.ub()
            S.op("dve", lambda j=j: nc.vector.tensor_copy(
                out=hTv[:, :, j * 128:(j + 1) * 128], in_=tpb.rearrange("p (c t) -> p c t", c=8)),
                reads=[Btp], writes=[self.BhT])
            S.ub()

    def mm_group(self, bank, ncols, pairs, reads, extra=None, inc=True):
        nc, S = self.nc, self.S
        n = len(pairs) + (len(extra) if extra else 0)
        out = self.banks[bank][:, 0:ncols]
        k = 0
        for (l, r) in pairs:
            k += 1
            S.op("pe", lambda l=l, r=r, k=k: nc.tensor.matmul(out, lhsT=l, rhs=r, start=(k == 1), stop=(k == n)),
                 reads=reads, writes=[self.Bbank[bank]], inc=(inc and k == n))
        for (o, l, r) in (extra or []):
            k += 1
            S.op("pe", lambda o=o, l=l, r=r, k=k: nc.tensor.matmul(o, lhsT=l, rhs=r, start=False, stop=(k == n)),
                 reads=reads, writes=[self.Bbank[bank]], inc=(inc and k == n))
        S.ub()

    def store_oT(self, j, q0, nq):
        nc, S = self.nc, self.S
        src = self.oTslot[:, q0 * 512:(q0 + nq) * 512].rearrange("p (q t) -> p q t", q=nq)
        dst = self.oTs[q0:q0 + nq, :, j * 512:(j + 1) * 512].rearrange("q p t -> p q t")
        S.dma(lambda: nc.sync.dma_start(out=dst, in_=src), reads=[self.BoTslot], writes=[self.BoTs[j]])

    def phase_sb(self):
        nc, S, A = self.nc, self.S, self.A
        SEQ, NKB, NS, NT = self.SEQ, self.NKB, self.NS, self.NT
        self.alloc_x(2)
        wsb = A.alloc("wsb", 8 * 1152, BF16)
        Bw = Buf("wsb")
        KT = A.alloc("KT", 3 * SEQ, BF16)
        V = A.alloc("V", NKB * 384, BF16)
        BKT = [Buf("KT%d" % t) for t in range(NT)]
        BV = [Buf("V%d" % t) for t in range(NT)]
        self.with_stage(lambda: self.load_w(self.w_in, 0, 8, 0, 1152, wsb, 1152, 0, Bw, gcol_off=0))
        mask = A.alloc("mask", 8 * 512, BF16)
        Bmask = Buf("mask")
        QTs = [A.alloc("QT%d" % i, 3 * 512, BF16) for i in range(2)]
        BQTs = [Buf("QT%d" % i) for i in range(2)]
        NE, NL, NA = 3, 3, 2
        eb = [A.alloc("e%d" % i, 512, F32) for i in range(NE)]
        Be = [Buf("e%d" % i) for i in range(NE)]
        Lb = [A.alloc("L%d" % i, 512, BF16) for i in range(NL)]
        BL = [Buf("L%d" % i) for i in range(NL)]
        wb = [A.alloc("w%d" % i, 512, F32) for i in range(2)]
        Bwb = [Buf("w%d" % i) for i in range(2)]
        Ab = [A.alloc("A%d" % i, 512, BF16) for i in range(NA)]
        BA = [Buf("A%d" % i) for i in range(NA)]
        Rb = [A.alloc("R%d" % i, 512, BF16) for i in range(2)]
        BR = [Buf("R%d" % i) for i in range(2)]
        hT = self.hT
        TP, ACC = 6, 7

        def evac(bank, ncols, dst, Bdst, eng="dve"):
            if eng == "dve":
                S.op("dve", lambda: nc.vector.tensor_copy(out=dst, in_=self.banks[bank][:, 0:ncols]),
                     reads=[self.Bbank[bank]], writes=[Bdst])
            else:
                S.op("act", lambda: nc.scalar.copy(out=dst, in_=self.banks[bank][:, 0:ncols]),
                     reads=[self.Bbank[bank]], writes=[Bdst])
            S.ub()

        def proj_kv(t):
            self.make_hT(self.xseq, t * 512, 4, TP)
            for pair in range(3):
                self.mm_group(ACC, 512, [(wsb[:, kc * 1152 + 384 + pair * 128:kc * 1152 + 384 + pair * 128 + 128],
                                          hT[:, kc * 512:(kc + 1) * 512]) for kc in range(8)], [Bw, self.BhT])
                evac(ACC, 512, KT[:, pair * SEQ + t * 512:pair * SEQ + (t + 1) * 512], BKT[t])
            for sub in range(4):
                self.mm_group(ACC, 384, [(hT[:, kc * 512 + sub * 128:kc * 512 + sub * 128 + 128],
                                          wsb[:, kc * 1152 + 768:kc * 1152 + 1152]) for kc in range(8)], [Bw, self.BhT])
                kb = t * 4 + sub
                evac(ACC, 384, V[:, kb * 384:(kb + 1) * 384], BV[t])

        def proj_q(j):
            QT, BQT = QTs[j % 2], BQTs[j % 2]
            self.make_hT(self.xown, j * 512, 4, TP)
            for pair in range(3):
                self.mm_group(ACC, 512, [(wsb[:, kc * 1152 + pair * 128:kc * 1152 + pair * 128 + 128],
                                          hT[:, kc * 512:(kc + 1) * 512]) for kc in range(8)], [Bw, self.BhT])
                evac(ACC, 512, QT[:, pair * 512:(pair + 1) * 512], BQT)

        def attn(j, feeder=None):
            QT, BQT = QTs[j % 2], BQTs[j % 2]
            nj = 8 * (j + 1)
            its = [(h, i) for h in range(6) for i in range(nj)]
            N = len(its)
            zb, Tb, Ob = [0, 1], [2, 3], [4, 5]

            def emit_z(s):
                h, i = its[s]
                kb = nj - 1 - i
                pair, hp = h // 2, h % 2
                bank = zb[s % 2]
                inwin = i < 8
                extra = None
                if inwin:
                    wi = 7 - i
                    extra = [(self.banks[bank][:, :], self.identb[:], mask[:, wi * 512:(wi + 1) * 512])]
                self.mm_group(bank, 512,
                              [(KT[hp * 64:hp * 64 + 64, pair * SEQ + kb * 128:pair * SEQ + kb * 128 + 128],
                                QT[hp * 64:hp * 64 + 64, pair * 512:(pair + 1) * 512])],
                              [BKT[kb // 4], BQT, Bmask, self.Bconst], extra=extra)

            def emit_eL(s):
                bank = zb[s % 2]
                e, L = eb[s % NE], Lb[s % NL]
                S.op("act", lambda: nc.scalar.activation(out=e[:], in_=self.banks[bank][:, :], func=AF.Exp, scale=0.125),
                     reads=[self.Bbank[bank]], writes=[Be[s % NE]])
                S.op("act", lambda: nc.scalar.activation(out=L[:], in_=e[:], func=AF.Ln, bias=self.cst[:, 1:2]),
                     reads=[Be[s % NE], self.Bconst], writes=[BL[s % NL]])

            def emit_T(s):
                h, i = its[s]
                bank = Tb[s % 2]
                pairs = [(self.trineg[:], Lb[s % NL][:])]
                rd = [BL[s % NL], self.Bconst]
                if i > 0:
                    pairs.append((self.negones[:], Rb[(i - 1) % 2][:]))
                    rd.append(BR[(i - 1) % 2])
                self.mm_group(bank, 512, pairs, rd)
                w = wb[s % 2]
                S.op("act", lambda: nc.scalar.activation(out=w[:], in_=self.banks[bank][:, :], func=AF.Exp),
                     reads=[self.Bbank[bank]], writes=[Bwb[s % 2]])
                Aa = Ab[s % NA]
                S.op("dve", lambda: nc.vector.tensor_tensor(out=Aa[:], in0=eb[s % NE][:], in1=w[:], op=ALU.mult),
                     reads=[Be[s % NE], Bwb[s % 2]], writes=[BA[s % NA]])
                if i < nj - 1:
                    if i == 0:
                        S.op("pool", lambda: nc.gpsimd.tensor_copy(out=Rb[0][:], in_=Lb[s % NL][:]),
                             reads=[BL[s % NL]], writes=[BR[0]])
                    else:
                        S.op("pool", lambda: nc.gpsimd.tensor_tensor(out=Rb[i % 2][:], in0=Rb[(i - 1) % 2][:],
                                                                     in1=Lb[s % NL][:], op=ALU.add),
                             reads=[BL[s % NL], BR[(i - 1) % 2]], writes=[BR[i % 2]])

            def emit_AV(s):
                h, i = its[s]
                kb = nj - 1 - i
                pair, hp = h // 2, h % 2
                bank = Ob[h % 2]
                S.op("pe", lambda: nc.tensor.matmul(self.banks[bank][:, :],
                                                    lhsT=V[:, kb * 384 + pair * 128:kb * 384 + pair * 128 + 128],
                                                    rhs=Ab[s % NA][:], start=(i == 0), stop=(i == nj - 1)),
                     reads=[BV[kb // 4], BA[s % NA]], writes=[self.Bbank[bank]], inc=True)
                if i == nj - 1:
                    S.op("dve", lambda: nc.vector.tensor_copy(
                        out=self.oTslot[hp * 64:hp * 64 + 64, pair * 512:(pair + 1) * 512],
                        in_=self.banks[bank][hp * 64:hp * 64 + 64, :]),
                        reads=[self.Bbank[bank]], writes=[self.BoTslot])

            emit_z(0)
            for s in range(N + 2):
                if s + 1 < N:
                    emit_z(s + 1)
                if s < N:
                    emit_eL(s)
                if 1 <= s <= N:
                    emit_T(s - 1)
                if s >= 2:
                    emit_AV(s - 2)
                if feeder is not None:
                    feeder.step()

        proj_kv(0)
        proj_kv(1)
        proj_q(0)
        for j in range(NS):
            S.dma(lambda j=j: nc.sync.dma_start(out=mask[:], in_=self.sbmask[j % 2]), writes=[Bmask])
            feeder = None
            if j + 1 < NS:
                S.start_defer()
                proj_kv(2 * j + 2)
                proj_kv(2 * j + 3)
                proj_q(j + 1)
                feeder = Feeder(S, S.end_defer(), 48 * (j + 1))
            attn(j, feeder)
            if feeder is not None:
                feeder.drain()
            self.store_oT(j, 0, 3)

    def phase_mo(self):
        nc, S, A = self.nc, self.S, self.A
        SEQ, NKB, NS, NT = self.SEQ, self.NKB, self.NS, self.NT
        self.alloc_x(2)
        WW = 1408
        wmo = A.alloc("wmo", 8 * WW, BF16)
        Bw = Buf("wmo")
        KT = A.alloc("KT", 3 * SEQ, BF16)
        V = A.alloc("V", NKB * 384, BF16)
        BKT = [Buf("KT%d" % t) for t in range(NT)]
        BV = [Buf("V%d" % t) for t in range(NT)]
        ksum = A.alloc("ksum", 6 * 32, F32)
        Bks = Buf("ksum")
        S.op("pool", lambda: nc.gpsimd.memset(ksum[:], 0.0), writes=[Bks])
        memKT = A.alloc("memKT", 2 * 256, BF16)
        memV = A.alloc("memV", 2 * 256, BF16)
        Bmem = Buf("memkv")

        wmem = V
        Bwm = Buf("wmem")

        def loadw():
            self.load_w(self.w_in, 0, 8, 1152, 1408, wmo, WW, 0, Bw, gcol_off=0)
            self.load_w(self.w_mem_kv, 0, 8, 0, 512, wmem, 512, 0, Bwm, gcol_off=8)
        self.with_stage(loadw)

        mask = A.alloc("mask", 8 * 512, BF16)
        Bmask = Buf("mask")
        QTz = A.alloc("QTz", 6 * 512, BF16)
        Qf = A.alloc("Qf", 3 * 512, F32)
        QTmz = A.alloc("QTmz", 4 * 512, BF16)
        BQTz, BQf, BQTmz = Buf("QTz"), Buf("Qf"), Buf("QTmz")
        S.op("pool", lambda: nc.gpsimd.memset(QTz[:], 0.0), writes=[BQTz])
        S.op("pool", lambda: nc.gpsimd.memset(QTmz[:], 0.0), writes=[BQTmz])
        pb = A.alloc("pb", 4 * 192, F32)
        Bpb = Buf("pb")
        MBT = A.alloc("MBT", 1024, BF16)
        BMBT = Buf("MBT")
        NP = 2
        pbuf = [A.alloc("p%d" % i, 512, BF16) for i in range(NP)]
        Bp = [Buf("p%d" % i) for i in range(NP)]
        cs = A.alloc("cs", 1024, F32)
        Bcs = Buf("cs")
        rawsb = A.alloc("rawsb", 512, F32)
        rstd = A.alloc("rstd", 512, F32)
        ta = A.alloc("ta", 512, F32)
        tb = A.alloc("tb", 512, F32)
        sq = tb
        Braw, Brstd, Bta, Btb = Buf("rawsb"), Buf("rstd"), Buf("ta"), Buf("tb")
        Bsq = Btb
        Gp = A.alloc("Gp", 192, F32)
        sel = A.alloc("sel", 192, F32)
        MBb = A.alloc("MBb", 192, BF16)
        mx = A.alloc("mx", 56, F32)
        BGp, Bsel, BMBb, Bmx = Buf("Gp"), Buf("sel"), Buf("MBb"), Buf("mx")
        rec = A.alloc("rec", 512, F32)
        Brec = Buf("rec")
        hT = self.hT
        TP, ACC, MSB, ROTB = 6, 7, 5, 4

        def norm_qk(ncols, gi, rope, dst_bf, Bdst, dst_f=None, Bdstf=None):
            acc = self.banks[ACC][:, 0:ncols]
            gA = self.gh[:, 2 * gi:2 * gi + 1]
            gB = self.gh[:, 2 * gi + 1:2 * gi + 2]
            S.op("act", lambda: nc.scalar.copy(out=rawsb[:, 0:ncols], in_=acc), reads=[self.Bbank[ACC]], writes=[Braw])
            S.op("act", lambda: nc.scalar.activation(out=sq[:, 0:ncols], in_=acc, func=AF.Square),
                 reads=[self.Bbank[ACC]], writes=[Bsq])
            S.ub()
            self.mm_group(MSB, ncols, [(self.blk64[:], sq[:, 0:ncols])], [Bsq, self.Bconst])
            if rope and "norot" not in VAR:
                self.mm_group(ROTB, ncols, [(self.rot[:], rawsb[:, 0:ncols])], [Braw, self.Bconst])
            S.op("act", lambda: nc.scalar.activation(out=rstd[:, 0:ncols], in_=self.banks[MSB][:, 0:ncols], func=AF.Ln,
                                                     bias=self.cst[:, 0:1]), reads=[self.Bbank[MSB], self.Bconst],
                 writes=[Brstd])
            S.op("act", lambda: nc.scalar.activation(out=rstd[:, 0:ncols], in_=rstd[:, 0:ncols], func=AF.Exp, scale=-0.5),
                 reads=[Brstd], writes=[Brstd])
            S.ub()
            if rope:
                S.op("dve", lambda: nc.vector.scalar_tensor_tensor(out=ta[:, 0:ncols], in0=rawsb[:, 0:ncols], scalar=gA,
                                                                   in1=cs[:, 0:ncols], op0=ALU.mult, op1=ALU.mult),
                     reads=[Braw, Bcs, self.Bg], writes=[Bta])
                S.op("dve", lambda: nc.vector.scalar_tensor_tensor(out=tb[:, 0:ncols], in0=self.banks[ROTB][:, 0:ncols],
                                                                   scalar=gB, in1=cs[:, 512:512 + ncols],
                                                                   op0=ALU.mult, op1=ALU.mult),
                     reads=[self.Bbank[ROTB], Bcs, self.Bg], writes=[Btb])
                S.ub()
                pe_ = "dve" if "nopool" in VAR else "pool"
                pen_ = nc.vector if "nopool" in VAR else nc.gpsimd
                S.op(pe_, lambda: pen_.tensor_tensor(out=ta[:, 0:ncols], in0=ta[:, 0:ncols], in1=tb[:, 0:ncols],
                                                     op=ALU.add), reads=[Bta, Btb], writes=[Bta])
                S.ub()
                fin = dst_f if dst_f is not None else tb[:, 0:ncols]
                Bfin = Bdstf if dst_f is not None else Btb
                S.op("dve", lambda: nc.vector.tensor_tensor(out=fin, in0=ta[:, 0:ncols], in1=rstd[:, 0:ncols], op=ALU.mult),
                     reads=[Bta, Brstd], writes=[Bfin])
            else:
                fin, Bfin = ta[:, 0:ncols], Bta
                S.op("dve", lambda: nc.vector.scalar_tensor_tensor(out=fin, in0=rawsb[:, 0:ncols], scalar=gA,
                                                                   in1=rstd[:, 0:ncols], op0=ALU.mult, op1=ALU.mult),
                     reads=[Braw, Brstd, self.Bg], writes=[Bfin])
            S.ub()
            if callable(dst_bf):
                for hp in range(2):
                    S.op("pool", lambda hp=hp: nc.gpsimd.tensor_copy(out=dst_bf(hp), in_=fin[hp * 64:hp * 64 + 64, :]),
                         reads=[Bfin], writes=[Bdst])
            else:
                S.op("pool", lambda: nc.gpsimd.tensor_copy(out=dst_bf, in_=fin), reads=[Bfin], writes=[Bdst])
            S.ub()
            return fin, Bfin

        def evac(bank, ncols, dst, Bdst):
            S.op("dve", lambda: nc.vector.tensor_copy(out=dst, in_=self.banks[bank][:, 0:ncols]),
                 reads=[self.Bbank[bank]], writes=[Bdst])
            S.ub()

        self.make_hT(self.mem, 0, 2, TP)
        for mp in range(2):
            self.mm_group(ACC, 256, [(wmem[:, kc * 512 + mp * 128:kc * 512 + mp * 128 + 128],
                                      hT[:, kc * 512:kc * 512 + 256]) for kc in range(8)], [Bwm, self.BhT])
            norm_qk(256, 3, False, memKT[:, mp * 256:(mp + 1) * 256], Bmem)
        for sub in range(2):
            self.mm_group(ACC, 256, [(hT[:, kc * 512 + sub * 128:kc * 512 + sub * 128 + 128],
                                      wmem[:, kc * 512 + 256:kc * 512 + 512]) for kc in range(8)], [Bwm, self.BhT])
            evac(ACC, 256, memV[:, sub * 256:(sub + 1) * 256], Bmem)

        S.barrier()

        def proj_kv(t):
            self.make_hT(self.xseq, t * 512, 4, TP)
            S.dma(lambda: nc.sync.dma_start(out=cs[:, 0:512], in_=self.cosK[:, t * 512:(t + 1) * 512]), writes=[Bcs])
            S.dma(lambda: nc.sync.dma_start(out=cs[:, 512:1024], in_=self.sinK[:, t * 512:(t + 1) * 512]), writes=[Bcs])
            for pair in range(3):
                self.mm_group(ACC, 512, [(wmo[:, kc * WW + 384 + pair * 128:kc * WW + 384 + pair * 128 + 128],
                                          hT[:, kc * 512:(kc + 1) * 512]) for kc in range(8)], [Bw, self.BhT])
                fin, Bfin = norm_qk(512, 1, True, KT[:, pair * SEQ + t * 512:pair * SEQ + (t + 1) * 512], BKT[t])
                for hp in range(2):
                    h = 2 * pair + hp
                    S.op("dve", lambda h=h, hp=hp, fin=fin: nc.vector.tensor_reduce(
                        out=ksum[hp * 64:hp * 64 + 64, h * 32 + 2 * t:h * 32 + 2 * t + 2],
                        in_=fin[hp * 64:hp * 64 + 64, :].rearrange("p (b k) -> p b k", b=2), axis=AX.X, op=ALU.add),
                        reads=[Bfin], writes=[Bks])
                S.ub()
            for sub in range(4):
                self.mm_group(ACC, 384, [(hT[:, kc * 512 + sub * 128:kc * 512 + sub * 128 + 128],
                                          wmo[:, kc * WW + 768:kc * WW + 1152]) for kc in range(8)], [Bw, self.BhT])
                kb = t * 4 + sub
                evac(ACC, 384, V[:, kb * 384:(kb + 1) * 384], BV[t])

        def proj_q(j):
            self.make_hT(self.xown, j * 512, 4, TP)
            S.dma(lambda: nc.sync.dma_start(out=cs[:, 0:512], in_=self.cosQ[:, j * 512:(j + 1) * 512]), writes=[Bcs])
            S.dma(lambda: nc.sync.dma_start(out=cs[:, 512:1024], in_=self.sinQ[:, j * 512:(j + 1) * 512]), writes=[Bcs])
            for pair in range(3):
                self.mm_group(ACC, 512, [(wmo[:, kc * WW + pair * 128:kc * WW + pair * 128 + 128],
                                          hT[:, kc * 512:(kc + 1) * 512]) for kc in range(8)], [Bw, self.BhT])
                norm_qk(512, 0, True,
                        lambda hp, pair=pair: QTz[hp * 64:hp * 64 + 64, (2 * pair + hp) * 512:(2 * pair + hp + 1) * 512],
                        BQTz, dst_f=Qf[:, pair * 512:(pair + 1) * 512], Bdstf=BQf)
            for mp in range(2):
                self.mm_group(ACC, 512, [(wmo[:, kc * WW + 1152 + mp * 128:kc * WW + 1152 + mp * 128 + 128],
                                          hT[:, kc * 512:(kc + 1) * 512]) for kc in range(8)], [Bw, self.BhT])
                norm_qk(512, 2, False,
                        lambda hp, mp=mp: QTmz[hp * 64:hp * 64 + 64, (2 * mp + hp) * 512:(2 * mp + hp + 1) * 512], BQTmz)

        def gate_select(j):
            GB = 4
            tpb = self.banks[TP][:].bitcast(BF16)
            S.dma(lambda: nc.sync.dma_start(out=pb[:], in_=self.pastb[:, j * 768:(j + 1) * 768]), writes=[Bpb])
            for sub in range(4):
                for h in range(6):
                    pair, hp = h // 2, h % 2
                    S.op("pe", lambda h=h, pair=pair, hp=hp: nc.tensor.matmul(
                        self.banks[GB][:, h * 32:(h + 1) * 32],
                        lhsT=Qf[:, pair * 512 + sub * 128:pair * 512 + sub * 128 + 128],
                        rhs=ksum[:, h * 32:(h + 1) * 32], start=True, stop=True),
                        reads=[BQf, Bks], writes=[self.Bbank[GB]], inc=(h == 5))
                S.op("dve", lambda: nc.vector.tensor_tensor(out=Gp[:], in0=self.banks[GB][:, 0:192],
                                                            in1=pb[:, sub * 192:(sub + 1) * 192], op=ALU.add),
                     reads=[self.Bbank[GB], Bpb], writes=[BGp])
                for h in range(6):
                    S.op("dve", lambda h=h: nc.vector.max(out=mx[:, h * 8:(h + 1) * 8], in_=Gp[:, h * 32:(h + 1) * 32]),
                         reads=[BGp], writes=[Bmx])
                S.op("dve", lambda: nc.vector.tensor_scalar(
                    out=mx[:, 48:54], in0=mx[:, 0:48].rearrange("p (h e) -> p h e", e=8)[:, :, 3],
                    scalar1=-1e29, scalar2=None, op0=ALU.max), reads=[Bmx], writes=[Bmx])
                for h in range(6):
                    S.op("dve", lambda h=h: nc.vector.tensor_scalar(
                        out=sel[:, h * 32:(h + 1) * 32], in0=Gp[:, h * 32:(h + 1) * 32],
                        scalar1=mx[:, 48 + h:49 + h], scalar2=None, op0=ALU.is_ge), reads=[BGp, Bmx], writes=[Bsel])
                S.op("dve", lambda: nc.vector.tensor_scalar(out=MBb[:], in0=sel[:], scalar1=-1.0, scalar2=-NEG,
                                                            op0=ALU.add, op1=ALU.mult), reads=[Bsel], writes=[BMBb])
                for g in range(2):
                    S.op("pe", lambda g=g: nc.tensor.transpose(
                        out=tpb[0:96, g * 512 + sub * 128:g * 512 + sub * 128 + 128],
                        in_=MBb[:, g * 96:(g + 1) * 96], identity=self.identb[:]),
                        reads=[BMBb, self.Bconst], writes=[self.Bbank[TP]], inc=(g == 1))
            S.op("dve", lambda: nc.vector.tensor_copy(out=MBT[0:96, :], in_=tpb[0:96, :]),
                 reads=[self.Bbank[TP]], writes=[BMBT])

        def attn(j, kind, feeder=None):
            if kind == "mo":
                nj = 8 * (j + 1)
                nh = 6
            else:
                nj = 2
                nh = 4
            its = [(h, kb) for h in range(nh) for kb in range(nj)]
            N = len(its)
            zb, Ob, Lbk = [0, 1], [2, 2], [3, 3]

            def emit_z(s):
                h, kb = its[s]
                pair, hp = h // 2, h % 2
                bank = zb[s % 2]
                out = self.banks[bank][:, :]
                if kind == "mo":
                    pairs = [(KT[:, pair * SEQ + kb * 128:pair * SEQ + kb * 128 + 128],
                              QTz[:, h * 512:(h + 1) * 512])]
                    r3 = 32 * (h % 3)
                    extra = [(out, self.esel[r3:r3 + 32, (kb // 2) * 128:(kb // 2) * 128 + 128],
                              MBT[r3:r3 + 32, (h // 3) * 512:(h // 3) * 512 + 512])]
                    if kb >= nj - 8:
                        wi = kb - (nj - 8)
                        extra.append((out, self.identb[:], mask[:, wi * 512:(wi + 1) * 512]))
                    self.mm_group(bank, 512, pairs, [BKT[kb // 4], BQTz, BMBT, Bmask, self.Bconst], extra=extra)
                else:
                    pairs = [(memKT[:, pair * 256 + kb * 128:pair * 256 + kb * 128 + 128],
                              QTmz[:, h * 512:(h + 1) * 512])]
                    self.mm_group(bank, 512, pairs, [Bmem, BQTmz])

            def emit_p(s):
                bank = zb[s % 2]
                p = pbuf[s % NP]
                S.op("act", lambda: nc.scalar.activation(out=p[:], in_=self.banks[bank][:, :], func=AF.Exp, scale=0.125),
                     reads=[self.Bbank[bank]], writes=[Bp[s % NP]])

            def emit_AV(s):
                h, kb = its[s]
                pair, hp = h // 2, h % 2
                ob, lb = Ob[h % 2], Lbk[h % 2]
                p = pbuf[s % NP]
                if kind == "mo":
                    vap = V[:, kb * 384 + pair * 128:kb * 384 + pair * 128 + 128]
                    rd = [BV[kb // 4], Bp[s % NP]]
                    q = 3 + pair
                else:
                    vap = memV[:, kb * 256 + pair * 128:kb * 256 + pair * 128 + 128]
                    rd = [Bmem, Bp[s % NP]]
                    q = 6 + pair
                last = (kb == nj - 1)
                S.op("pe", lambda: nc.tensor.matmul(self.banks[ob][:, :], lhsT=vap, rhs=p[:], start=(kb == 0), stop=last),
                     reads=rd, writes=[self.Bbank[ob]], inc=False)
                S.op("pe", lambda: nc.tensor.matmul(self.banks[lb][:, :], lhsT=self.onesb[:], rhs=p[:], start=(kb == 0),
                                                    stop=last),
                     reads=[Bp[s % NP], self.Bconst], writes=[self.Bbank[lb]], inc=True)
                if last:
                    r0 = hp * 64
                    S.op("dve", lambda: nc.vector.reciprocal(out=rec[r0:r0 + 64, :], in_=self.banks[lb][r0:r0 + 64, :]),
                         reads=[self.Bbank[lb]], writes=[Brec])
                    S.op("dve", lambda: nc.vector.tensor_tensor(out=self.oTslot[r0:r0 + 64, q * 512:(q + 1) * 512],
                                                                in0=self.banks[ob][r0:r0 + 64, :], in1=rec[r0:r0 + 64, :],
                                                                op=ALU.mult),
                         reads=[self.Bbank[ob], Brec], writes=[self.BoTslot])

            emit_z(0)
            for s in range(N + 1):
                if s + 1 < N:
                    emit_z(s + 1)
                if s < N:
                    emit_p(s)
                if s >= 1:
                    emit_AV(s - 1)
                if feeder is not None:
                    feeder.step()

        if STRESS:
            for rep in range(STRESS):
                proj_kv(rep % NT)
            return
        proj_kv(0)
        proj_kv(1)
        for j in range(NS):
            S.dma(lambda j=j: nc.sync.dma_start(out=mask[:], in_=self.momask[j % 2]), writes=[Bmask])
            proj_q(j)
            gate_select(j)
            feeder = None
            if j + 1 < NS:
                S.start_defer()
                proj_kv(2 * j + 2)
                proj_kv(2 * j + 3)
                feeder = Feeder(S, S.end_defer(), 48 * (j + 1) + 8)
            attn(j, "mo", feeder)
            attn(j, "me", feeder)
            if feeder is not None:
                feeder.drain()
            self.store_oT(j, 3, 5)

    def phase_f1(self):
        nc, S, A = self.nc, self.S, self.A
        NS = self.NS
        self.alloc_x(4)
        wg = A.alloc("wg", 8 * 3072, BF16)
        wup = A.alloc("wup", 8 * 1024, BF16)
        wout = A.alloc("wout", 8 * 1024, BF16)
        Bwg, Bwup, Bwout = Buf("wg"), Buf("wup"), Buf("wout")

        def loadw():
            self.load_w(self.w_in, 0, 8, 2560, 3072, wg, 3072, 0, Bwg, gcol_off=0)
            self.load_w(self.w_up_sb, 0, 3, 0, 1024, wup, 1024, 0, Bwup)
            self.load_w(self.w_up_moba, 0, 3, 0, 1024, wup, 1024, 3 * 1024, Bwup)
            self.load_w(self.w_up_mem, 0, 2, 0, 1024, wup, 1024, 6 * 1024, Bwup)
            self.load_w(self.w_out, 0, 8, 0, 1024, wout, 1024, 0, Bwout)
        self.with_stage(loadw)
        oT = A.alloc("oT", 8 * 512, BF16)
        BoT = Buf("oT")
        sg = [A.alloc("sg%d" % i, 512, F32) for i in range(3)]
        tt = [A.alloc("tt%d" % i, 512, F32) for i in range(3)]
        Bsg = [Buf("sg%d" % i) for i in range(3)]
        Btt = [Buf("tt%d" % i) for i in range(3)]
        mixb = A.alloc("mixb", 8 * 512, BF16)
        Bmix = [Buf("mix%d" % i) for i in range(8)]
        x1c = [A.alloc("x1c%d" % i, 512, F32) for i in range(2)]
        Bx1c = [Buf("x1c%d" % i) for i in range(2)]
        hT = self.hT
        TP = 7
        rr = [0]

        def nb():
            b = rr[0] % 7
            rr[0] += 1
            return b

        branches = [(0, [0, 1, 2]), (1, [3, 4, 5]), (2, [6, 7])]
        for j in range(NS):
            src = self.oTs[:, :, j * 512:(j + 1) * 512].rearrange("q p t -> p q t")
            S.dma(lambda src=src: nc.sync.dma_start(out=oT[:].rearrange("p (q t) -> p q t", q=8), in_=src),
                  reads=[self.BoTs[j]], writes=[BoT])
            self.make_hT(self.xown, j * 512, 4, TP, keep_x=True)
            for c in range(8):
                for (b, qs) in branches:
                    gbk = nb()
                    self.mm_group(gbk, 512, [(wg[:, kc * 3072 + b * 1024 + c * 128:kc * 3072 + b * 1024 + c * 128 + 128],
                                              hT[:, kc * 512:(kc + 1) * 512]) for kc in range(8)], [Bwg, self.BhT])
                    ubk = nb()
                    self.mm_group(ubk, 512, [(wup[:, q * 1024 + c * 128:q * 1024 + c * 128 + 128],
                                              oT[:, q * 512:(q + 1) * 512]) for q in qs], [Bwup, BoT])
                    S.op("act", lambda b=b, gbk=gbk: nc.scalar.activation(out=sg[b][:], in_=self.banks[gbk][:, :],
                                                                          func=AF.Sigmoid),
                         reads=[self.Bbank[gbk]], writes=[Bsg[b]])
                    S.op("dve", lambda b=b, ubk=ubk: nc.vector.tensor_tensor(out=tt[b][:], in0=sg[b][:],
                                                                             in1=self.banks[ubk][:, :], op=ALU.mult),
                         reads=[Bsg[b], self.Bbank[ubk]], writes=[Btt[b]])
                S.op("pool", lambda: nc.gpsimd.tensor_tensor(out=tt[0][:], in0=tt[0][:], in1=tt[1][:], op=ALU.add),
                     reads=[Btt[0], Btt[1]], writes=[Btt[0]])
                S.op("pool", lambda c=c: nc.gpsimd.tensor_tensor(out=mixb[:, c * 512:(c + 1) * 512], in0=tt[0][:],
                                                                 in1=tt[2][:], op=ALU.add),
                     reads=[Btt[0], Btt[2]], writes=[Bmix[c]])
            for c in range(8):
                bk = nb()
                extra = [(self.banks[bk][:, sub * 128:(sub + 1) * 128],
                          self.xt[:, sub * D + c * 128:sub * D + c * 128 + 128], self.ident[:]) for sub in range(4)]
                self.mm_group(bk, 512, [(wout[:, k * 1024 + c * 128:k * 1024 + c * 128 + 128],
                                         mixb[:, k * 512:(k + 1) * 512]) for k in range(8)],
                              [Bwout, self.Bconst] + Bmix + self.Bxt, extra=extra)
                xc = x1c[c % 2]
                S.op("act", lambda bk=bk, xc=xc: nc.scalar.copy(out=xc[:], in_=self.banks[bk][:, :]),
                     reads=[self.Bbank[bk]], writes=[Bx1c[c % 2]])
                S.dma(lambda xc=xc, c=c, j=j: nc.sync.dma_start(out=self.x1s[j * 8 + c], in_=xc[:]),
                      reads=[Bx1c[c % 2]], writes=[self.Bx1s[j]])

    def phase_f2(self):
        nc, S, A = self.nc, self.S, self.A
        NS = self.NS
        NF = DFF // 128
        wfi = A.alloc("wfi", 8 * 2 * DFF, BF16)
        wfd = A.alloc("wfd", NF * 1024, BF16)
        Bwfi, Bwfd = Buf("wfi"), Buf("wfd")

        def loadw():
            self.load_w(self.w_ffn_in, 0, 8, 0, 2 * DFF, wfi, 2 * DFF, 0, Bwfi, gcol_off=16)
            self.load_w(self.w_ffn_down, 0, NF, 0, 1024, wfd, 1024, 0, Bwfd)
        self.with_stage(loadw)
        x1T = A.alloc("x1T", 8 * 512, F32)
        Bx1 = [Buf("x1T%d" % c) for c in range(8)]
        sq = [A.alloc("sq%d" % i, 512, F32) for i in range(2)]
        Bsq = [Buf("sq%d" % i) for i in range(2)]
        rstd = A.alloc("rstd", 512, F32)
        Brstd = Buf("rstd")
        h2T = A.alloc("h2T", 8 * 512, BF16)
        Bh2 = Buf("h2T")
        ffT = A.alloc("ffT", NF * 512, BF16)
        Bff = [Buf("ff%d" % f) for f in range(NF)]
        sl = [A.alloc("sl%d" % i, 512, F32) for i in range(2)]
        Bsl = [Buf("sl%d" % i) for i in range(2)]
        ot = A.alloc("ot", 1024, F32)
        Bot = Buf("ot")
        SSB = 7
        rr = [0]

        def nb():
            b = rr[0] % 7
            rr[0] += 1
            return b

        for j in range(NS):
            for c in range(8):
                S.dma(lambda c=c, j=j: nc.sync.dma_start(out=x1T[:, c * 512:(c + 1) * 512], in_=self.x1s[j * 8 + c]),
                      reads=[self.Bx1s[j]], writes=[Bx1[c]])
            for c in range(8):
                S.op("act", lambda c=c: nc.scalar.activation(out=sq[c % 2][:], in_=x1T[:, c * 512:(c + 1) * 512],
                                                             func=AF.Square), reads=[Bx1[c]], writes=[Bsq[c % 2]])
                S.op("pe", lambda c=c: nc.tensor.matmul(self.banks[SSB][:, :], lhsT=self.onesf[:], rhs=sq[c % 2][:],
                                                        start=(c == 0), stop=(c == 7)),
                     reads=[Bsq[c % 2], self.Bconst], writes=[self.Bbank[SSB]], inc=True)
            S.op("act", lambda: nc.scalar.activation(out=rstd[:], in_=self.banks[SSB][:, :], func=AF.Ln,
                                                     bias=self.cst[:, 0:1], scale=1.0 / D),
                 reads=[self.Bbank[SSB], self.Bconst], writes=[Brstd])
            S.op("act", lambda: nc.scalar.activation(out=rstd[:], in_=rstd[:], func=AF.Exp, scale=-0.5),
                 reads=[Brstd], writes=[Brstd])
            for c in range(8):
                S.op("dve", lambda c=c: nc.vector.tensor_tensor(out=h2T[:, c * 512:(c + 1) * 512],
                                                                in0=x1T[:, c * 512:(c + 1) * 512], in1=rstd[:],
                                                                op=ALU.mult), reads=[Bx1[c], Brstd], writes=[Bh2])
            for f in range(NF):
                gbk = nb()
                self.mm_group(gbk, 512, [(wfi[:, kc * 2 * DFF + f * 128:kc * 2 * DFF + f * 128 + 128],
                                          h2T[:, kc * 512:(kc + 1) * 512]) for kc in range(8)], [Bwfi, Bh2])
                ubk = nb()
                self.mm_group(ubk, 512, [(wfi[:, kc * 2 * DFF + DFF + f * 128:kc * 2 * DFF + DFF + f * 128 + 128],
                                          h2T[:, kc * 512:(kc + 1) * 512]) for kc in range(8)], [Bwfi, Bh2])
                S.op("act", lambda f=f, gbk=gbk: nc.scalar.activation(out=sl[f % 2][:], in_=self.banks[gbk][:, :],
                                                                      func=AF.Silu),
                     reads=[self.Bbank[gbk]], writes=[Bsl[f % 2]])
                S.op("dve", lambda f=f, ubk=ubk: nc.vector.tensor_tensor(out=ffT[:, f * 512:(f + 1) * 512], in0=sl[f % 2][:],
                                                                         in1=self.banks[ubk][:, :], op=ALU.mult),
                     reads=[Bsl[f % 2], self.Bbank[ubk]], writes=[Bff[f]])
            for sub in range(4):
                for half in range(2):
                    bk = nb()
                    extra = [(self.banks[bk][:, (c % 4) * 128:(c % 4) * 128 + 128],
                              x1T[:, c * 512 + sub * 128:c * 512 + sub * 128 + 128], self.ident[:])
                             for c in range(half * 4, half * 4 + 4)]
                    self.mm_group(bk, 512, [(ffT[:, f * 512 + sub * 128:f * 512 + sub * 128 + 128],
                                             wfd[:, f * 1024 + half * 512:f * 1024 + half * 512 + 512]) for f in range(NF)],
                                  [Bwfd, self.Bconst] + Bff + Bx1, extra=extra)
                    if half == 0:
                        S.op("act", lambda bk=bk: nc.scalar.copy(out=ot[:, 0:512], in_=self.banks[bk][:, :]),
                             reads=[self.Bbank[bk]], writes=[Bot])
                    else:
                        S.op("dve", lambda bk=bk: nc.vector.tensor_copy(out=ot[:, 512:1024], in_=self.banks[bk][:, :]),
                             reads=[self.Bbank[bk]], writes=[Bot])
                r0 = j * 512 + sub * 128
                S.dma(lambda r0=r0: nc.sync.dma_start(out=self.out[r0:r0 + 128, :], in_=ot[:]), reads=[Bot])


def own_tiles(role, NS):
    tiles = []
    for j in range(NS):
        early = (j % 2 == 0) if role == 0 else (j % 2 == 1)
        tiles.append(2 * j if early else 2 * j + 1)
    return tiles


def host_constants(role, NT):
    NS = NT // 2
    SEQ = NT * 512
    OWN = NS * 512
    tiles = own_tiles(role, NS)
    half = HD // 2
    inv_freq = (np.float32(10000.0) ** (-(np.arange(half, dtype=np.float32) * np.float32(2.0) / np.float32(HD)))).astype(np.float32)
    pos = np.arange(SEQ, dtype=np.float32)
    ang = (pos[:, None] * inv_freq[None, :]).astype(np.float32)
    cos = np.cos(ang).astype(np.float32)
    sin = np.sin(ang).astype(np.float32)
    fidx = np.arange(128) % 32
    cosK = np.ascontiguousarray(cos[:, fidx].T)
    sinK = np.ascontiguousarray(sin[:, fidx].T)
    own_pos = np.concatenate([np.arange(t * 512, (t + 1) * 512) for t in tiles])
    cosQ = np.ascontiguousarray(cosK[:, own_pos])
    sinQ = np.ascontiguousarray(sinK[:, own_pos])
    qblk = own_pos // 256
    n = np.arange(32)
    pb = np.where(n[None, :] < qblk[:, None], 0.0,
                  np.where(n[None, :] == qblk[:, None], 1e30, -1e30)).astype(np.float32)
    def lay(a):
        a = np.tile(a[:, None, :], (1, 6, 1)).reshape(OWN // 128, 128, 192)
        return np.ascontiguousarray(a.transpose(1, 0, 2).reshape(128, -1))
    pastb = lay(pb)
    sbmask = np.zeros((2, 128, 8, 512), np.float32)
    momask = np.zeros((2, 128, 8, 512), np.float32)
    s_idx = np.arange(128)[:, None]
    q_idx = np.arange(512)[None, :]
    for par in range(2):
        early = (par == 0) if role == 0 else (par == 1)
        own_off = 0 if early else 512
        for wi in range(8):
            kpos = wi * 128 + s_idx
            qpos = own_off + q_idx
            sbmask[par, :, wi, :] = np.where(kpos < qpos, 0.0, NEG)
            same_blk = (kpos // 256) == (qpos // 256)
            momask[par, :, wi, :] = np.where(same_blk & (kpos > qpos), NEG, 0.0)
    bf = ml_dtypes.bfloat16
    return dict(cosK=cosK, sinK=sinK, cosQ=cosQ, sinQ=sinQ, pastb=pastb,
                sbmask=sbmask.reshape(2, 128, 4096).astype(bf), momask=momask.reshape(2, 128, 4096).astype(bf))


_CACHE = {}


STOP_AFTER = 4
NSTAGE = 4
MO_STOP = 5
VAR = set()
STRESS = 0


def _program(NT, debug=False):
    key = (NT, debug, STOP_AFTER, MO_STOP, tuple(sorted(VAR)), STRESS)
    if key not in _CACHE:
        b = Builder(NT, debug, STOP_AFTER)
        nc = b.build()
        _CACHE[key] = nc
    return _CACHE[key]


WNAMES = ["w_in", "w_mem_kv", "w_up_sb", "w_up_moba", "w_up_mem", "w_out", "w_ffn_in", "w_ffn_down",
          "mix_norm_g", "mem_norm_g", "ffn_norm_g", "moba_q_norm_g", "moba_k_norm_g", "mem_q_norm_g", "mem_k_norm_g"]


def run(inputs, debug=False, trace=False):
    x = np.asarray(inputs["x"], dtype=np.float32)
    mem = np.asarray(inputs["mem"], dtype=np.float32)
    B, SEQ, _ = x.shape
    NT = SEQ // 512
    NS = NT // 2
    assert B * 2 == N_CORES
    nc = _program(NT, debug)
    wmaps = {k: np.ascontiguousarray(np.asarray(inputs[k], dtype=np.float32)[0]) for k in WNAMES}
    consts = [host_constants(r, NT) for r in range(2)]
    in_maps = []
    for core in range(N_CORES):
        b, role = core // 2, core % 2
        tiles = own_tiles(role, NS)
        xown = np.concatenate([x[b, t * 512:(t + 1) * 512] for t in tiles], axis=0)
        m = dict(wmaps)
        m.update(consts[role])
        m["xseq"] = np.ascontiguousarray(x[b])
        m["xown"] = np.ascontiguousarray(xown)
        m["mem"] = np.ascontiguousarray(mem[b])
        in_maps.append(m)
    res = run_bass_kernel_spmd(nc, in_maps, core_ids=list(range(N_CORES)), **({"trace": True} if trace else {}))
    out = np.empty((B, SEQ, D), np.float32)
    for core in range(N_CORES):
        b, role = core // 2, core % 2
        tiles = own_tiles(role, NS)
        o = np.asarray(res.results[core]["out"])
        for j, t in enumerate(tiles):
            out[b, t * 512:(t + 1) * 512] = o[j * 512:(j + 1) * 512]
    return out, res


def kernel(**inputs):
    out, _ = run(inputs)
    return out
```

```python
import math
from contextlib import ExitStack

import numpy as np
import ml_dtypes

import concourse.bass as bass
import concourse.mybir as mybir
from concourse.bass_utils import run_bass_kernel_spmd

F32 = mybir.dt.float32
BF16 = mybir.dt.bfloat16
AF = mybir.ActivationFunctionType
ALU = mybir.AluOpType
AX = mybir.AxisListType

D = 1024
HD = 64
DFF = 2816
MEMLEN = 256
INW = 5632
NEG = -30000.0
EPS = 1e-6
N_CORES = 8


class Buf:
    __slots__ = ("name", "w", "r", "psum")

    def __init__(self, name, psum=False):
        self.name = name
        self.w = None
        self.r = []
        self.psum = psum


class Ev:
    __slots__ = ("eng", "seq", "sem", "val", "clock")

    def __init__(self, eng, seq):
        self.eng = eng
        self.seq = seq
        self.sem = None
        self.val = None
        self.clock = None


class Sched:
    def __init__(self, nc, engsems, dmasems):
        self.nc = nc
        self.eng = {"pe": nc.tensor, "act": nc.scalar, "dve": nc.vector,
                    "pool": nc.gpsimd, "sp": nc.sync}
        self.sem = engsems
        self.clock = {k: {} for k in self.eng}
        self.count = {k: 0 for k in self.eng}
        self.seq = {k: 0 for k in self.eng}
        self.pending = {k: [] for k in self.eng}
        self.last = {k: None for k in self.eng}
        self.dma_sems = list(dmasems)
        self.dma_cnt = [0] * len(dmasems)
        self.dma_last = [None] * len(dmasems)
        self.dma_rr = 0
        self.nwaits = 0
        self.nops = 0
        self.defer = None

    def _deps(self, reads, writes):
        deps = []
        for b in reads:
            if b.w is not None:
                deps.append(b.w)
            if b.psum:
                deps.extend(b.r)
        for b in writes:
            if b.w is not None:
                deps.append(b.w)
            deps.extend(b.r)
        return deps

    def _waits(self, e, deps):
        ck = self.clock[e]
        need = {}
        for d in deps:
            if d.eng == "pe" and e == "pe":
                continue
            if ck.get(d.eng, 0) >= d.seq:
                continue
            assert d.val is not None, "dependency on unresolved milestone (%s)" % d.eng
            cur = need.get(d.eng)
            if cur is None or cur.seq < d.seq:
                need[d.eng] = d
        waits = []
        for d in need.values():
            if ck.get(d.eng, 0) >= d.seq:
                continue
            waits.append((d.sem, d.val))
            for k, v in d.clock.items():
                if ck.get(k, 0) < v:
                    ck[k] = v
            if ck.get(d.eng, 0) < d.seq:
                ck[d.eng] = d.seq
        return waits

    def _apply(self, e, waits, make):
        eng = self.eng[e]
        self.nwaits += len(waits)
        for (s, v) in waits[1:]:
            eng.wait_ge(s, v)
        ins = make()
        if waits:
            ins._wait_ge(waits[0][0], waits[0][1])
        return ins

    def _record(self, ev, reads, writes):
        for b in reads:
            b.r.append(ev)
        for b in writes:
            b.w = ev
            b.r = []

    def start_defer(self):
        self.defer = []

    def end_defer(self):
        d, self.defer = self.defer, None
        return d

    def gate(self, pos):
        if self.defer is not None:
            self.ub()
            self.defer.append(("gate", pos))

    def ub(self):
        if self.defer is not None and self.defer and self.defer[-1] is not None:
            self.defer.append(None)

    def emit_deferred(self, q, nunits):
        while nunits > 0 and q:
            it = q.popleft()
            if it is None:
                nunits -= 1
                continue
            kind = it[0]
            if kind == "op":
                self.op(*it[1:])
            elif kind == "dma":
                self.dma(*it[1:])
            elif nunits < (1 << 29):
                q.appendleft(it)
                return

    def op(self, e, make, reads=(), writes=(), inc=True):
        if self.defer is not None:
            self.defer.append(("op", e, make, tuple(reads), tuple(writes), inc))
            return None
        self.nops += 1
        waits = self._waits(e, self._deps(reads, writes))
        ins = self._apply(e, waits, make)
        self.seq[e] += 1
        ev = Ev(e, self.seq[e])
        ev.sem = self.sem[e]
        ev.clock = dict(self.clock[e])
        if inc:
            self.count[e] += 1
            ins.then_inc(self.sem[e], 1)
            ev.val = self.count[e]
            for p in self.pending[e]:
                p.val = ev.val
            self.pending[e] = []
        else:
            self.pending[e].append(ev)
        self.last[e] = ev
        self._record(ev, reads, writes)
        return ev

    def dma(self, make, reads=(), writes=(), q="sp"):
        if self.defer is not None:
            self.defer.append(("dma", make, tuple(reads), tuple(writes), q))
            return None
        self.nops += 1
        j = self.dma_rr
        self.dma_rr = (self.dma_rr + 1) % len(self.dma_sems)
        deps = self._deps(reads, writes)
        if self.dma_last[j] is not None:
            deps.append(self.dma_last[j])
        waits = self._waits(q, deps)
        ins = self._apply(q, waits, make)
        self.dma_cnt[j] += 1
        ins.then_inc(self.dma_sems[j], 16)
        ev = Ev("dma%d" % j, self.dma_cnt[j])
        ev.sem = self.dma_sems[j]
        ev.val = 16 * self.dma_cnt[j]
        ev.clock = dict(self.clock[q])
        self.dma_last[j] = ev
        self._record(ev, reads, writes)
        return ev

    def barrier(self):
        evs = []
        for k in self.eng:
            if self.pending[k]:
                self.op(k, lambda k=k: self.eng[k].drain(), inc=True)
            if self.last[k] is not None and self.last[k].val is not None:
                evs.append(self.last[k])
        for d in self.dma_last:
            if d is not None:
                evs.append(d)
        for e in self.eng:
            waits = self._waits(e, evs)
            self.nwaits += len(waits)
            for (s, v) in waits:
                self.eng[e].wait_ge(s, v)


class Feeder:
    def __init__(self, S, entries, steps):
        from collections import deque
        self.S = S
        self.q = deque(entries)
        self.units = sum(1 for x in entries if x is None) + 1
        self.steps = max(steps, 1)

    def step(self, pos=-1):
        if not self.q:
            return
        n = -(-self.units // self.steps)
        self.steps = max(self.steps - 1, 1)
        while n > 0 and self.q:
            head = self.q[0]
            if head is not None and head[0] == "gate":
                if pos >= head[1]:
                    self.q.popleft()
                    continue
                return
            self.S.emit_deferred(self.q, 1)
            self.units = max(self.units - 1, 0)
            n -= 1

    def drain(self):
        self.S.emit_deferred(self.q, 1 << 30)


class SbAlloc:
    def __init__(self, nc, limit):
        self.nc = nc
        self.off = 16512
        self.limit = limit
        self.n = 0
        self.peak = 0

    def alloc(self, name, cols, dt):
        isz = 4 if dt == F32 else 2
        nbytes = (cols * isz + 63) // 64 * 64
        off = self.off
        assert off + nbytes <= self.limit, "SBUF overflow at %s: %d + %d > %d" % (name, off, nbytes, self.limit)
        self.off += nbytes
        self.peak = max(self.peak, self.off)
        self.n += 1
        return self.nc.alloc_sbuf_tensor_at("%s_%d" % (name, self.n), [128, cols], dt, offset=off)

    def mark(self):
        return self.off

    def release(self, m):
        self.off = m


class Builder:
    def __init__(self, NT, debug=False, stop_after=4):
        self.stop_after = stop_after
        assert NT % 4 == 0
        self.NT = NT
        self.NS = NT // 2
        self.SEQ = NT * 512
        self.NKB = self.SEQ // 128
        self.OWN = self.NS * 512
        self.debug = debug
        self.nc = bass.Bass("TRN2", target_bir_lowering=False)
        self.bank_rr = 0

    def din(self, name, shape, dt=F32):
        return self.nc.dram_tensor(name, list(shape), dt, kind="ExternalInput").ap()

    def dout(self, name, shape, dt=F32):
        return self.nc.dram_tensor(name, list(shape), dt, kind="ExternalOutput").ap()

    def build(self):
        nc = self.nc
        SEQ, OWN, NS = self.SEQ, self.OWN, self.NS
        self.xseq = self.din("xseq", [SEQ, D])
        self.xown = self.din("xown", [OWN, D])
        self.mem = self.din("mem", [MEMLEN, D])
        self.w_in = self.din("w_in", [D, INW])
        self.w_mem_kv = self.din("w_mem_kv", [D, 512])
        self.w_up_sb = self.din("w_up_sb", [384, D])
        self.w_up_moba = self.din("w_up_moba", [384, D])
        self.w_up_mem = self.din("w_up_mem", [256, D])
        self.w_out = self.din("w_out", [D, D])
        self.w_ffn_in = self.din("w_ffn_in", [D, 2 * DFF])
        self.w_ffn_down = self.din("w_ffn_down", [DFF, D])
        self.g_mix = self.din("mix_norm_g", [D])
        self.g_memn = self.din("mem_norm_g", [D])
        self.g_ffn = self.din("ffn_norm_g", [D])
        self.g_moq = self.din("moba_q_norm_g", [HD])
        self.g_mok = self.din("moba_k_norm_g", [HD])
        self.g_meq = self.din("mem_q_norm_g", [HD])
        self.g_mek = self.din("mem_k_norm_g", [HD])
        self.cosK = self.din("cosK", [128, SEQ])
        self.sinK = self.din("sinK", [128, SEQ])
        self.cosQ = self.din("cosQ", [128, OWN])
        self.sinQ = self.din("sinQ", [128, OWN])
        self.pastb = self.din("pastb", [128, NS * 4 * 192])
        self.sbmask = self.din("sbmask", [2, 128, 8 * 512], BF16)
        self.momask = self.din("momask", [2, 128, 8 * 512], BF16)
        self.out = self.dout("out", [OWN, D])
        if self.debug:
            self.oTs = self.dout("oTs", [8, 128, OWN], BF16)
            self.x1s = self.dout("x1s", [NS * 8, 128, 512], F32)
        else:
            self.oTs = nc.dram_tensor("oTs", [8, 128, OWN], BF16).ap()
            self.x1s = nc.dram_tensor("x1s", [NS * 8, 128, 512], F32).ap()
        self.BoTs = [Buf("oTs%d" % j) for j in range(NS)]
        self.Bx1s = [Buf("x1s%d" % j) for j in range(NS)]

        with ExitStack() as es:
            engsems = {k: es.enter_context(nc.semaphore("s_" + k)) for k in ["pe", "act", "dve", "pool", "sp"]}
            dmasems = [es.enter_context(nc.semaphore("d%d" % i)) for i in range(24)]
            self.S = Sched(nc, engsems, dmasems)
            self.A = SbAlloc(nc, 229376)
            self.banks = [nc.alloc_psum_tensor("bank%d" % i, [128, 512], F32) for i in range(8)]
            self.Bbank = [Buf("bank%d" % i, psum=True) for i in range(8)]
            self.consts()
            m0 = self.A.mark()
            for i, ph in enumerate((self.phase_sb, self.phase_mo, self.phase_f1)):
                if self.stop_after < i + 1:
                    break
                if STRESS and i == 0:
                    continue
                ph()
                self.S.barrier()
                self.A.release(m0)
            if self.stop_after >= 4:
                self.A.release(self.m_consts)
                self.phase_f2()
            self.S.barrier()
        return nc

    def consts(self):
        nc, S, A = self.nc, self.S, self.A
        self.Bconst = Bc = Buf("consts")

        def diag_select(ap, ncols, base, op=ALU.is_equal):
            S.op("pool", lambda: nc.gpsimd.affine_select(out=ap, in_=ap, pattern=[[-1, ncols]], compare_op=op,
                                                         fill=0.0, base=base, channel_multiplier=1),
                 reads=[Bc], writes=[Bc])

        self.ident = A.alloc("ident", 128, F32)
        S.op("pool", lambda: nc.gpsimd.memset(self.ident[:], 1.0), writes=[Bc])
        diag_select(self.ident[:], 128, 0)
        self.identb = A.alloc("identb", 128, BF16)
        S.op("pool", lambda: nc.gpsimd.memset(self.identb[:], 1.0), writes=[Bc])
        diag_select(self.identb[:], 128, 0)
        self.trineg = A.alloc("trineg", 128, BF16)
        S.op("pool", lambda: nc.gpsimd.memset(self.trineg[:], -1.0), writes=[Bc])
        diag_select(self.trineg[:], 128, 0, ALU.is_ge)
        self.negones = A.alloc("negones", 128, BF16)
        S.op("pool", lambda: nc.gpsimd.memset(self.negones[:], -1.0), writes=[Bc])
        self.onesb = A.alloc("onesb", 128, BF16)
        S.op("pool", lambda: nc.gpsimd.memset(self.onesb[:], 1.0), writes=[Bc])
        self.onesf = A.alloc("onesf", 128, F32)
        S.op("pool", lambda: nc.gpsimd.memset(self.onesf[:], 1.0), writes=[Bc])
        self.blk64 = A.alloc("blk64", 128, F32)
        S.op("pool", lambda: nc.gpsimd.memset(self.blk64[:], 0.0), writes=[Bc])
        S.op("pool", lambda: nc.gpsimd.memset(self.blk64[0:64, 0:64], 1.0 / 64), reads=[Bc], writes=[Bc])
        S.op("pool", lambda: nc.gpsimd.memset(self.blk64[64:128, 64:128], 1.0 / 64), reads=[Bc], writes=[Bc])
        self.rot = A.alloc("rot", 128, F32)
        for (c0, val, base) in ((0, -1.0, -32), (32, 1.0, 0), (64, -1.0, -96), (96, 1.0, -64)):
            S.op("pool", lambda c0=c0, val=val: nc.gpsimd.memset(self.rot[:, c0:c0 + 32], val), reads=[Bc], writes=[Bc])
            diag_select(self.rot[:, c0:c0 + 32], 32, base)
        self.cst = A.alloc("cst", 8, F32)
        S.op("pool", lambda: nc.gpsimd.memset(self.cst[:, 0:1], EPS), writes=[Bc])
        S.op("pool", lambda: nc.gpsimd.memset(self.cst[:, 1:2], 1.0), reads=[Bc], writes=[Bc])
        self.esel = A.alloc("esel", 32 * 128, BF16)
        mtmp = A.mark()
        etmp = A.alloc("etmp", 32 * 128, BF16)
        for g in range(3):
            dstt = self.esel if g == 0 else etmp
            v = dstt[:].rearrange("p (n m) -> p n m", n=32)
            S.op("pool", lambda dstt=dstt: nc.gpsimd.memset(dstt[:], 1.0), reads=[Bc], writes=[Bc])
            S.op("pool", lambda v=v, g=g: nc.gpsimd.affine_select(
                out=v, in_=v, pattern=[[-1, 32], [0, 128]], compare_op=ALU.is_equal, fill=0.0, base=-32 * g,
                channel_multiplier=1), reads=[Bc], writes=[Bc])
            if g > 0:
                S.op("pool", lambda: nc.gpsimd.tensor_tensor(out=self.esel[:], in0=self.esel[:], in1=etmp[:], op=ALU.add),
                     reads=[Bc], writes=[Bc])
        S.barrier()
        A.release(mtmp)
        self.gcol = A.alloc("gcol", 24, F32)
        self.Bg = Buf("gains")
        for i, g in enumerate([self.g_mix, self.g_memn, self.g_ffn]):
            S.dma(lambda i=i, g=g: nc.sync.dma_start(out=self.gcol[:, 8 * i:8 * i + 8],
                                                     in_=g.rearrange("(c p) -> p c", p=128),
                                                     allow_slow_non_contiguous=True), writes=[self.Bg])
        self.gh = A.alloc("gh", 8, F32)
        for i, g in enumerate([self.g_moq, self.g_mok, self.g_meq, self.g_mek]):
            g2 = g.rearrange("(p o) -> p o", o=1)
            for half in range(2):
                S.dma(lambda i=i, g2=g2, half=half: nc.sync.dma_start(
                    out=self.gh[64 * half:64 * half + 64, 2 * i:2 * i + 1], in_=g2[0:64, :]), writes=[self.Bg])
                for q in range(2):
                    S.dma(lambda i=i, g2=g2, half=half, q=q: nc.sync.dma_start(
                        out=self.gh[64 * half + 32 * q:64 * half + 32 * q + 32, 2 * i + 1:2 * i + 2],
                        in_=g2[32 * (1 - q):32 * (1 - q) + 32, :]), writes=[self.Bg])
        self.m_consts = A.mark()
        self.xs = [A.alloc("xs%d" % i, D, BF16) for i in range(2)]
        self.Bxs = [Buf("xs%d" % i) for i in range(2)]
        self.stat = A.alloc("stat", 16, F32)
        self.Bstat = [Buf("stat%d" % i) for i in range(2)]
        self.hT = A.alloc("hT", 8 * 512, BF16)
        self.BhT = Buf("hT")
        self.oTslot = A.alloc("oTslot", 8 * 512, BF16)
        self.BoTslot = Buf("oTslot")
        self.sub_rr = 0

    def alloc_x(self, n):
        self.xt = self.A.alloc("xt", n * D, F32)
        self.Bxt = [Buf("xt%d" % i) for i in range(n)]

    def with_stage(self, fn, keep=False):
        m = self.A.mark()
        self.stage = [self.A.alloc("stage%d" % i, 2048, F32) for i in range(NSTAGE)]
        self.Bstage = [Buf("stage%d" % i) for i in range(NSTAGE)]
        self.stage_rr = 0
        fn()
        self.S.barrier()
        if not keep:
            self.A.release(m)
        self._stage_mark = m

    def load_w(self, wd, k0, nk, c0, ncols, dst, dstw, dst_c0, Bdst, gcol_off=None):
        nc, S = self.nc, self.S
        for kc in range(nk):
            for cc in range(0, ncols, 2048):
                n = min(2048, ncols - cc)
                si = self.stage_rr
                self.stage_rr = (self.stage_rr + 1) % NSTAGE
                st = self.stage[si]
                Bst = self.Bstage[si]
                S.dma(lambda kc=kc, cc=cc, n=n, st=st: nc.sync.dma_start(
                    out=st[:, 0:n], in_=wd[k0 + kc * 128:k0 + kc * 128 + 128, c0 + cc:c0 + cc + n]), writes=[Bst])
                o0 = kc * dstw + dst_c0 + cc
                if gcol_off is None:
                    S.op("dve", lambda st=st, n=n, o0=o0: nc.vector.tensor_copy(out=dst[:, o0:o0 + n], in_=st[:, 0:n]),
                         reads=[Bst], writes=[Bdst])
                else:
                    S.op("dve", lambda st=st, n=n, o0=o0, kc=kc: nc.vector.tensor_scalar(
                        out=dst[:, o0:o0 + n], in0=st[:, 0:n],
                        scalar1=self.gcol[:, gcol_off + kc:gcol_off + kc + 1], scalar2=None, op0=ALU.mult),
                        reads=[Bst, self.Bg], writes=[Bdst])

    def make_hT(self, xd, row0, nsub, tpbank, keep_x=False):
        nc, S = self.nc, self.S
        Btp = self.Bbank[tpbank]
        tpb = self.banks[tpbank][:].bitcast(BF16)
        hTv = self.hT[:].rearrange("p (c t) -> p c t", c=8)
        for j in range(nsub):
            xi = j if keep_x else (self.sub_rr % 2)
            si = self.sub_rr % 2
            self.sub_rr += 1
            xt = self.xt[:, xi * D:(xi + 1) * D]
            Bx = self.Bxt[xi]
            S.dma(lambda xt=xt, j=j: nc.sync.dma_start(out=xt, in_=xd[row0 + j * 128:row0 + j * 128 + 128, :]), writes=[Bx])
            ss = self.stat[:, 4 * si:4 * si + 1]
            lnv = self.stat[:, 4 * si + 1:4 * si + 2]
            rstd = self.stat[:, 4 * si + 2:4 * si + 3]
            Bs = self.Bstat[si]
            S.op("act", lambda xt=xt, ss=ss, si=si: nc.scalar.activation(out=self.xs[si][:], in_=xt, func=AF.Square,
                                                                        accum_out=ss),
                 reads=[Bx], writes=[self.Bxs[si], Bs])
            S.op("act", lambda ss=ss, lnv=lnv: nc.scalar.activation(out=lnv, in_=ss, func=AF.Ln, bias=self.cst[:, 0:1],
                                                                   scale=1.0 / D), reads=[Bs, self.Bconst], writes=[Bs])
            S.op("act", lambda lnv=lnv, rstd=rstd: nc.scalar.activation(out=rstd, in_=lnv, func=AF.Exp, scale=-0.5),
                 reads=[Bs], writes=[Bs])
            xs = self.xs[si]
            Bxs = self.Bxs[si]
            S.op("dve", lambda xs=xs, xt=xt, rstd=rstd: nc.vector.tensor_scalar(
                out=xs[:], in0=xt, scalar1=rstd, scalar2=None, op0=ALU.mult), reads=[Bx, Bs], writes=[Bxs])
            S.ub()
            for c in range(8):
                S.op("pe", lambda c=c, xs=xs: nc.tensor.transpose(
                    out=tpb[:, c * 128:(c + 1) * 128], in_=xs[:, c * 128:(c + 1) * 128], identity=self.identb[:]),
                    reads=[Bxs, self.Bconst], writes=[Btp], inc=(c == 7))
            S.ub()
            S.op("dve", lambda j=j: nc.vector.tensor_copy(
                out=hTv[:, :, j * 128:(j + 1) * 128], in_=tpb.rearrange("p (c t) -> p c t", c=8)),
                reads=[Btp], writes=[self.BhT])
            S.ub()

    def mm_group(self, bank, ncols, pairs, reads, extra=None, inc=True):
        nc, S = self.nc, self.S
        n = len(pairs) + (len(extra) if extra else 0)
        out = self.banks[bank][:, 0:ncols]
        k = 0
        for (l, r) in pairs:
            k += 1
            S.op("pe", lambda l=l, r=r, k=k: nc.tensor.matmul(out, lhsT=l, rhs=r, start=(k == 1), stop=(k == n)),
                 reads=reads, writes=[self.Bbank[bank]], inc=(inc and k == n))
        for (o, l, r) in (extra or []):
            k += 1
            S.op("pe", lambda o=o, l=l, r=r, k=k: nc.tensor.matmul(o, lhsT=l, rhs=r, start=False, stop=(k == n)),
                 reads=reads, writes=[self.Bbank[bank]], inc=(inc and k == n))
        S.ub()

    def store_oT(self, j, q0, nq):
        nc, S = self.nc, self.S
        src = self.oTslot[:, q0 * 512:(q0 + nq) * 512].rearrange("p (q t) -> p q t", q=nq)
        dst = self.oTs[q0:q0 + nq, :, j * 512:(j + 1) * 512].rearrange("q p t -> p q t")
        S.dma(lambda: nc.sync.dma_start(out=dst, in_=src), reads=[self.BoTslot], writes=[self.BoTs[j]])

    def phase_sb(self):
        nc, S, A = self.nc, self.S, self.A
        SEQ, NKB, NS, NT = self.SEQ, self.NKB, self.NS, self.NT
        self.alloc_x(2)
        wsb = A.alloc("wsb", 8 * 1152, BF16)
        Bw = Buf("wsb")
        KT = A.alloc("KT", 3 * SEQ, BF16)
        V = A.alloc("V", NKB * 384, BF16)
        BKT = [Buf("KT%d" % t) for t in range(NT)]
        BV = [Buf("V%d" % t) for t in range(NT)]
        self.with_stage(lambda: self.load_w(self.w_in, 0, 8, 0, 1152, wsb, 1152, 0, Bw, gcol_off=0))
        mask = A.alloc("mask", 8 * 512, BF16)
        Bmask = Buf("mask")
        QTs = [A.alloc("QT%d" % i, 3 * 512, BF16) for i in range(2)]
        BQTs = [Buf("QT%d" % i) for i in range(2)]
        NE, NL, NA = 3, 3, 2
        eb = [A.alloc("e%d" % i, 512, F32) for i in range(NE)]
        Be = [Buf("e%d" % i) for i in range(NE)]
        Lb = [A.alloc("L%d" % i, 512, BF16) for i in range(NL)]
        BL = [Buf("L%d" % i) for i in range(NL)]
        wb = [A.alloc("w%d" % i, 512, F32) for i in range(2)]
        Bwb = [Buf("w%d" % i) for i in range(2)]
        Ab = [A.alloc("A%d" % i, 512, BF16) for i in range(NA)]
        BA = [Buf("A%d" % i) for i in range(NA)]
        Rb = [A.alloc("R%d" % i, 512, BF16) for i in range(2)]
        BR = [Buf("R%d" % i) for i in range(2)]
        hT = self.hT
        TP, ACC = 6, 7

        def evac(bank, ncols, dst, Bdst, eng="dve"):
            if eng == "dve":
                S.op("dve", lambda: nc.vector.tensor_copy(out=dst, in_=self.banks[bank][:, 0:ncols]),
                     reads=[self.Bbank[bank]], writes=[Bdst])
            else:
                S.op("act", lambda: nc.scalar.copy(out=dst, in_=self.banks[bank][:, 0:ncols]),
                     reads=[self.Bbank[bank]], writes=[Bdst])
            S.ub()

        def proj_kv(t):
            self.make_hT(self.xseq, t * 512, 4, TP)
            for pair in range(3):
                self.mm_group(ACC, 512, [(wsb[:, kc * 1152 + 384 + pair * 128:kc * 1152 + 384 + pair * 128 + 128],
                                          hT[:, kc * 512:(kc + 1) * 512]) for kc in range(8)], [Bw, self.BhT])
                evac(ACC, 512, KT[:, pair * SEQ + t * 512:pair * SEQ + (t + 1) * 512], BKT[t])
            for sub in range(4):
                self.mm_group(ACC, 384, [(hT[:, kc * 512 + sub * 128:kc * 512 + sub * 128 + 128],
                                          wsb[:, kc * 1152 + 768:kc * 1152 + 1152]) for kc in range(8)], [Bw, self.BhT])
                kb = t * 4 + sub
                evac(ACC, 384, V[:, kb * 384:(kb + 1) * 384], BV[t])

        def proj_q(j):
            QT, BQT = QTs[j % 2], BQTs[j % 2]
            self.make_hT(self.xown, j * 512, 4, TP)
            for pair in range(3):
                self.mm_group(ACC, 512, [(wsb[:, kc * 1152 + pair * 128:kc * 1152 + pair * 128 + 128],
                                          hT[:, kc * 512:(kc + 1) * 512]) for kc in range(8)], [Bw, self.BhT])
                evac(ACC, 512, QT[:, pair * 512:(pair + 1) * 512], BQT)

        def attn(j, feeder=None):
            QT, BQT = QTs[j % 2], BQTs[j % 2]
            nj = 8 * (j + 1)
            its = [(h, i) for h in range(6) for i in range(nj)]
            N = len(its)
            zb, Tb, Ob = [0, 1], [2, 3], [4, 5]

            def emit_z(s):
                h, i = its[s]
                kb = nj - 1 - i
                pair, hp = h // 2, h % 2
                bank = zb[s % 2]
                inwin = i < 8
                extra = None
                if inwin:
                    wi = 7 - i
                    extra = [(self.banks[bank][:, :], self.identb[:], mask[:, wi * 512:(wi + 1) * 512])]
                self.mm_group(bank, 512,
                              [(KT[hp * 64:hp * 64 + 64, pair * SEQ + kb * 128:pair * SEQ + kb * 128 + 128],
                                QT[hp * 64:hp * 64 + 64, pair * 512:(pair + 1) * 512])],
                              [BKT[kb // 4], BQT, Bmask, self.Bconst], extra=extra)

            def emit_eL(s):
                bank = zb[s % 2]
                e, L = eb[s % NE], Lb[s % NL]
                S.op("act", lambda: nc.scalar.activation(out=e[:], in_=self.banks[bank][:, :], func=AF.Exp, scale=0.125),
                     reads=[self.Bbank[bank]], writes=[Be[s % NE]])
                S.op("act", lambda: nc.scalar.activation(out=L[:], in_=e[:], func=AF.Ln, bias=self.cst[:, 1:2]),
                     reads=[Be[s % NE], self.Bconst], writes=[BL[s % NL]])

            def emit_T(s):
                h, i = its[s]
                bank = Tb[s % 2]
                pairs = [(self.trineg[:], Lb[s % NL][:])]
                rd = [BL[s % NL], self.Bconst]
                if i > 0:
                    pairs.append((self.negones[:], Rb[(i - 1) % 2][:]))
                    rd.append(BR[(i - 1) % 2])
                self.mm_group(bank, 512, pairs, rd)
                w = wb[s % 2]
                S.op("act", lambda: nc.scalar.activation(out=w[:], in_=self.banks[bank][:, :], func=AF.Exp),
                     reads=[self.Bbank[bank]], writes=[Bwb[s % 2]])
                Aa = Ab[s % NA]
                S.op("dve", lambda: nc.vector.tensor_tensor(out=Aa[:], in0=eb[s % NE][:], in1=w[:], op=ALU.mult),
                     reads=[Be[s % NE], Bwb[s % 2]], writes=[BA[s % NA]])
                if i < nj - 1:
                    if i == 0:
                        S.op("pool", lambda: nc.gpsimd.tensor_copy(out=Rb[0][:], in_=Lb[s % NL][:]),
                             reads=[BL[s % NL]], writes=[BR[0]])
                    else:
                        S.op("pool", lambda: nc.gpsimd.tensor_tensor(out=Rb[i % 2][:], in0=Rb[(i - 1) % 2][:],
                                                                     in1=Lb[s % NL][:], op=ALU.add),
                             reads=[BL[s % NL], BR[(i - 1) % 2]], writes=[BR[i % 2]])

            def emit_AV(s):
                h, i = its[s]
                kb = nj - 1 - i
                pair, hp = h // 2, h % 2
                bank = Ob[h % 2]
                S.op("pe", lambda: nc.tensor.matmul(self.banks[bank][:, :],
                                                    lhsT=V[:, kb * 384 + pair * 128:kb * 384 + pair * 128 + 128],
                                                    rhs=Ab[s % NA][:], start=(i == 0), stop=(i == nj - 1)),
                     reads=[BV[kb // 4], BA[s % NA]], writes=[self.Bbank[bank]], inc=True)
                if i == nj - 1:
                    S.op("dve", lambda: nc.vector.tensor_copy(
                        out=self.oTslot[hp * 64:hp * 64 + 64, pair * 512:(pair + 1) * 512],
                        in_=self.banks[bank][hp * 64:hp * 64 + 64, :]),
                        reads=[self.Bbank[bank]], writes=[self.BoTslot])

            emit_z(0)
            for s in range(N + 2):
                if s + 1 < N:
                    emit_z(s + 1)
                if s < N:
                    emit_eL(s)
                if 1 <= s <= N:
                    emit_T(s - 1)
                if s >= 2:
                    emit_AV(s - 2)
                if feeder is not None:
                    feeder.step()

        proj_kv(0)
        proj_kv(1)
        proj_q(0)
        for j in range(NS):
            S.dma(lambda j=j: nc.sync.dma_start(out=mask[:], in_=self.sbmask[j % 2]), writes=[Bmask])
            feeder = None
            if j + 1 < NS:
                S.start_defer()
                proj_kv(2 * j + 2)
                proj_kv(2 * j + 3)
                proj_q(j + 1)
                feeder = Feeder(S, S.end_defer(), 48 * (j + 1))
            attn(j, feeder)
            if feeder is not None:
                feeder.drain()
            self.store_oT(j, 0, 3)

    def phase_mo(self):
        nc, S, A = self.nc, self.S, self.A
        SEQ, NKB, NS, NT = self.SEQ, self.NKB, self.NS, self.NT
        self.alloc_x(2)
        WW = 1408
        wmo = A.alloc("wmo", 8 * WW, BF16)
        Bw = Buf("wmo")
        KT = A.alloc("KT", 3 * SEQ, BF16)
        V = A.alloc("V", NKB * 384, BF16)
        BKT = [Buf("KT%d" % t) for t in range(NT)]
        BV = [Buf("V%d" % t) for t in range(NT)]
        ksum = A.alloc("ksum", 6 * 32, F32)
        Bks = Buf("ksum")
        S.op("pool", lambda: nc.gpsimd.memset(ksum[:], 0.0), writes=[Bks])
        memKT = A.alloc("memKT", 2 * 256, BF16)
        memV = A.alloc("memV", 2 * 256, BF16)
        Bmem = Buf("memkv")

        wmem = V
        Bwm = Buf("wmem")

        def loadw():
            self.load_w(self.w_in, 0, 8, 1152, 1408, wmo, WW, 0, Bw, gcol_off=0)
            self.load_w(self.w_mem_kv, 0, 8, 0, 512, wmem, 512, 0, Bwm, gcol_off=8)
        self.with_stage(loadw)

        mask = A.alloc("mask", 8 * 512, BF16)
        Bmask = Buf("mask")
        QTz = A.alloc("QTz", 6 * 512, BF16)
        Qf = A.alloc("Qf", 3 * 512, F32)
        QTmz = A.alloc("QTmz", 4 * 512, BF16)
        BQf = Buf("Qf")
        BQTz = [Buf("QTz%d" % h) for h in range(6)]
        BQTmz = [Buf("QTmz%d" % h) for h in range(4)]
        S.op("pool", lambda: nc.gpsimd.memset(QTz[:], 0.0), writes=BQTz)
        S.op("pool", lambda: nc.gpsimd.memset(QTmz[:], 0.0), writes=BQTmz)
        pb = A.alloc("pb", 4 * 192, F32)
        Bpb = Buf("pb")
        MBT = A.alloc("MBT", 1024, BF16)
        BMBT = Buf("MBT")
        NP = 2
        pbuf = [A.alloc("p%d" % i, 512, BF16) for i in range(NP)]
        Bp = [Buf("p%d" % i) for i in range(NP)]
        cs = A.alloc("cs", 1024, F32)
        Bcs = Buf("cs")
        rawsb = A.alloc("rawsb", 512, F32)
        rstd = A.alloc("rstd", 512, F32)
        ta = A.alloc("ta", 512, F32)
        tb = A.alloc("tb", 512, F32)
        sq = tb
        Braw, Brstd, Bta, Btb = Buf("rawsb"), Buf("rstd"), Buf("ta"), Buf("tb")
        Bsq = Btb
        Gp = A.alloc("Gp", 192, F32)
        sel = A.alloc("sel", 192, F32)
        MBb = A.alloc("MBb", 192, BF16)
        mx = A.alloc("mx", 56, F32)
        BGp, Bsel, BMBb, Bmx = Buf("Gp"), Buf("sel"), Buf("MBb"), Buf("mx")
        rec = A.alloc("rec", 512, F32)
        Brec = Buf("rec")
        hT = self.hT
        TP, ACC, MSB, ROTB = 6, 7, 5, 4

        def norm_qk(ncols, gi, rope, dst_bf, Bdst, dst_f=None, Bdstf=None):
            acc = self.banks[ACC][:, 0:ncols]
            gA = self.gh[:, 2 * gi:2 * gi + 1]
            gB = self.gh[:, 2 * gi + 1:2 * gi + 2]
            S.op("act", lambda: nc.scalar.copy(out=rawsb[:, 0:ncols], in_=acc), reads=[self.Bbank[ACC]], writes=[Braw])
            S.op("act", lambda: nc.scalar.activation(out=sq[:, 0:ncols], in_=acc, func=AF.Square),
                 reads=[self.Bbank[ACC]], writes=[Bsq])
            S.ub()
            self.mm_group(MSB, ncols, [(self.blk64[:], sq[:, 0:ncols])], [Bsq, self.Bconst])
            if rope and "norot" not in VAR:
                self.mm_group(ROTB, ncols, [(self.rot[:], rawsb[:, 0:ncols])], [Braw, self.Bconst])
            S.op("act", lambda: nc.scalar.activation(out=rstd[:, 0:ncols], in_=self.banks[MSB][:, 0:ncols], func=AF.Ln,
                                                     bias=self.cst[:, 0:1]), reads=[self.Bbank[MSB], self.Bconst],
                 writes=[Brstd])
            S.op("act", lambda: nc.scalar.activation(out=rstd[:, 0:ncols], in_=rstd[:, 0:ncols], func=AF.Exp, scale=-0.5),
                 reads=[Brstd], writes=[Brstd])
            S.ub()
            if rope:
                S.op("dve", lambda: nc.vector.scalar_tensor_tensor(out=ta[:, 0:ncols], in0=rawsb[:, 0:ncols], scalar=gA,
                                                                   in1=cs[:, 0:ncols], op0=ALU.mult, op1=ALU.mult),
                     reads=[Braw, Bcs, self.Bg], writes=[Bta])
                S.op("dve", lambda: nc.vector.scalar_tensor_tensor(out=tb[:, 0:ncols], in0=self.banks[ROTB][:, 0:ncols],
                                                                   scalar=gB, in1=cs[:, 512:512 + ncols],
                                                                   op0=ALU.mult, op1=ALU.mult),
                     reads=[self.Bbank[ROTB], Bcs, self.Bg], writes=[Btb])
                S.ub()
                pe_ = "dve" if "nopool" in VAR else "pool"
                pen_ = nc.vector if "nopool" in VAR else nc.gpsimd
                S.op(pe_, lambda: pen_.tensor_tensor(out=ta[:, 0:ncols], in0=ta[:, 0:ncols], in1=tb[:, 0:ncols],
                                                     op=ALU.add), reads=[Bta, Btb], writes=[Bta])
                S.ub()
                fin = dst_f if dst_f is not None else tb[:, 0:ncols]
                Bfin = Bdstf if dst_f is not None else Btb
                S.op("dve", lambda: nc.vector.tensor_tensor(out=fin, in0=ta[:, 0:ncols], in1=rstd[:, 0:ncols], op=ALU.mult),
                     reads=[Bta, Brstd], writes=[Bfin])
            else:
                fin, Bfin = ta[:, 0:ncols], Bta
                S.op("dve", lambda: nc.vector.scalar_tensor_tensor(out=fin, in0=rawsb[:, 0:ncols], scalar=gA,
                                                                   in1=rstd[:, 0:ncols], op0=ALU.mult, op1=ALU.mult),
                     reads=[Braw, Brstd, self.Bg], writes=[Bfin])
            S.ub()
            if callable(dst_bf):
                for hp in range(2):
                    S.op("pool", lambda hp=hp: nc.gpsimd.tensor_copy(out=dst_bf(hp), in_=fin[hp * 64:hp * 64 + 64, :]),
                         reads=[Bfin], writes=[Bdst(hp)])
            else:
                S.op("pool", lambda: nc.gpsimd.tensor_copy(out=dst_bf, in_=fin), reads=[Bfin], writes=[Bdst])
            S.ub()
            return fin, Bfin

        def evac(bank, ncols, dst, Bdst):
            S.op("dve", lambda: nc.vector.tensor_copy(out=dst, in_=self.banks[bank][:, 0:ncols]),
                 reads=[self.Bbank[bank]], writes=[Bdst])
            S.ub()

        self.make_hT(self.mem, 0, 2, TP)
        for mp in range(2):
            self.mm_group(ACC, 256, [(wmem[:, kc * 512 + mp * 128:kc * 512 + mp * 128 + 128],
                                      hT[:, kc * 512:kc * 512 + 256]) for kc in range(8)], [Bwm, self.BhT])
            norm_qk(256, 3, False, memKT[:, mp * 256:(mp + 1) * 256], Bmem)
        for sub in range(2):
            self.mm_group(ACC, 256, [(hT[:, kc * 512 + sub * 128:kc * 512 + sub * 128 + 128],
                                      wmem[:, kc * 512 + 256:kc * 512 + 512]) for kc in range(8)], [Bwm, self.BhT])
            evac(ACC, 256, memV[:, sub * 256:(sub + 1) * 256], Bmem)

        S.barrier()

        def proj_kv(t):
            self.make_hT(self.xseq, t * 512, 4, TP)
            S.dma(lambda: nc.sync.dma_start(out=cs[:, 0:512], in_=self.cosK[:, t * 512:(t + 1) * 512]), writes=[Bcs])
            S.dma(lambda: nc.sync.dma_start(out=cs[:, 512:1024], in_=self.sinK[:, t * 512:(t + 1) * 512]), writes=[Bcs])
            for pair in range(3):
                self.mm_group(ACC, 512, [(wmo[:, kc * WW + 384 + pair * 128:kc * WW + 384 + pair * 128 + 128],
                                          hT[:, kc * 512:(kc + 1) * 512]) for kc in range(8)], [Bw, self.BhT])
                fin, Bfin = norm_qk(512, 1, True, KT[:, pair * SEQ + t * 512:pair * SEQ + (t + 1) * 512], BKT[t])
                for hp in range(2):
                    h = 2 * pair + hp
                    S.op("dve", lambda h=h, hp=hp, fin=fin: nc.vector.tensor_reduce(
                        out=ksum[hp * 64:hp * 64 + 64, h * 32 + 2 * t:h * 32 + 2 * t + 2],
                        in_=fin[hp * 64:hp * 64 + 64, :].rearrange("p (b k) -> p b k", b=2), axis=AX.X, op=ALU.add),
                        reads=[Bfin], writes=[Bks])
                S.ub()
            for sub in range(4):
                self.mm_group(ACC, 384, [(hT[:, kc * 512 + sub * 128:kc * 512 + sub * 128 + 128],
                                          wmo[:, kc * WW + 768:kc * WW + 1152]) for kc in range(8)], [Bw, self.BhT])
                kb = t * 4 + sub
                evac(ACC, 384, V[:, kb * 384:(kb + 1) * 384], BV[t])

        def proj_q(j):
            self.make_hT(self.xown, j * 512, 4, TP)
            S.dma(lambda: nc.sync.dma_start(out=cs[:, 0:512], in_=self.cosQ[:, j * 512:(j + 1) * 512]), writes=[Bcs])
            S.dma(lambda: nc.sync.dma_start(out=cs[:, 512:1024], in_=self.sinQ[:, j * 512:(j + 1) * 512]), writes=[Bcs])
            S.gate(0)
            for mp in range(2):
                self.mm_group(ACC, 512, [(wmo[:, kc * WW + 1152 + mp * 128:kc * WW + 1152 + mp * 128 + 128],
                                          hT[:, kc * 512:(kc + 1) * 512]) for kc in range(8)], [Bw, self.BhT])
                norm_qk(512, 2, False,
                        lambda hp, mp=mp: QTmz[hp * 64:hp * 64 + 64, (2 * mp + hp) * 512:(2 * mp + hp + 1) * 512],
                        lambda hp, mp=mp: BQTmz[2 * mp + hp])
            for pair in range(3):
                S.gate((2 * pair + 2) * 8 * j)
                self.mm_group(ACC, 512, [(wmo[:, kc * WW + pair * 128:kc * WW + pair * 128 + 128],
                                          hT[:, kc * 512:(kc + 1) * 512]) for kc in range(8)], [Bw, self.BhT])
                norm_qk(512, 0, True,
                        lambda hp, pair=pair: QTz[hp * 64:hp * 64 + 64, (2 * pair + hp) * 512:(2 * pair + hp + 1) * 512],
                        lambda hp, pair=pair: BQTz[2 * pair + hp], dst_f=Qf[:, pair * 512:(pair + 1) * 512], Bdstf=BQf)

        def gate_select(j):
            GB = 4
            tpb = self.banks[TP][:].bitcast(BF16)
            S.dma(lambda: nc.sync.dma_start(out=pb[:], in_=self.pastb[:, j * 768:(j + 1) * 768]), writes=[Bpb])
            for sub in range(4):
                for h in range(6):
                    pair, hp = h // 2, h % 2
                    S.op("pe", lambda h=h, pair=pair, hp=hp: nc.tensor.matmul(
                        self.banks[GB][:, h * 32:(h + 1) * 32],
                        lhsT=Qf[:, pair * 512 + sub * 128:pair * 512 + sub * 128 + 128],
                        rhs=ksum[:, h * 32:(h + 1) * 32], start=True, stop=True),
                        reads=[BQf, Bks], writes=[self.Bbank[GB]], inc=(h == 5))
                S.op("dve", lambda: nc.vector.tensor_tensor(out=Gp[:], in0=self.banks[GB][:, 0:192],
                                                            in1=pb[:, sub * 192:(sub + 1) * 192], op=ALU.add),
                     reads=[self.Bbank[GB], Bpb], writes=[BGp])
                for h in range(6):
                    S.op("dve", lambda h=h: nc.vector.max(out=mx[:, h * 8:(h + 1) * 8], in_=Gp[:, h * 32:(h + 1) * 32]),
                         reads=[BGp], writes=[Bmx])
                S.op("dve", lambda: nc.vector.tensor_scalar(
                    out=mx[:, 48:54], in0=mx[:, 0:48].rearrange("p (h e) -> p h e", e=8)[:, :, 3],
                    scalar1=-1e29, scalar2=None, op0=ALU.max), reads=[Bmx], writes=[Bmx])
                for h in range(6):
                    S.op("dve", lambda h=h: nc.vector.tensor_scalar(
                        out=sel[:, h * 32:(h + 1) * 32], in0=Gp[:, h * 32:(h + 1) * 32],
                        scalar1=mx[:, 48 + h:49 + h], scalar2=None, op0=ALU.is_ge), reads=[BGp, Bmx], writes=[Bsel])
                S.op("dve", lambda: nc.vector.tensor_scalar(out=MBb[:], in0=sel[:], scalar1=-1.0, scalar2=-NEG,
                                                            op0=ALU.add, op1=ALU.mult), reads=[Bsel], writes=[BMBb])
                for g in range(2):
                    S.op("pe", lambda g=g: nc.tensor.transpose(
                        out=tpb[0:96, g * 512 + sub * 128:g * 512 + sub * 128 + 128],
                        in_=MBb[:, g * 96:(g + 1) * 96], identity=self.identb[:]),
                        reads=[BMBb, self.Bconst], writes=[self.Bbank[TP]], inc=(g == 1))
            S.op("dve", lambda: nc.vector.tensor_copy(out=MBT[0:96, :], in_=tpb[0:96, :]),
                 reads=[self.Bbank[TP]], writes=[BMBT])

        def attn(j, kind, feeder=None):
            if kind == "mo":
                nj = 8 * (j + 1)
                nh = 6
            else:
                nj = 2
                nh = 4
            its = [(h, kb) for h in range(nh) for kb in range(nj)]
            N = len(its)
            zb, Ob, Lbk = [0, 1], [2, 2], [3, 3]

            def emit_z(s):
                h, kb = its[s]
                pair, hp = h // 2, h % 2
                bank = zb[s % 2]
                out = self.banks[bank][:, :]
                if kind == "mo":
                    pairs = [(KT[:, pair * SEQ + kb * 128:pair * SEQ + kb * 128 + 128],
                              QTz[:, h * 512:(h + 1) * 512])]
                    r3 = 32 * (h % 3)
                    extra = [(out, self.esel[r3:r3 + 32, (kb // 2) * 128:(kb // 2) * 128 + 128],
                              MBT[r3:r3 + 32, (h // 3) * 512:(h // 3) * 512 + 512])]
                    if kb >= nj - 8:
                        wi = kb - (nj - 8)
                        extra.append((out, self.identb[:], mask[:, wi * 512:(wi + 1) * 512]))
                    self.mm_group(bank, 512, pairs, [BKT[kb // 4], BQTz[h], BMBT, Bmask, self.Bconst], extra=extra)
                else:
                    pairs = [(memKT[:, pair * 256 + kb * 128:pair * 256 + kb * 128 + 128],
                              QTmz[:, h * 512:(h + 1) * 512])]
                    self.mm_group(bank, 512, pairs, [Bmem, BQTmz[h]])

            def emit_p(s):
                bank = zb[s % 2]
                p = pbuf[s % NP]
                S.op("act", lambda: nc.scalar.activation(out=p[:], in_=self.banks[bank][:, :], func=AF.Exp, scale=0.125),
                     reads=[self.Bbank[bank]], writes=[Bp[s % NP]])

            def emit_AV(s):
                h, kb = its[s]
                pair, hp = h // 2, h % 2
                ob, lb = Ob[h % 2], Lbk[h % 2]
                p = pbuf[s % NP]
                if kind == "mo":
                    vap = V[:, kb * 384 + pair * 128:kb * 384 + pair * 128 + 128]
                    rd = [BV[kb // 4], Bp[s % NP]]
                    q = 3 + pair
                else:
                    vap = memV[:, kb * 256 + pair * 128:kb * 256 + pair * 128 + 128]
                    rd = [Bmem, Bp[s % NP]]
                    q = 6 + pair
                last = (kb == nj - 1)
                S.op("pe", lambda: nc.tensor.matmul(self.banks[ob][:, :], lhsT=vap, rhs=p[:], start=(kb == 0), stop=last),
                     reads=rd, writes=[self.Bbank[ob]], inc=False)
                S.op("pe", lambda: nc.tensor.matmul(self.banks[lb][:, :], lhsT=self.onesb[:], rhs=p[:], start=(kb == 0),
                                                    stop=last),
                     reads=[Bp[s % NP], self.Bconst], writes=[self.Bbank[lb]], inc=True)
                if last:
                    r0 = hp * 64
                    S.op("dve", lambda: nc.vector.reciprocal(out=rec[r0:r0 + 64, :], in_=self.banks[lb][r0:r0 + 64, :]),
                         reads=[self.Bbank[lb]], writes=[Brec])
                    S.op("dve", lambda: nc.vector.tensor_tensor(out=self.oTslot[r0:r0 + 64, q * 512:(q + 1) * 512],
                                                                in0=self.banks[ob][r0:r0 + 64, :], in1=rec[r0:r0 + 64, :],
                                                                op=ALU.mult),
                         reads=[self.Bbank[ob], Brec], writes=[self.BoTslot])

            emit_z(0)
            for s in range(N + 1):
                if s + 1 < N:
                    emit_z(s + 1)
                if s < N:
                    emit_p(s)
                if s >= 1:
                    emit_AV(s - 1)
                if feeder is not None:
                    feeder.step(s + 1 if kind == "mo" else -1)

        if STRESS:
            for rep in range(STRESS):
                proj_kv(rep % NT)
            return
        proj_kv(0)
        proj_kv(1)
        proj_q(0)
        for j in range(NS):
            S.dma(lambda j=j: nc.sync.dma_start(out=mask[:], in_=self.momask[j % 2]), writes=[Bmask])
            gate_select(j)
            feeder = None
            if j + 1 < NS:
                S.start_defer()
                proj_kv(2 * j + 2)
                proj_kv(2 * j + 3)
                proj_q(j + 1)
                feeder = Feeder(S, S.end_defer(), 48 * (j + 1) + 8)
            attn(j, "me", feeder)
            attn(j, "mo", feeder)
            if feeder is not None:
                feeder.drain()
            self.store_oT(j, 3, 5)

    def phase_f1(self):
        nc, S, A = self.nc, self.S, self.A
        NS = self.NS
        self.alloc_x(4)
        wg = A.alloc("wg", 8 * 3072, BF16)
        wup = A.alloc("wup", 8 * 1024, BF16)
        wout = A.alloc("wout", 8 * 1024, BF16)
        Bwg, Bwup, Bwout = Buf("wg"), Buf("wup"), Buf("wout")

        def loadw():
            self.load_w(self.w_in, 0, 8, 2560, 3072, wg, 3072, 0, Bwg, gcol_off=0)
            self.load_w(self.w_up_sb, 0, 3, 0, 1024, wup, 1024, 0, Bwup)
            self.load_w(self.w_up_moba, 0, 3, 0, 1024, wup, 1024, 3 * 1024, Bwup)
            self.load_w(self.w_up_mem, 0, 2, 0, 1024, wup, 1024, 6 * 1024, Bwup)
            self.load_w(self.w_out, 0, 8, 0, 1024, wout, 1024, 0, Bwout)
        self.with_stage(loadw)
        oT = A.alloc("oT", 8 * 512, BF16)
        BoT = Buf("oT")
        sg = [A.alloc("sg%d" % i, 512, F32) for i in range(3)]
        tt = [A.alloc("tt%d" % i, 512, F32) for i in range(3)]
        Bsg = [Buf("sg%d" % i) for i in range(3)]
        Btt = [Buf("tt%d" % i) for i in range(3)]
        mixb = A.alloc("mixb", 8 * 512, BF16)
        Bmix = [Buf("mix%d" % i) for i in range(8)]
        x1c = [A.alloc("x1c%d" % i, 512, F32) for i in range(2)]
        Bx1c = [Buf("x1c%d" % i) for i in range(2)]
        hT = self.hT
        TP = 7
        rr = [0]

        def nb():
            b = rr[0] % 7
            rr[0] += 1
            return b

        branches = [(0, [0, 1, 2]), (1, [3, 4, 5]), (2, [6, 7])]
        for j in range(NS):
            src = self.oTs[:, :, j * 512:(j + 1) * 512].rearrange("q p t -> p q t")
            S.dma(lambda src=src: nc.sync.dma_start(out=oT[:].rearrange("p (q t) -> p q t", q=8), in_=src),
                  reads=[self.BoTs[j]], writes=[BoT])
            self.make_hT(self.xown, j * 512, 4, TP, keep_x=True)
            for c in range(8):
                for (b, qs) in branches:
                    gbk = nb()
                    self.mm_group(gbk, 512, [(wg[:, kc * 3072 + b * 1024 + c * 128:kc * 3072 + b * 1024 + c * 128 + 128],
                                              hT[:, kc * 512:(kc + 1) * 512]) for kc in range(8)], [Bwg, self.BhT])
                    ubk = nb()
                    self.mm_group(ubk, 512, [(wup[:, q * 1024 + c * 128:q * 1024 + c * 128 + 128],
                                              oT[:, q * 512:(q + 1) * 512]) for q in qs], [Bwup, BoT])
                    S.op("act", lambda b=b, gbk=gbk: nc.scalar.activation(out=sg[b][:], in_=self.banks[gbk][:, :],
                                                                          func=AF.Sigmoid),
                         reads=[self.Bbank[gbk]], writes=[Bsg[b]])
                    S.op("dve", lambda b=b, ubk=ubk: nc.vector.tensor_tensor(out=tt[b][:], in0=sg[b][:],
                                                                             in1=self.banks[ubk][:, :], op=ALU.mult),
                         reads=[Bsg[b], self.Bbank[ubk]], writes=[Btt[b]])
                S.op("pool", lambda: nc.gpsimd.tensor_tensor(out=tt[0][:], in0=tt[0][:], in1=tt[1][:], op=ALU.add),
                     reads=[Btt[0], Btt[1]], writes=[Btt[0]])
                S.op("pool", lambda c=c: nc.gpsimd.tensor_tensor(out=mixb[:, c * 512:(c + 1) * 512], in0=tt[0][:],
                                                                 in1=tt[2][:], op=ALU.add),
                     reads=[Btt[0], Btt[2]], writes=[Bmix[c]])
            for c in range(8):
                bk = nb()
                extra = [(self.banks[bk][:, sub * 128:(sub + 1) * 128],
                          self.xt[:, sub * D + c * 128:sub * D + c * 128 + 128], self.ident[:]) for sub in range(4)]
                self.mm_group(bk, 512, [(wout[:, k * 1024 + c * 128:k * 1024 + c * 128 + 128],
                                         mixb[:, k * 512:(k + 1) * 512]) for k in range(8)],
                              [Bwout, self.Bconst] + Bmix + self.Bxt, extra=extra)
                xc = x1c[c % 2]
                S.op("act", lambda bk=bk, xc=xc: nc.scalar.copy(out=xc[:], in_=self.banks[bk][:, :]),
                     reads=[self.Bbank[bk]], writes=[Bx1c[c % 2]])
                S.dma(lambda xc=xc, c=c, j=j: nc.sync.dma_start(out=self.x1s[j * 8 + c], in_=xc[:]),
                      reads=[Bx1c[c % 2]], writes=[self.Bx1s[j]])

    def phase_f2(self):
        nc, S, A = self.nc, self.S, self.A
        NS = self.NS
        NF = DFF // 128
        wfi = A.alloc("wfi", 8 * 2 * DFF, BF16)
        wfd = A.alloc("wfd", NF * 1024, BF16)
        Bwfi, Bwfd = Buf("wfi"), Buf("wfd")

        def loadw():
            self.load_w(self.w_ffn_in, 0, 8, 0, 2 * DFF, wfi, 2 * DFF, 0, Bwfi, gcol_off=16)
            self.load_w(self.w_ffn_down, 0, NF, 0, 1024, wfd, 1024, 0, Bwfd)
        self.with_stage(loadw)
        x1T = A.alloc("x1T", 8 * 512, F32)
        Bx1 = [Buf("x1T%d" % c) for c in range(8)]
        sq = [A.alloc("sq%d" % i, 512, F32) for i in range(2)]
        Bsq = [Buf("sq%d" % i) for i in range(2)]
        rstd = A.alloc("rstd", 512, F32)
        Brstd = Buf("rstd")
        h2T = A.alloc("h2T", 8 * 512, BF16)
        Bh2 = Buf("h2T")
        ffT = A.alloc("ffT", NF * 512, BF16)
        Bff = [Buf("ff%d" % f) for f in range(NF)]
        sl = [A.alloc("sl%d" % i, 512, F32) for i in range(2)]
        Bsl = [Buf("sl%d" % i) for i in range(2)]
        ot = A.alloc("ot", 1024, F32)
        Bot = Buf("ot")
        SSB = 7
        rr = [0]

        def nb():
            b = rr[0] % 7
            rr[0] += 1
            return b

        for j in range(NS):
            for c in range(8):
                S.dma(lambda c=c, j=j: nc.sync.dma_start(out=x1T[:, c * 512:(c + 1) * 512], in_=self.x1s[j * 8 + c]),
                      reads=[self.Bx1s[j]], writes=[Bx1[c]])
            for c in range(8):
                S.op("act", lambda c=c: nc.scalar.activation(out=sq[c % 2][:], in_=x1T[:, c * 512:(c + 1) * 512],
                                                             func=AF.Square), reads=[Bx1[c]], writes=[Bsq[c % 2]])
                S.op("pe", lambda c=c: nc.tensor.matmul(self.banks[SSB][:, :], lhsT=self.onesf[:], rhs=sq[c % 2][:],
                                                        start=(c == 0), stop=(c == 7)),
                     reads=[Bsq[c % 2], self.Bconst], writes=[self.Bbank[SSB]], inc=True)
            S.op("act", lambda: nc.scalar.activation(out=rstd[:], in_=self.banks[SSB][:, :], func=AF.Ln,
                                                     bias=self.cst[:, 0:1], scale=1.0 / D),
                 reads=[self.Bbank[SSB], self.Bconst], writes=[Brstd])
            S.op("act", lambda: nc.scalar.activation(out=rstd[:], in_=rstd[:], func=AF.Exp, scale=-0.5),
                 reads=[Brstd], writes=[Brstd])
            for c in range(8):
                S.op("dve", lambda c=c: nc.vector.tensor_tensor(out=h2T[:, c * 512:(c + 1) * 512],
                                                                in0=x1T[:, c * 512:(c + 1) * 512], in1=rstd[:],
                                                                op=ALU.mult), reads=[Bx1[c], Brstd], writes=[Bh2])
            for f in range(NF):
                gbk = nb()
                self.mm_group(gbk, 512, [(wfi[:, kc * 2 * DFF + f * 128:kc * 2 * DFF + f * 128 + 128],
                                          h2T[:, kc * 512:(kc + 1) * 512]) for kc in range(8)], [Bwfi, Bh2])
                ubk = nb()
                self.mm_group(ubk, 512, [(wfi[:, kc * 2 * DFF + DFF + f * 128:kc * 2 * DFF + DFF + f * 128 + 128],
                                          h2T[:, kc * 512:(kc + 1) * 512]) for kc in range(8)], [Bwfi, Bh2])
                S.op("act", lambda f=f, gbk=gbk: nc.scalar.activation(out=sl[f % 2][:], in_=self.banks[gbk][:, :],
                                                                      func=AF.Silu),
                     reads=[self.Bbank[gbk]], writes=[Bsl[f % 2]])
                S.op("dve", lambda f=f, ubk=ubk: nc.vector.tensor_tensor(out=ffT[:, f * 512:(f + 1) * 512], in0=sl[f % 2][:],
                                                                         in1=self.banks[ubk][:, :], op=ALU.mult),
                     reads=[Bsl[f % 2], self.Bbank[ubk]], writes=[Bff[f]])
            for sub in range(4):
                for half in range(2):
                    bk = nb()
                    extra = [(self.banks[bk][:, (c % 4) * 128:(c % 4) * 128 + 128],
                              x1T[:, c * 512 + sub * 128:c * 512 + sub * 128 + 128], self.ident[:])
                             for c in range(half * 4, half * 4 + 4)]
                    self.mm_group(bk, 512, [(ffT[:, f * 512 + sub * 128:f * 512 + sub * 128 + 128],
                                             wfd[:, f * 1024 + half * 512:f * 1024 + half * 512 + 512]) for f in range(NF)],
                                  [Bwfd, self.Bconst] + Bff + Bx1, extra=extra)
                    if half == 0:
                        S.op("act", lambda bk=bk: nc.scalar.copy(out=ot[:, 0:512], in_=self.banks[bk][:, :]),
                             reads=[self.Bbank[bk]], writes=[Bot])
                    else:
                        S.op("dve", lambda bk=bk: nc.vector.tensor_copy(out=ot[:, 512:1024], in_=self.banks[bk][:, :]),
                             reads=[self.Bbank[bk]], writes=[Bot])
                r0 = j * 512 + sub * 128
                S.dma(lambda r0=r0: nc.sync.dma_start(out=self.out[r0:r0 + 128, :], in_=ot[:]), reads=[Bot])


def own_tiles(role, NS):
    tiles = []
    for j in range(NS):
        early = (j % 2 == 0) if role == 0 else (j % 2 == 1)
        tiles.append(2 * j if early else 2 * j + 1)
    return tiles


def host_constants(role, NT):
    NS = NT // 2
    SEQ = NT * 512
    OWN = NS * 512
    tiles = own_tiles(role, NS)
    half = HD // 2
    inv_freq = (np.float32(10000.0) ** (-(np.arange(half, dtype=np.float32) * np.float32(2.0) / np.float32(HD)))).astype(np.float32)
    pos = np.arange(SEQ, dtype=np.float32)
    ang = (pos[:, None] * inv_freq[None, :]).astype(np.float32)
    cos = np.cos(ang).astype(np.float32)
    sin = np.sin(ang).astype(np.float32)
    fidx = np.arange(128) % 32
    cosK = np.ascontiguousarray(cos[:, fidx].T)
    sinK = np.ascontiguousarray(sin[:, fidx].T)
    own_pos = np.concatenate([np.arange(t * 512, (t + 1) * 512) for t in tiles])
    cosQ = np.ascontiguousarray(cosK[:, own_pos])
    sinQ = np.ascontiguousarray(sinK[:, own_pos])
    qblk = own_pos // 256
    n = np.arange(32)
    pb = np.where(n[None, :] < qblk[:, None], 0.0,
                  np.where(n[None, :] == qblk[:, None], 1e30, -1e30)).astype(np.float32)
    def lay(a):
        a = np.tile(a[:, None, :], (1, 6, 1)).reshape(OWN // 128, 128, 192)
        return np.ascontiguousarray(a.transpose(1, 0, 2).reshape(128, -1))
    pastb = lay(pb)
    sbmask = np.zeros((2, 128, 8, 512), np.float32)
    momask = np.zeros((2, 128, 8, 512), np.float32)
    s_idx = np.arange(128)[:, None]
    q_idx = np.arange(512)[None, :]
    for par in range(2):
        early = (par == 0) if role == 0 else (par == 1)
        own_off = 0 if early else 512
        for wi in range(8):
            kpos = wi * 128 + s_idx
            qpos = own_off + q_idx
            sbmask[par, :, wi, :] = np.where(kpos < qpos, 0.0, NEG)
            same_blk = (kpos // 256) == (qpos // 256)
            momask[par, :, wi, :] = np.where(same_blk & (kpos > qpos), NEG, 0.0)
    bf = ml_dtypes.bfloat16
    return dict(cosK=cosK, sinK=sinK, cosQ=cosQ, sinQ=sinQ, pastb=pastb,
                sbmask=sbmask.reshape(2, 128, 4096).astype(bf), momask=momask.reshape(2, 128, 4096).astype(bf))


_CACHE = {}


STOP_AFTER = 4
NSTAGE = 4
MO_STOP = 5
VAR = set()
STRESS = 0


def _program(NT, debug=False):
    key = (NT, debug, STOP_AFTER, MO_STOP, tuple(sorted(VAR)), STRESS)
    if key not in _CACHE:
        b = Builder(NT, debug, STOP_AFTER)
        nc = b.build()
        _CACHE[key] = nc
    return _CACHE[key]


WNAMES = ["w_in", "w_mem_kv", "w_up_sb", "w_up_moba", "w_up_mem", "w_out", "w_ffn_in", "w_ffn_down",
          "mix_norm_g", "mem_norm_g", "ffn_norm_g", "moba_q_norm_g", "moba_k_norm_g", "mem_q_norm_g", "mem_k_norm_g"]


def run(inputs, debug=False, trace=False):
    x = np.asarray(inputs["x"], dtype=np.float32)
    mem = np.asarray(inputs["mem"], dtype=np.float32)
    B, SEQ, _ = x.shape
    NT = SEQ // 512
    NS = NT // 2
    assert B * 2 == N_CORES
    nc = _program(NT, debug)
    wmaps = {k: np.ascontiguousarray(np.asarray(inputs[k], dtype=np.float32)[0]) for k in WNAMES}
    consts = [host_constants(r, NT) for r in range(2)]
    in_maps = []
    for core in range(N_CORES):
        b, role = core // 2, core % 2
        tiles = own_tiles(role, NS)
        xown = np.concatenate([x[b, t * 512:(t + 1) * 512] for t in tiles], axis=0)
        m = dict(wmaps)
        m.update(consts[role])
        m["xseq"] = np.ascontiguousarray(x[b])
        m["xown"] = np.ascontiguousarray(xown)
        m["mem"] = np.ascontiguousarray(mem[b])
        in_maps.append(m)
    res = run_bass_kernel_spmd(nc, in_maps, core_ids=list(range(N_CORES)), **({"trace": True} if trace else {}))
    out = np.empty((B, SEQ, D), np.float32)
    for core in range(N_CORES):
        b, role = core // 2, core % 2
        tiles = own_tiles(role, NS)
        o = np.asarray(res.results[core]["out"])
        for j, t in enumerate(tiles):
            out[b, t * 512:(t + 1) * 512] = o[j * 512:(j + 1) * 512]
    return out, res


def kernel(**inputs):
    out, _ = run(inputs)
    return out
```

```python
import math
from contextlib import ExitStack

import numpy as np
import ml_dtypes

import concourse.bass as bass
import concourse.mybir as mybir
from concourse.bass_utils import run_bass_kernel_spmd

F32 = mybir.dt.float32
BF16 = mybir.dt.bfloat16
AF = mybir.ActivationFunctionType
ALU = mybir.AluOpType
AX = mybir.AxisListType

D = 1024
HD = 64
DFF = 2816
MEMLEN = 256
INW = 5632
NEG = -30000.0
EPS = 1e-6
N_CORES = 8


class Buf:
    __slots__ = ("name", "w", "r", "psum")

    def __init__(self, name, psum=False):
        self.name = name
        self.w = None
        self.r = []
        self.psum = psum


class Ev:
    __slots__ = ("eng", "seq", "sem", "val", "clock")

    def __init__(self, eng, seq):
        self.eng = eng
        self.seq = seq
        self.sem = None
        self.val = None
        self.clock = None


class Sched:
    def __init__(self, nc, engsems, dmasems):
        self.nc = nc
        self.eng = {"pe": nc.tensor, "act": nc.scalar, "dve": nc.vector,
                    "pool": nc.gpsimd, "sp": nc.sync}
        self.sem = engsems
        self.clock = {k: {} for k in self.eng}
        self.count = {k: 0 for k in self.eng}
        self.seq = {k: 0 for k in self.eng}
        self.pending = {k: [] for k in self.eng}
        self.last = {k: None for k in self.eng}
        self.dma_sems = list(dmasems)
        self.dma_cnt = [0] * len(dmasems)
        self.dma_last = [None] * len(dmasems)
        self.dma_rr = 0
        self.nwaits = 0
        self.nops = 0
        self.defer = None

    def _deps(self, reads, writes):
        deps = []
        for b in reads:
            if b.w is not None:
                deps.append(b.w)
            if b.psum:
                deps.extend(b.r)
        for b in writes:
            if b.w is not None:
                deps.append(b.w)
            deps.extend(b.r)
        return deps

    def _waits(self, e, deps):
        ck = self.clock[e]
        need = {}
        for d in deps:
            if d.eng == "pe" and e == "pe":
                continue
            if ck.get(d.eng, 0) >= d.seq:
                continue
            assert d.val is not None, "dependency on unresolved milestone (%s)" % d.eng
            cur = need.get(d.eng)
            if cur is None or cur.seq < d.seq:
                need[d.eng] = d
        waits = []
        for d in need.values():
            if ck.get(d.eng, 0) >= d.seq:
                continue
            waits.append((d.sem, d.val))
            for k, v in d.clock.items():
                if ck.get(k, 0) < v:
                    ck[k] = v
            if ck.get(d.eng, 0) < d.seq:
                ck[d.eng] = d.seq
        return waits

    def _apply(self, e, waits, make):
        eng = self.eng[e]
        self.nwaits += len(waits)
        for (s, v) in waits[1:]:
            eng.wait_ge(s, v)
        ins = make()
        if waits:
            ins._wait_ge(waits[0][0], waits[0][1])
        return ins

    def _record(self, ev, reads, writes):
        for b in reads:
            b.r.append(ev)
        for b in writes:
            b.w = ev
            b.r = []

    def start_defer(self):
        self.defer = []

    def end_defer(self):
        d, self.defer = self.defer, None
        return d

    def gate(self, pos):
        if self.defer is not None:
            self.ub()
            self.defer.append(("gate", pos))

    def ub(self):
        if self.defer is not None and self.defer and self.defer[-1] is not None:
            self.defer.append(None)

    def emit_deferred(self, q, nunits):
        while nunits > 0 and q:
            it = q.popleft()
            if it is None:
                nunits -= 1
                continue
            kind = it[0]
            if kind == "op":
                self.op(*it[1:])
            elif kind == "dma":
                self.dma(*it[1:])
            elif nunits < (1 << 29):
                q.appendleft(it)
                return

    def op(self, e, make, reads=(), writes=(), inc=True):
        if self.defer is not None:
            self.defer.append(("op", e, make, tuple(reads), tuple(writes), inc))
            return None
        self.nops += 1
        waits = self._waits(e, self._deps(reads, writes))
        ins = self._apply(e, waits, make)
        self.seq[e] += 1
        ev = Ev(e, self.seq[e])
        ev.sem = self.sem[e]
        ev.clock = dict(self.clock[e])
        if inc:
            self.count[e] += 1
            ins.then_inc(self.sem[e], 1)
            ev.val = self.count[e]
            for p in self.pending[e]:
                p.val = ev.val
            self.pending[e] = []
        else:
            self.pending[e].append(ev)
        self.last[e] = ev
        self._record(ev, reads, writes)
        return ev

    def dma(self, make, reads=(), writes=(), q="sp"):
        if self.defer is not None:
            self.defer.append(("dma", make, tuple(reads), tuple(writes), q))
            return None
        self.nops += 1
        j = self.dma_rr
        self.dma_rr = (self.dma_rr + 1) % len(self.dma_sems)
        deps = self._deps(reads, writes)
        if self.dma_last[j] is not None:
            deps.append(self.dma_last[j])
        waits = self._waits(q, deps)
        ins = self._apply(q, waits, make)
        self.dma_cnt[j] += 1
        ins.then_inc(self.dma_sems[j], 16)
        ev = Ev("dma%d" % j, self.dma_cnt[j])
        ev.sem = self.dma_sems[j]
        ev.val = 16 * self.dma_cnt[j]
        ev.clock = dict(self.clock[q])
        self.dma_last[j] = ev
        self._record(ev, reads, writes)
        return ev

    def barrier(self):
        evs = []
        for k in self.eng:
            if self.pending[k]:
                self.op(k, lambda k=k: self.eng[k].drain(), inc=True)
            if self.last[k] is not None and self.last[k].val is not None:
                evs.append(self.last[k])
        for d in self.dma_last:
            if d is not None:
                evs.append(d)
        for e in self.eng:
            waits = self._waits(e, evs)
            self.nwaits += len(waits)
            for (s, v) in waits:
                self.eng[e].wait_ge(s, v)


class Feeder:
    def __init__(self, S, entries, steps):
        from collections import deque
        self.S = S
        self.q = deque(entries)
        self.units = sum(1 for x in entries if x is None) + 1
        self.steps = max(steps, 1)

    def step(self, pos=-1):
        if not self.q:
            return
        n = -(-self.units // self.steps)
        self.steps = max(self.steps - 1, 1)
        while n > 0 and self.q:
            head = self.q[0]
            if head is not None and head[0] == "gate":
                if pos >= head[1]:
                    self.q.popleft()
                    continue
                return
            self.S.emit_deferred(self.q, 1)
            self.units = max(self.units - 1, 0)
            n -= 1

    def drain(self):
        self.S.emit_deferred(self.q, 1 << 30)


class SbAlloc:
    def __init__(self, nc, limit):
        self.nc = nc
        self.off = 16512
        self.limit = limit
        self.n = 0
        self.peak = 0

    def alloc(self, name, cols, dt):
        isz = 4 if dt == F32 else 2
        nbytes = (cols * isz + 63) // 64 * 64
        off = self.off
        assert off + nbytes <= self.limit, "SBUF overflow at %s: %d + %d > %d" % (name, off, nbytes, self.limit)
        self.off += nbytes
        self.peak = max(self.peak, self.off)
        self.n += 1
        return self.nc.alloc_sbuf_tensor_at("%s_%d" % (name, self.n), [128, cols], dt, offset=off)

    def mark(self):
        return self.off

    def release(self, m):
        self.off = m


class Builder:
    def __init__(self, NT, debug=False, stop_after=4):
        self.stop_after = stop_after
        assert NT % 4 == 0
        self.NT = NT
        self.NS = NT // 2
        self.SEQ = NT * 512
        self.NKB = self.SEQ // 128
        self.OWN = self.NS * 512
        self.debug = debug
        self.nc = bass.Bass("TRN2", target_bir_lowering=False)
        self.bank_rr = 0

    def din(self, name, shape, dt=F32):
        return self.nc.dram_tensor(name, list(shape), dt, kind="ExternalInput").ap()

    def dout(self, name, shape, dt=F32):
        return self.nc.dram_tensor(name, list(shape), dt, kind="ExternalOutput").ap()

    def build(self):
        nc = self.nc
        SEQ, OWN, NS = self.SEQ, self.OWN, self.NS
        self.xseq = self.din("xseq", [SEQ, D])
        self.xown = self.din("xown", [OWN, D])
        self.mem = self.din("mem", [MEMLEN, D])
        self.w_in = self.din("w_in", [D, INW])
        self.w_mem_kv = self.din("w_mem_kv", [D, 512])
        self.w_up_sb = self.din("w_up_sb", [384, D])
        self.w_up_moba = self.din("w_up_moba", [384, D])
        self.w_up_mem = self.din("w_up_mem", [256, D])
        self.w_out = self.din("w_out", [D, D])
        self.w_ffn_in = self.din("w_ffn_in", [D, 2 * DFF])
        self.w_ffn_down = self.din("w_ffn_down", [DFF, D])
        self.g_mix = self.din("mix_norm_g", [D])
        self.g_memn = self.din("mem_norm_g", [D])
        self.g_ffn = self.din("ffn_norm_g", [D])
        self.g_moq = self.din("moba_q_norm_g", [HD])
        self.g_mok = self.din("moba_k_norm_g", [HD])
        self.g_meq = self.din("mem_q_norm_g", [HD])
        self.g_mek = self.din("mem_k_norm_g", [HD])
        self.cosK = self.din("cosK", [128, SEQ])
        self.sinK = self.din("sinK", [128, SEQ])
        self.cosQ = self.din("cosQ", [128, OWN])
        self.sinQ = self.din("sinQ", [128, OWN])
        self.pastb = self.din("pastb", [128, NS * 4 * 192])
        self.sbmask = self.din("sbmask", [2, 128, 8 * 512], BF16)
        self.momask = self.din("momask", [2, 128, 8 * 512], BF16)
        self.out = self.dout("out", [OWN, D])
        if self.debug:
            self.oTs = self.dout("oTs", [8, 128, OWN], BF16)
            self.x1s = self.dout("x1s", [NS * 8, 128, 512], F32)
        else:
            self.oTs = nc.dram_tensor("oTs", [8, 128, OWN], BF16).ap()
            self.x1s = nc.dram_tensor("x1s", [NS * 8, 128, 512], F32).ap()
        self.BoTs = [Buf("oTs%d" % j) for j in range(NS)]
        self.Bx1s = [Buf("x1s%d" % j) for j in range(NS)]

        with ExitStack() as es:
            engsems = {k: es.enter_context(nc.semaphore("s_" + k)) for k in ["pe", "act", "dve", "pool", "sp"]}
            dmasems = [es.enter_context(nc.semaphore("d%d" % i)) for i in range(24)]
            self.S = Sched(nc, engsems, dmasems)
            self.A = SbAlloc(nc, 229376)
            self.banks = [nc.alloc_psum_tensor("bank%d" % i, [128, 512], F32) for i in range(8)]
            self.Bbank = [Buf("bank%d" % i, psum=True) for i in range(8)]
            self.consts()
            m0 = self.A.mark()
            for i, ph in enumerate((self.phase_sb, self.phase_mo, self.phase_f1)):
                if self.stop_after < i + 1:
                    break
                if STRESS and i == 0:
                    continue
                ph()
                self.S.barrier()
                self.A.release(m0)
            if self.stop_after >= 4:
                self.A.release(self.m_consts)
                self.phase_f2()
            self.S.barrier()
        return nc

    def consts(self):
        nc, S, A = self.nc, self.S, self.A
        self.Bconst = Bc = Buf("consts")

        def diag_select(ap, ncols, base, op=ALU.is_equal):
            S.op("pool", lambda: nc.gpsimd.affine_select(out=ap, in_=ap, pattern=[[-1, ncols]], compare_op=op,
                                                         fill=0.0, base=base, channel_multiplier=1),
                 reads=[Bc], writes=[Bc])

        self.ident = A.alloc("ident", 128, F32)
        S.op("pool", lambda: nc.gpsimd.memset(self.ident[:], 1.0), writes=[Bc])
        diag_select(self.ident[:], 128, 0)
        self.identb = A.alloc("identb", 128, BF16)
        S.op("pool", lambda: nc.gpsimd.memset(self.identb[:], 1.0), writes=[Bc])
        diag_select(self.identb[:], 128, 0)
        self.trineg = A.alloc("trineg", 128, BF16)
        S.op("pool", lambda: nc.gpsimd.memset(self.trineg[:], -1.0), writes=[Bc])
        diag_select(self.trineg[:], 128, 0, ALU.is_ge)
        self.negones = A.alloc("negones", 128, BF16)
        S.op("pool", lambda: nc.gpsimd.memset(self.negones[:], -1.0), writes=[Bc])
        self.onesb = A.alloc("onesb", 128, BF16)
        S.op("pool", lambda: nc.gpsimd.memset(self.onesb[:], 1.0), writes=[Bc])
        self.onesf = A.alloc("onesf", 128, F32)
        S.op("pool", lambda: nc.gpsimd.memset(self.onesf[:], 1.0), writes=[Bc])
        self.blk64 = A.alloc("blk64", 128, F32)
        S.op("pool", lambda: nc.gpsimd.memset(self.blk64[:], 0.0), writes=[Bc])
        S.op("pool", lambda: nc.gpsimd.memset(self.blk64[0:64, 0:64], 1.0 / 64), reads=[Bc], writes=[Bc])
        S.op("pool", lambda: nc.gpsimd.memset(self.blk64[64:128, 64:128], 1.0 / 64), reads=[Bc], writes=[Bc])
        self.rot = A.alloc("rot", 128, F32)
        for (c0, val, base) in ((0, -1.0, -32), (32, 1.0, 0), (64, -1.0, -96), (96, 1.0, -64)):
            S.op("pool", lambda c0=c0, val=val: nc.gpsimd.memset(self.rot[:, c0:c0 + 32], val), reads=[Bc], writes=[Bc])
            diag_select(self.rot[:, c0:c0 + 32], 32, base)
        self.cst = A.alloc("cst", 8, F32)
        S.op("pool", lambda: nc.gpsimd.memset(self.cst[:, 0:1], EPS), writes=[Bc])
        S.op("pool", lambda: nc.gpsimd.memset(self.cst[:, 1:2], 1.0), reads=[Bc], writes=[Bc])
        self.esel = A.alloc("esel", 32 * 128, BF16)
        mtmp = A.mark()
        etmp = A.alloc("etmp", 32 * 128, BF16)
        for g in range(3):
            dstt = self.esel if g == 0 else etmp
            v = dstt[:].rearrange("p (n m) -> p n m", n=32)
            S.op("pool", lambda dstt=dstt: nc.gpsimd.memset(dstt[:], 1.0), reads=[Bc], writes=[Bc])
            S.op("pool", lambda v=v, g=g: nc.gpsimd.affine_select(
                out=v, in_=v, pattern=[[-1, 32], [0, 128]], compare_op=ALU.is_equal, fill=0.0, base=-32 * g,
                channel_multiplier=1), reads=[Bc], writes=[Bc])
            if g > 0:
                S.op("pool", lambda: nc.gpsimd.tensor_tensor(out=self.esel[:], in0=self.esel[:], in1=etmp[:], op=ALU.add),
                     reads=[Bc], writes=[Bc])
        S.barrier()
        A.release(mtmp)
        self.gcol = A.alloc("gcol", 24, F32)
        self.Bg = Buf("gains")
        for i, g in enumerate([self.g_mix, self.g_memn, self.g_ffn]):
            S.dma(lambda i=i, g=g: nc.sync.dma_start(out=self.gcol[:, 8 * i:8 * i + 8],
                                                     in_=g.rearrange("(c p) -> p c", p=128),
                                                     allow_slow_non_contiguous=True), writes=[self.Bg])
        self.gh = A.alloc("gh", 8, F32)
        for i, g in enumerate([self.g_moq, self.g_mok, self.g_meq, self.g_mek]):
            g2 = g.rearrange("(p o) -> p o", o=1)
            for half in range(2):
                S.dma(lambda i=i, g2=g2, half=half: nc.sync.dma_start(
                    out=self.gh[64 * half:64 * half + 64, 2 * i:2 * i + 1], in_=g2[0:64, :]), writes=[self.Bg])
                for q in range(2):
                    S.dma(lambda i=i, g2=g2, half=half, q=q: nc.sync.dma_start(
                        out=self.gh[64 * half + 32 * q:64 * half + 32 * q + 32, 2 * i + 1:2 * i + 2],
                        in_=g2[32 * (1 - q):32 * (1 - q) + 32, :]), writes=[self.Bg])
        self.m_consts = A.mark()
        self.xs = [A.alloc("xs%d" % i, D, BF16) for i in range(2)]
        self.Bxs = [Buf("xs%d" % i) for i in range(2)]
        self.stat = A.alloc("stat", 16, F32)
        self.Bstat = [Buf("stat%d" % i) for i in range(2)]
        self.hT = A.alloc("hT", 8 * 512, BF16)
        self.BhT = Buf("hT")
        self.oTslot = A.alloc("oTslot", 8 * 512, BF16)
        self.BoTslot = Buf("oTslot")
        self.sub_rr = 0

    def alloc_x(self, n):
        self.xt = self.A.alloc("xt", n * D, F32)
        self.Bxt = [Buf("xt%d" % i) for i in range(n)]

    def with_stage(self, fn, keep=False):
        m = self.A.mark()
        self.stage = [self.A.alloc("stage%d" % i, 2048, F32) for i in range(NSTAGE)]
        self.Bstage = [Buf("stage%d" % i) for i in range(NSTAGE)]
        self.stage_rr = 0
        fn()
        self.S.barrier()
        if not keep:
            self.A.release(m)
        self._stage_mark = m

    def load_w(self, wd, k0, nk, c0, ncols, dst, dstw, dst_c0, Bdst, gcol_off=None):
        nc, S = self.nc, self.S
        for kc in range(nk):
            for cc in range(0, ncols, 2048):
                n = min(2048, ncols - cc)
                si = self.stage_rr
                self.stage_rr = (self.stage_rr + 1) % NSTAGE
                st = self.stage[si]
                Bst = self.Bstage[si]
                S.dma(lambda kc=kc, cc=cc, n=n, st=st: nc.sync.dma_start(
                    out=st[:, 0:n], in_=wd[k0 + kc * 128:k0 + kc * 128 + 128, c0 + cc:c0 + cc + n]), writes=[Bst])
                o0 = kc * dstw + dst_c0 + cc
                if gcol_off is None:
                    S.op("dve", lambda st=st, n=n, o0=o0: nc.vector.tensor_copy(out=dst[:, o0:o0 + n], in_=st[:, 0:n]),
                         reads=[Bst], writes=[Bdst])
                else:
                    S.op("dve", lambda st=st, n=n, o0=o0, kc=kc: nc.vector.tensor_scalar(
                        out=dst[:, o0:o0 + n], in0=st[:, 0:n],
                        scalar1=self.gcol[:, gcol_off + kc:gcol_off + kc + 1], scalar2=None, op0=ALU.mult),
                        reads=[Bst, self.Bg], writes=[Bdst])

    def make_hT(self, xd, row0, nsub, tpbank, keep_x=False):
        nc, S = self.nc, self.S
        Btp = self.Bbank[tpbank]
        tpb = self.banks[tpbank][:].bitcast(BF16)
        hTv = self.hT[:].rearrange("p (c t) -> p c t", c=8)
        for j in range(nsub):
            xi = j if keep_x else (self.sub_rr % 2)
            si = self.sub_rr % 2
            self.sub_rr += 1
            xt = self.xt[:, xi * D:(xi + 1) * D]
            Bx = self.Bxt[xi]
            S.dma(lambda xt=xt, j=j: nc.sync.dma_start(out=xt, in_=xd[row0 + j * 128:row0 + j * 128 + 128, :]), writes=[Bx])
            ss = self.stat[:, 4 * si:4 * si + 1]
            lnv = self.stat[:, 4 * si + 1:4 * si + 2]
            rstd = self.stat[:, 4 * si + 2:4 * si + 3]
            Bs = self.Bstat[si]
            S.op("act", lambda xt=xt, ss=ss, si=si: nc.scalar.activation(out=self.xs[si][:], in_=xt, func=AF.Square,
                                                                        accum_out=ss),
                 reads=[Bx], writes=[self.Bxs[si], Bs])
            S.op("act", lambda ss=ss, lnv=lnv: nc.scalar.activation(out=lnv, in_=ss, func=AF.Ln, bias=self.cst[:, 0:1],
                                                                   scale=1.0 / D), reads=[Bs, self.Bconst], writes=[Bs])
            S.op("act", lambda lnv=lnv, rstd=rstd: nc.scalar.activation(out=rstd, in_=lnv, func=AF.Exp, scale=-0.5),
                 reads=[Bs], writes=[Bs])
            xs = self.xs[si]
            Bxs = self.Bxs[si]
            S.op("dve", lambda xs=xs, xt=xt, rstd=rstd: nc.vector.tensor_scalar(
                out=xs[:], in0=xt, scalar1=rstd, scalar2=None, op0=ALU.mult), reads=[Bx, Bs], writes=[Bxs])
            S.ub()
            for c in range(8):
                S.op("pe", lambda c=c, xs=xs: nc.tensor.transpose(
                    out=tpb[:, c * 128:(c + 1) * 128], in_=xs[:, c * 128:(c + 1) * 128], identity=self.identb[:]),
                    reads=[Bxs, self.Bconst], writes=[Btp], inc=(c == 7))
            S.ub()
            S.op("dve", lambda j=j: nc.vector.tensor_copy(
                out=hTv[:, :, j * 128:(j + 1) * 128], in_=tpb.rearrange("p (c t) -> p c t", c=8)),
                reads=[Btp], writes=[self.BhT])
            S.ub()

    def mm_group(self, bank, ncols, pairs, reads, extra=None, inc=True):
        nc, S = self.nc, self.S
        n = len(pairs) + (len(extra) if extra else 0)
        out = self.banks[bank][:, 0:ncols]
        k = 0
        for (l, r) in pairs:
            k += 1
            S.op("pe", lambda l=l, r=r, k=k: nc.tensor.matmul(out, lhsT=l, rhs=r, start=(k == 1), stop=(k == n)),
                 reads=reads, writes=[self.Bbank[bank]], inc=(inc and k == n))
        for (o, l, r) in (extra or []):
            k += 1
            S.op("pe", lambda o=o, l=l, r=r, k=k: nc.tensor.matmul(o, lhsT=l, rhs=r, start=False, stop=(k == n)),
                 reads=reads, writes=[self.Bbank[bank]], inc=(inc and k == n))
        S.ub()

    def store_oT(self, j, q0, nq):
        nc, S = self.nc, self.S
        src = self.oTslot[:, q0 * 512:(q0 + nq) * 512].rearrange("p (q t) -> p q t", q=nq)
        dst = self.oTs[q0:q0 + nq, :, j * 512:(j + 1) * 512].rearrange("q p t -> p q t")
        S.dma(lambda: nc.sync.dma_start(out=dst, in_=src), reads=[self.BoTslot], writes=[self.BoTs[j]])

    def phase_sb(self):
        nc, S, A = self.nc, self.S, self.A
        SEQ, NKB, NS, NT = self.SEQ, self.NKB, self.NS, self.NT
        self.alloc_x(2)
        wsb = A.alloc("wsb", 8 * 1152, BF16)
        Bw = Buf("wsb")
        KT = A.alloc("KT", 3 * SEQ, BF16)
        V = A.alloc("V", NKB * 384, BF16)
        BKT = [Buf("KT%d" % t) for t in range(NT)]
        BV = [Buf("V%d" % t) for t in range(NT)]
        self.with_stage(lambda: self.load_w(self.w_in, 0, 8, 0, 1152, wsb, 1152, 0, Bw, gcol_off=0))
        mask = A.alloc("mask", 8 * 512, BF16)
        Bmask = Buf("mask")
        QTs = [A.alloc("QT%d" % i, 3 * 512, BF16) for i in range(2)]
        BQTs = [Buf("QT%d" % i) for i in range(2)]
        NE, NL, NA = 3, 3, 2
        eb = [A.alloc("e%d" % i, 512, F32) for i in range(NE)]
        Be = [Buf("e%d" % i) for i in range(NE)]
        Lb = [A.alloc("L%d" % i, 512, BF16) for i in range(NL)]
        BL = [Buf("L%d" % i) for i in range(NL)]
        wb = [A.alloc("w%d" % i, 512, F32) for i in range(2)]
        Bwb = [Buf("w%d" % i) for i in range(2)]
        Ab = [A.alloc("A%d" % i, 512, BF16) for i in range(NA)]
        BA = [Buf("A%d" % i) for i in range(NA)]
        Rb = [A.alloc("R%d" % i, 512, BF16) for i in range(2)]
        BR = [Buf("R%d" % i) for i in range(2)]
        hT = self.hT
        TP, ACC = 6, 7

        def evac(bank, ncols, dst, Bdst, eng="dve"):
            if eng == "dve":
                S.op("dve", lambda: nc.vector.tensor_copy(out=dst, in_=self.banks[bank][:, 0:ncols]),
                     reads=[self.Bbank[bank]], writes=[Bdst])
            else:
                S.op("act", lambda: nc.scalar.copy(out=dst, in_=self.banks[bank][:, 0:ncols]),
                     reads=[self.Bbank[bank]], writes=[Bdst])
            S.ub()

        def proj_kv(t):
            self.make_hT(self.xseq, t * 512, 4, TP)
            for pair in range(3):
                self.mm_group(ACC, 512, [(wsb[:, kc * 1152 + 384 + pair * 128:kc * 1152 + 384 + pair * 128 + 128],
                                          hT[:, kc * 512:(kc + 1) * 512]) for kc in range(8)], [Bw, self.BhT])
                evac(ACC, 512, KT[:, pair * SEQ + t * 512:pair * SEQ + (t + 1) * 512], BKT[t])
            for sub in range(4):
                self.mm_group(ACC, 384, [(hT[:, kc * 512 + sub * 128:kc * 512 + sub * 128 + 128],
                                          wsb[:, kc * 1152 + 768:kc * 1152 + 1152]) for kc in range(8)], [Bw, self.BhT])
                kb = t * 4 + sub
                evac(ACC, 384, V[:, kb * 384:(kb + 1) * 384], BV[t])

        def proj_q(j):
            QT, BQT = QTs[j % 2], BQTs[j % 2]
            self.make_hT(self.xown, j * 512, 4, TP)
            for pair in range(3):
                self.mm_group(ACC, 512, [(wsb[:, kc * 1152 + pair * 128:kc * 1152 + pair * 128 + 128],
                                          hT[:, kc * 512:(kc + 1) * 512]) for kc in range(8)], [Bw, self.BhT])
                evac(ACC, 512, QT[:, pair * 512:(pair + 1) * 512], BQT)

        def attn(j, feeder=None):
            QT, BQT = QTs[j % 2], BQTs[j % 2]
            nj = 8 * (j + 1)
            its = [(h, i) for h in range(6) for i in range(nj)]
            N = len(its)
            zb, Tb, Ob = [0, 1], [2, 3], [4, 5]

            def emit_z(s):
                h, i = its[s]
                kb = nj - 1 - i
                pair, hp = h // 2, h % 2
                bank = zb[s % 2]
                inwin = i < 8
                extra = None
                if inwin:
                    wi = 7 - i
                    extra = [(self.banks[bank][:, :], self.identb[:], mask[:, wi * 512:(wi + 1) * 512])]
                self.mm_group(bank, 512,
                              [(KT[hp * 64:hp * 64 + 64, pair * SEQ + kb * 128:pair * SEQ + kb * 128 + 128],
                                QT[hp * 64:hp * 64 + 64, pair * 512:(pair + 1) * 512])],
                              [BKT[kb // 4], BQT, Bmask, self.Bconst], extra=extra)

            def emit_eL(s):
                bank = zb[s % 2]
                e, L = eb[s % NE], Lb[s % NL]
                S.op("act", lambda: nc.scalar.activation(out=e[:], in_=self.banks[bank][:, :], func=AF.Exp, scale=0.125),
                     reads=[self.Bbank[bank]], writes=[Be[s % NE]])
                S.op("act", lambda: nc.scalar.activation(out=L[:], in_=e[:], func=AF.Ln, bias=self.cst[:, 1:2]),
                     reads=[Be[s % NE], self.Bconst], writes=[BL[s % NL]])

            def emit_T(s):
                h, i = its[s]
                bank = Tb[s % 2]
                pairs = [(self.trineg[:], Lb[s % NL][:])]
                rd = [BL[s % NL], self.Bconst]
                if i > 0:
                    pairs.append((self.negones[:], Rb[(i - 1) % 2][:]))
                    rd.append(BR[(i - 1) % 2])
                self.mm_group(bank, 512, pairs, rd)
                w = wb[s % 2]
                S.op("act", lambda: nc.scalar.activation(out=w[:], in_=self.banks[bank][:, :], func=AF.Exp),
                     reads=[self.Bbank[bank]], writes=[Bwb[s % 2]])
                Aa = Ab[s % NA]
                S.op("dve", lambda: nc.vector.tensor_tensor(out=Aa[:], in0=eb[s % NE][:], in1=w[:], op=ALU.mult),
                     reads=[Be[s % NE], Bwb[s % 2]], writes=[BA[s % NA]])
                if i < nj - 1:
                    if i == 0:
                        S.op("pool", lambda: nc.gpsimd.tensor_copy(out=Rb[0][:], in_=Lb[s % NL][:]),
                             reads=[BL[s % NL]], writes=[BR[0]])
                    else:
                        S.op("pool", lambda: nc.gpsimd.tensor_tensor(out=Rb[i % 2][:], in0=Rb[(i - 1) % 2][:],
                                                                     in1=Lb[s % NL][:], op=ALU.add),
                             reads=[BL[s % NL], BR[(i - 1) % 2]], writes=[BR[i % 2]])

            def emit_AV(s):
                h, i = its[s]
                kb = nj - 1 - i
                pair, hp = h // 2, h % 2
                bank = Ob[h % 2]
                S.op("pe", lambda: nc.tensor.matmul(self.banks[bank][:, :],
                                                    lhsT=V[:, kb * 384 + pair * 128:kb * 384 + pair * 128 + 128],
                                                    rhs=Ab[s % NA][:], start=(i == 0), stop=(i == nj - 1)),
                     reads=[BV[kb // 4], BA[s % NA]], writes=[self.Bbank[bank]], inc=True)
                if i == nj - 1:
                    S.op("dve", lambda: nc.vector.tensor_copy(
                        out=self.oTslot[hp * 64:hp * 64 + 64, pair * 512:(pair + 1) * 512],
                        in_=self.banks[bank][hp * 64:hp * 64 + 64, :]),
                        reads=[self.Bbank[bank]], writes=[self.BoTslot])

            emit_z(0)
            for s in range(N + 2):
                if s + 1 < N:
                    emit_z(s + 1)
                if s < N:
                    emit_eL(s)
                if 1 <= s <= N:
                    emit_T(s - 1)
                if s >= 2:
                    emit_AV(s - 2)
                if feeder is not None:
                    feeder.step()

        proj_kv(0)
        proj_kv(1)
        proj_q(0)
        for j in range(NS):
            S.dma(lambda j=j: nc.sync.dma_start(out=mask[:], in_=self.sbmask[j % 2]), writes=[Bmask])
            feeder = None
            if j + 1 < NS:
                S.start_defer()
                proj_kv(2 * j + 2)
                proj_kv(2 * j + 3)
                proj_q(j + 1)
                feeder = Feeder(S, S.end_defer(), 48 * (j + 1))
            attn(j, feeder)
            if feeder is not None:
                feeder.drain()
            self.store_oT(j, 0, 3)

    def phase_mo(self):
        nc, S, A = self.nc, self.S, self.A
        SEQ, NKB, NS, NT = self.SEQ, self.NKB, self.NS, self.NT
        self.alloc_x(2)
        WW = 1408
        wmo = A.alloc("wmo", 8 * WW, BF16)
        Bw = Buf("wmo")
        KT = A.alloc("KT", 3 * SEQ, BF16)
        V = A.alloc("V", NKB * 384, BF16)
        BKT = [Buf("KT%d" % t) for t in range(NT)]
        BV = [Buf("V%d" % t) for t in range(NT)]
        ksum = A.alloc("ksum", 6 * 32, F32)
        Bks = Buf("ksum")
        S.op("pool", lambda: nc.gpsimd.memset(ksum[:], 0.0), writes=[Bks])
        memKT = A.alloc("memKT", 2 * 256, BF16)
        memV = A.alloc("memV", 2 * 256, BF16)
        Bmem = Buf("memkv")

        wmem = V
        Bwm = Buf("wmem")

        def loadw():
            self.load_w(self.w_in, 0, 8, 1152, 1408, wmo, WW, 0, Bw, gcol_off=0)
            self.load_w(self.w_mem_kv, 0, 8, 0, 512, wmem, 512, 0, Bwm, gcol_off=8)
        self.with_stage(loadw)

        mask = A.alloc("mask", 8 * 512, BF16)
        Bmask = Buf("mask")
        QTz = A.alloc("QTz", 6 * 512, BF16)
        Qf = A.alloc("Qf", 3 * 512, F32)
        QTmz = A.alloc("QTmz", 4 * 512, BF16)
        BQf = Buf("Qf")
        BQTz = [Buf("QTz%d" % h) for h in range(6)]
        BQTmz = [Buf("QTmz%d" % h) for h in range(4)]
        S.op("pool", lambda: nc.gpsimd.memset(QTz[:], 0.0), writes=BQTz)
        S.op("pool", lambda: nc.gpsimd.memset(QTmz[:], 0.0), writes=BQTmz)
        pb = A.alloc("pb", 4 * 192, F32)
        Bpb = Buf("pb")
        MBT = A.alloc("MBT", 1024, BF16)
        BMBT = Buf("MBT")
        NP = 2
        pbuf = [A.alloc("p%d" % i, 512, BF16) for i in range(NP)]
        Bp = [Buf("p%d" % i) for i in range(NP)]
        cs = A.alloc("cs", 1024, F32)
        Bcs = Buf("cs")
        rawsb = A.alloc("rawsb", 512, F32)
        rstd = A.alloc("rstd", 512, F32)
        ta = A.alloc("ta", 512, F32)
        tb = A.alloc("tb", 512, F32)
        sq = tb
        Braw, Brstd, Bta, Btb = Buf("rawsb"), Buf("rstd"), Buf("ta"), Buf("tb")
        Bsq = Btb
        Gp = A.alloc("Gp", 192, F32)
        sel = A.alloc("sel", 192, F32)
        MBb = A.alloc("MBb", 192, BF16)
        mx = A.alloc("mx", 56, F32)
        BGp, Bsel, BMBb, Bmx = Buf("Gp"), Buf("sel"), Buf("MBb"), Buf("mx")
        rec = A.alloc("rec", 512, F32)
        Brec = Buf("rec")
        hT = self.hT
        TP, ACC, MSB, ROTB = 6, 7, 5, 4

        def norm_qk(ncols, gi, rope, dst_bf, Bdst, dst_f=None, Bdstf=None):
            acc = self.banks[ACC][:, 0:ncols]
            gA = self.gh[:, 2 * gi:2 * gi + 1]
            gB = self.gh[:, 2 * gi + 1:2 * gi + 2]
            S.op("act", lambda: nc.scalar.copy(out=rawsb[:, 0:ncols], in_=acc), reads=[self.Bbank[ACC]], writes=[Braw])
            S.op("act", lambda: nc.scalar.activation(out=sq[:, 0:ncols], in_=acc, func=AF.Square),
                 reads=[self.Bbank[ACC]], writes=[Bsq])
            S.ub()
            self.mm_group(MSB, ncols, [(self.blk64[:], sq[:, 0:ncols])], [Bsq, self.Bconst])
            if rope and "norot" not in VAR:
                self.mm_group(ROTB, ncols, [(self.rot[:], rawsb[:, 0:ncols])], [Braw, self.Bconst])
            S.op("act", lambda: nc.scalar.activation(out=rstd[:, 0:ncols], in_=self.banks[MSB][:, 0:ncols], func=AF.Ln,
                                                     bias=self.cst[:, 0:1]), reads=[self.Bbank[MSB], self.Bconst],
                 writes=[Brstd])
            S.op("act", lambda: nc.scalar.activation(out=rstd[:, 0:ncols], in_=rstd[:, 0:ncols], func=AF.Exp, scale=-0.5),
                 reads=[Brstd], writes=[Brstd])
            S.ub()
            if rope:
                S.op("dve", lambda: nc.vector.scalar_tensor_tensor(out=ta[:, 0:ncols], in0=rawsb[:, 0:ncols], scalar=gA,
                                                                   in1=cs[:, 0:ncols], op0=ALU.mult, op1=ALU.mult),
                     reads=[Braw, Bcs, self.Bg], writes=[Bta])
                S.op("dve", lambda: nc.vector.scalar_tensor_tensor(out=tb[:, 0:ncols], in0=self.banks[ROTB][:, 0:ncols],
                                                                   scalar=gB, in1=cs[:, 512:512 + ncols],
                                                                   op0=ALU.mult, op1=ALU.mult),
                     reads=[self.Bbank[ROTB], Bcs, self.Bg], writes=[Btb])
                S.ub()
                pe_ = "dve" if "nopool" in VAR else "pool"
                pen_ = nc.vector if "nopool" in VAR else nc.gpsimd
                S.op(pe_, lambda: pen_.tensor_tensor(out=ta[:, 0:ncols], in0=ta[:, 0:ncols], in1=tb[:, 0:ncols],
                                                     op=ALU.add), reads=[Bta, Btb], writes=[Bta])
                S.ub()
                fin = dst_f if dst_f is not None else tb[:, 0:ncols]
                Bfin = Bdstf if dst_f is not None else Btb
                S.op("dve", lambda: nc.vector.tensor_tensor(out=fin, in0=ta[:, 0:ncols], in1=rstd[:, 0:ncols], op=ALU.mult),
                     reads=[Bta, Brstd], writes=[Bfin])
            else:
                fin, Bfin = ta[:, 0:ncols], Bta
                S.op("dve", lambda: nc.vector.scalar_tensor_tensor(out=fin, in0=rawsb[:, 0:ncols], scalar=gA,
                                                                   in1=rstd[:, 0:ncols], op0=ALU.mult, op1=ALU.mult),
                     reads=[Braw, Brstd, self.Bg], writes=[Bfin])
            S.ub()
            if callable(dst_bf):
                for hp in range(2):
                    S.op("pool", lambda hp=hp: nc.gpsimd.tensor_copy(out=dst_bf(hp), in_=fin[hp * 64:hp * 64 + 64, :]),
                         reads=[Bfin], writes=[Bdst(hp)])
            else:
                S.op("pool", lambda: nc.gpsimd.tensor_copy(out=dst_bf, in_=fin), reads=[Bfin], writes=[Bdst])
            S.ub()
            return fin, Bfin

        def evac(bank, ncols, dst, Bdst):
            S.op("dve", lambda: nc.vector.tensor_copy(out=dst, in_=self.banks[bank][:, 0:ncols]),
                 reads=[self.Bbank[bank]], writes=[Bdst])
            S.ub()

        self.make_hT(self.mem, 0, 2, TP)
        for mp in range(2):
            self.mm_group(ACC, 256, [(wmem[:, kc * 512 + mp * 128:kc * 512 + mp * 128 + 128],
                                      hT[:, kc * 512:kc * 512 + 256]) for kc in range(8)], [Bwm, self.BhT])
            norm_qk(256, 3, False, memKT[:, mp * 256:(mp + 1) * 256], Bmem)
        for sub in range(2):
            self.mm_group(ACC, 256, [(hT[:, kc * 512 + sub * 128:kc * 512 + sub * 128 + 128],
                                      wmem[:, kc * 512 + 256:kc * 512 + 512]) for kc in range(8)], [Bwm, self.BhT])
            evac(ACC, 256, memV[:, sub * 256:(sub + 1) * 256], Bmem)

        S.barrier()

        def proj_kv(t):
            self.make_hT(self.xseq, t * 512, 4, TP)
            S.dma(lambda: nc.sync.dma_start(out=cs[:, 0:512], in_=self.cosK[:, t * 512:(t + 1) * 512]), writes=[Bcs])
            S.dma(lambda: nc.sync.dma_start(out=cs[:, 512:1024], in_=self.sinK[:, t * 512:(t + 1) * 512]), writes=[Bcs])
            for pair in range(3):
                self.mm_group(ACC, 512, [(wmo[:, kc * WW + 384 + pair * 128:kc * WW + 384 + pair * 128 + 128],
                                          hT[:, kc * 512:(kc + 1) * 512]) for kc in range(8)], [Bw, self.BhT])
                fin, Bfin = norm_qk(512, 1, True, KT[:, pair * SEQ + t * 512:pair * SEQ + (t + 1) * 512], BKT[t])
                for hp in range(2):
                    h = 2 * pair + hp
                    S.op("dve", lambda h=h, hp=hp, fin=fin: nc.vector.tensor_reduce(
                        out=ksum[hp * 64:hp * 64 + 64, h * 32 + 2 * t:h * 32 + 2 * t + 2],
                        in_=fin[hp * 64:hp * 64 + 64, :].rearrange("p (b k) -> p b k", b=2), axis=AX.X, op=ALU.add),
                        reads=[Bfin], writes=[Bks])
                S.ub()
            for sub in range(4):
                self.mm_group(ACC, 384, [(hT[:, kc * 512 + sub * 128:kc * 512 + sub * 128 + 128],
                                          wmo[:, kc * WW + 768:kc * WW + 1152]) for kc in range(8)], [Bw, self.BhT])
                kb = t * 4 + sub
                evac(ACC, 384, V[:, kb * 384:(kb + 1) * 384], BV[t])

        def proj_q(j):
            self.make_hT(self.xown, j * 512, 4, TP)
            S.dma(lambda: nc.sync.dma_start(out=cs[:, 0:512], in_=self.cosQ[:, j * 512:(j + 1) * 512]), writes=[Bcs])
            S.dma(lambda: nc.sync.dma_start(out=cs[:, 512:1024], in_=self.sinQ[:, j * 512:(j + 1) * 512]), writes=[Bcs])
            S.gate(0)
            for mp in range(2):
                self.mm_group(ACC, 512, [(wmo[:, kc * WW + 1152 + mp * 128:kc * WW + 1152 + mp * 128 + 128],
                                          hT[:, kc * 512:(kc + 1) * 512]) for kc in range(8)], [Bw, self.BhT])
                norm_qk(512, 2, False,
                        lambda hp, mp=mp: QTmz[hp * 64:hp * 64 + 64, (2 * mp + hp) * 512:(2 * mp + hp + 1) * 512],
                        lambda hp, mp=mp: BQTmz[2 * mp + hp])
            for pair in range(3):
                S.gate((2 * pair + 2) * 8 * j)
                self.mm_group(ACC, 512, [(wmo[:, kc * WW + pair * 128:kc * WW + pair * 128 + 128],
                                          hT[:, kc * 512:(kc + 1) * 512]) for kc in range(8)], [Bw, self.BhT])
                norm_qk(512, 0, True,
                        lambda hp, pair=pair: QTz[hp * 64:hp * 64 + 64, (2 * pair + hp) * 512:(2 * pair + hp + 1) * 512],
                        lambda hp, pair=pair: BQTz[2 * pair + hp], dst_f=Qf[:, pair * 512:(pair + 1) * 512], Bdstf=BQf)

        def gate_select(j):
            GB = 4
            tpb = self.banks[TP][:].bitcast(BF16)
            S.dma(lambda: nc.sync.dma_start(out=pb[:], in_=self.pastb[:, j * 768:(j + 1) * 768]), writes=[Bpb])
            for sub in range(4):
                for h in range(6):
                    pair, hp = h // 2, h % 2
                    S.op("pe", lambda h=h, pair=pair, hp=hp: nc.tensor.matmul(
                        self.banks[GB][:, h * 32:(h + 1) * 32],
                        lhsT=Qf[:, pair * 512 + sub * 128:pair * 512 + sub * 128 + 128],
                        rhs=ksum[:, h * 32:(h + 1) * 32], start=True, stop=True),
                        reads=[BQf, Bks], writes=[self.Bbank[GB]], inc=(h == 5))
                S.op("dve", lambda: nc.vector.tensor_tensor(out=Gp[:], in0=self.banks[GB][:, 0:192],
                                                            in1=pb[:, sub * 192:(sub + 1) * 192], op=ALU.add),
                     reads=[self.Bbank[GB], Bpb], writes=[BGp])
                for h in range(6):
                    S.op("dve", lambda h=h: nc.vector.max(out=mx[:, h * 8:(h + 1) * 8], in_=Gp[:, h * 32:(h + 1) * 32]),
                         reads=[BGp], writes=[Bmx])
                S.op("dve", lambda: nc.vector.tensor_scalar(
                    out=mx[:, 48:54], in0=mx[:, 0:48].rearrange("p (h e) -> p h e", e=8)[:, :, 3],
                    scalar1=-1e29, scalar2=None, op0=ALU.max), reads=[Bmx], writes=[Bmx])
                for h in range(6):
                    S.op("dve", lambda h=h: nc.vector.tensor_scalar(
                        out=sel[:, h * 32:(h + 1) * 32], in0=Gp[:, h * 32:(h + 1) * 32],
                        scalar1=mx[:, 48 + h:49 + h], scalar2=None, op0=ALU.is_ge), reads=[BGp, Bmx], writes=[Bsel])
                S.op("dve", lambda: nc.vector.tensor_scalar(out=MBb[:], in0=sel[:], scalar1=-1.0, scalar2=-NEG,
                                                            op0=ALU.add, op1=ALU.mult), reads=[Bsel], writes=[BMBb])
                for g in range(2):
                    S.op("pe", lambda g=g: nc.tensor.transpose(
                        out=tpb[0:96, g * 512 + sub * 128:g * 512 + sub * 128 + 128],
                        in_=MBb[:, g * 96:(g + 1) * 96], identity=self.identb[:]),
                        reads=[BMBb, self.Bconst], writes=[self.Bbank[TP]], inc=(g == 1))
            S.op("dve", lambda: nc.vector.tensor_copy(out=MBT[0:96, :], in_=tpb[0:96, :]),
                 reads=[self.Bbank[TP]], writes=[BMBT])

        def attn(j, kind, feeder=None):
            if kind == "mo":
                nj = 8 * (j + 1)
                nh = 6
            else:
                nj = 2
                nh = 4
            its = [(h, kb) for h in range(nh) for kb in range(nj)]
            N = len(its)
            zb, Ob, Lbk = [0, 1], [2, 2], [3, 3]

            def emit_z(s):
                h, kb = its[s]
                pair, hp = h // 2, h % 2
                bank = zb[s % 2]
                out = self.banks[bank][:, :]
                if kind == "mo":
                    pairs = [(KT[:, pair * SEQ + kb * 128:pair * SEQ + kb * 128 + 128],
                              QTz[:, h * 512:(h + 1) * 512])]
                    r3 = 32 * (h % 3)
                    extra = [(out, self.esel[r3:r3 + 32, (kb // 2) * 128:(kb // 2) * 128 + 128],
                              MBT[r3:r3 + 32, (h // 3) * 512:(h // 3) * 512 + 512])]
                    if kb >= nj - 8:
                        wi = kb - (nj - 8)
                        extra.append((out, self.identb[:], mask[:, wi * 512:(wi + 1) * 512]))
                    self.mm_group(bank, 512, pairs, [BKT[kb // 4], BQTz[h], BMBT, Bmask, self.Bconst], extra=extra)
                else:
                    pairs = [(memKT[:, pair * 256 + kb * 128:pair * 256 + kb * 128 + 128],
                              QTmz[:, h * 512:(h + 1) * 512])]
                    self.mm_group(bank, 512, pairs, [Bmem, BQTmz[h]])

            def emit_p(s):
                bank = zb[s % 2]
                p = pbuf[s % NP]
                S.op("act", lambda: nc.scalar.activation(out=p[:], in_=self.banks[bank][:, :], func=AF.Exp, scale=0.125),
                     reads=[self.Bbank[bank]], writes=[Bp[s % NP]])

            def emit_AV(s):
                h, kb = its[s]
                pair, hp = h // 2, h % 2
                ob, lb = Ob[h % 2], Lbk[h % 2]
                p = pbuf[s % NP]
                if kind == "mo":
                    vap = V[:, kb * 384 + pair * 128:kb * 384 + pair * 128 + 128]
                    rd = [BV[kb // 4], Bp[s % NP]]
                    q = 3 + pair
                else:
                    vap = memV[:, kb * 256 + pair * 128:kb * 256 + pair * 128 + 128]
                    rd = [Bmem, Bp[s % NP]]
                    q = 6 + pair
                last = (kb == nj - 1)
                S.op("pe", lambda: nc.tensor.matmul(self.banks[ob][:, :], lhsT=vap, rhs=p[:], start=(kb == 0), stop=last),
                     reads=rd, writes=[self.Bbank[ob]], inc=False)
                S.op("pe", lambda: nc.tensor.matmul(self.banks[lb][:, :], lhsT=self.onesb[:], rhs=p[:], start=(kb == 0),
                                                    stop=last),
                     reads=[Bp[s % NP], self.Bconst], writes=[self.Bbank[lb]], inc=True)
                if last:
                    r0 = hp * 64
                    S.op("dve", lambda: nc.vector.reciprocal(out=rec[r0:r0 + 64, :], in_=self.banks[lb][r0:r0 + 64, :]),
                         reads=[self.Bbank[lb]], writes=[Brec])
                    S.op("dve", lambda: nc.vector.tensor_tensor(out=self.oTslot[r0:r0 + 64, q * 512:(q + 1) * 512],
                                                                in0=self.banks[ob][r0:r0 + 64, :], in1=rec[r0:r0 + 64, :],
                                                                op=ALU.mult),
                         reads=[self.Bbank[ob], Brec], writes=[self.BoTslot])

            emit_z(0)
            for s in range(N + 1):
                if s + 1 < N:
                    emit_z(s + 1)
                if s < N:
                    emit_p(s)
                if s >= 1:
                    emit_AV(s - 1)
                if feeder is not None:
                    feeder.step(s + 1 if kind == "mo" else -1)

        if STRESS:
            for rep in range(STRESS):
                proj_kv(rep % NT)
            return
        proj_kv(0)
        proj_kv(1)
        proj_q(0)
        for j in range(NS):
            S.dma(lambda j=j: nc.sync.dma_start(out=mask[:], in_=self.momask[j % 2]), writes=[Bmask])
            gate_select(j)
            feeder = None
            if j + 1 < NS:
                S.start_defer()
                proj_kv(2 * j + 2)
                proj_kv(2 * j + 3)
                proj_q(j + 1)
                feeder = Feeder(S, S.end_defer(), 48 * (j + 1) + 8)
            attn(j, "me", feeder)
            attn(j, "mo", feeder)
            if feeder is not None:
                feeder.drain()
            self.store_oT(j, 3, 5)

    def phase_f1(self):
        nc, S, A = self.nc, self.S, self.A
        NS = self.NS
        self.alloc_x(4)
        wg = A.alloc("wg", 8 * 3072, BF16)
        wup = A.alloc("wup", 8 * 1024, BF16)
        wout = A.alloc("wout", 8 * 1024, BF16)
        Bwg, Bwup, Bwout = Buf("wg"), Buf("wup"), Buf("wout")

        def loadw():
            self.load_w(self.w_in, 0, 8, 2560, 3072, wg, 3072, 0, Bwg, gcol_off=0)
            self.load_w(self.w_up_sb, 0, 3, 0, 1024, wup, 1024, 0, Bwup)
            self.load_w(self.w_up_moba, 0, 3, 0, 1024, wup, 1024, 3 * 1024, Bwup)
            self.load_w(self.w_up_mem, 0, 2, 0, 1024, wup, 1024, 6 * 1024, Bwup)
            self.load_w(self.w_out, 0, 8, 0, 1024, wout, 1024, 0, Bwout)
        self.with_stage(loadw)
        oT = A.alloc("oT", 8 * 512, BF16)
        BoT = Buf("oT")
        sg = [A.alloc("sg%d" % i, 512, F32) for i in range(3)]
        tt = [A.alloc("tt%d" % i, 512, F32) for i in range(3)]
        Bsg = [Buf("sg%d" % i) for i in range(3)]
        Btt = [Buf("tt%d" % i) for i in range(3)]
        mixb = A.alloc("mixb", 8 * 512, BF16)
        Bmix = [Buf("mix%d" % i) for i in range(8)]
        x1c = [A.alloc("x1c%d" % i, 512, F32) for i in range(2)]
        Bx1c = [Buf("x1c%d" % i) for i in range(2)]
        hT = self.hT
        TP = 7
        rr = [0]

        def nb():
            b = rr[0] % 7
            rr[0] += 1
            return b

        branches = [(0, [0, 1, 2]), (1, [3, 4, 5]), (2, [6, 7])]
        for j in range(NS):
            src = self.oTs[:, :, j * 512:(j + 1) * 512].rearrange("q p t -> p q t")
            S.dma(lambda src=src: nc.sync.dma_start(out=oT[:].rearrange("p (q t) -> p q t", q=8), in_=src),
                  reads=[self.BoTs[j]], writes=[BoT])
            self.make_hT(self.xown, j * 512, 4, TP, keep_x=True)
            for c in range(8):
                for (b, qs) in branches:
                    gbk = nb()
                    self.mm_group(gbk, 512, [(wg[:, kc * 3072 + b * 1024 + c * 128:kc * 3072 + b * 1024 + c * 128 + 128],
                                              hT[:, kc * 512:(kc + 1) * 512]) for kc in range(8)], [Bwg, self.BhT])
                    ubk = nb()
                    self.mm_group(ubk, 512, [(wup[:, q * 1024 + c * 128:q * 1024 + c * 128 + 128],
                                              oT[:, q * 512:(q + 1) * 512]) for q in qs], [Bwup, BoT])
                    S.op("act", lambda b=b, gbk=gbk: nc.scalar.activation(out=sg[b][:], in_=self.banks[gbk][:, :],
                                                                          func=AF.Sigmoid),
                         reads=[self.Bbank[gbk]], writes=[Bsg[b]])
                    S.op("dve", lambda b=b, ubk=ubk: nc.vector.tensor_tensor(out=tt[b][:], in0=sg[b][:],
                                                                             in1=self.banks[ubk][:, :], op=ALU.mult),
                         reads=[Bsg[b], self.Bbank[ubk]], writes=[Btt[b]])
                S.op("pool", lambda: nc.gpsimd.tensor_tensor(out=tt[0][:], in0=tt[0][:], in1=tt[1][:], op=ALU.add),
                     reads=[Btt[0], Btt[1]], writes=[Btt[0]])
                S.op("pool", lambda c=c: nc.gpsimd.tensor_tensor(out=mixb[:, c * 512:(c + 1) * 512], in0=tt[0][:],
                                                                 in1=tt[2][:], op=ALU.add),
                     reads=[Btt[0], Btt[2]], writes=[Bmix[c]])
            for c in range(8):
                bk = nb()
                extra = [(self.banks[bk][:, sub * 128:(sub + 1) * 128],
                          self.xt[:, sub * D + c * 128:sub * D + c * 128 + 128], self.ident[:]) for sub in range(4)]
                self.mm_group(bk, 512, [(wout[:, k * 1024 + c * 128:k * 1024 + c * 128 + 128],
                                         mixb[:, k * 512:(k + 1) * 512]) for k in range(8)],
                              [Bwout, self.Bconst] + Bmix + self.Bxt, extra=extra)
                xc = x1c[c % 2]
                S.op("act", lambda bk=bk, xc=xc: nc.scalar.copy(out=xc[:], in_=self.banks[bk][:, :]),
                     reads=[self.Bbank[bk]], writes=[Bx1c[c % 2]])
                S.dma(lambda xc=xc, c=c, j=j: nc.sync.dma_start(out=self.x1s[j * 8 + c], in_=xc[:]),
                      reads=[Bx1c[c % 2]], writes=[self.Bx1s[j]])

    def phase_f2(self):
        nc, S, A = self.nc, self.S, self.A
        NS = self.NS
        NF = DFF // 128
        wfi = A.alloc("wfi", 8 * 2 * DFF, BF16)
        wfd = A.alloc("wfd", NF * 1024, BF16)
        Bwfi, Bwfd = Buf("wfi"), Buf("wfd")

        def loadw():
            self.load_w(self.w_ffn_in, 0, 8, 0, 2 * DFF, wfi, 2 * DFF, 0, Bwfi, gcol_off=16)
            self.load_w(self.w_ffn_down, 0, NF, 0, 1024, wfd, 1024, 0, Bwfd)
        self.with_stage(loadw)
        x1T = A.alloc("x1T", 8 * 512, F32)
        Bx1 = [Buf("x1T%d" % c) for c in range(8)]
        sq = [A.alloc("sq%d" % i, 512, F32) for i in range(2)]
        Bsq = [Buf("sq%d" % i) for i in range(2)]
        rstd = A.alloc("rstd", 512, F32)
        Brstd = Buf("rstd")
        h2T = A.alloc("h2T", 8 * 512, BF16)
        Bh2 = Buf("h2T")
        ffT = A.alloc("ffT", NF * 512, BF16)
        Bff = [Buf("ff%d" % f) for f in range(NF)]
        sl = [A.alloc("sl%d" % i, 512, F32) for i in range(2)]
        Bsl = [Buf("sl%d" % i) for i in range(2)]
        ot = A.alloc("ot", 1024, F32)
        Both = [Buf("ot0"), Buf("ot1")]
        SSB = 7
        rr = [0]

        def nb():
            b = rr[0] % 7
            rr[0] += 1
            return b

        for j in range(NS):
            for c in range(8):
                S.dma(lambda c=c, j=j: nc.sync.dma_start(out=x1T[:, c * 512:(c + 1) * 512], in_=self.x1s[j * 8 + c]),
                      reads=[self.Bx1s[j]], writes=[Bx1[c]])
            for c in range(8):
                S.op("act", lambda c=c: nc.scalar.activation(out=sq[c % 2][:], in_=x1T[:, c * 512:(c + 1) * 512],
                                                             func=AF.Square), reads=[Bx1[c]], writes=[Bsq[c % 2]])
                S.op("pe", lambda c=c: nc.tensor.matmul(self.banks[SSB][:, :], lhsT=self.onesf[:], rhs=sq[c % 2][:],
                                                        start=(c == 0), stop=(c == 7)),
                     reads=[Bsq[c % 2], self.Bconst], writes=[self.Bbank[SSB]], inc=True)
            S.op("act", lambda: nc.scalar.activation(out=rstd[:], in_=self.banks[SSB][:, :], func=AF.Ln,
                                                     bias=self.cst[:, 0:1], scale=1.0 / D),
                 reads=[self.Bbank[SSB], self.Bconst], writes=[Brstd])
            S.op("act", lambda: nc.scalar.activation(out=rstd[:], in_=rstd[:], func=AF.Exp, scale=-0.5),
                 reads=[Brstd], writes=[Brstd])
            for c in range(8):
                S.op("dve", lambda c=c: nc.vector.tensor_tensor(out=h2T[:, c * 512:(c + 1) * 512],
                                                                in0=x1T[:, c * 512:(c + 1) * 512], in1=rstd[:],
                                                                op=ALU.mult), reads=[Bx1[c], Brstd], writes=[Bh2])
            for f in range(NF):
                gbk = nb()
                self.mm_group(gbk, 512, [(wfi[:, kc * 2 * DFF + f * 128:kc * 2 * DFF + f * 128 + 128],
                                          h2T[:, kc * 512:(kc + 1) * 512]) for kc in range(8)], [Bwfi, Bh2])
                ubk = nb()
                self.mm_group(ubk, 512, [(wfi[:, kc * 2 * DFF + DFF + f * 128:kc * 2 * DFF + DFF + f * 128 + 128],
                                          h2T[:, kc * 512:(kc + 1) * 512]) for kc in range(8)], [Bwfi, Bh2])
                S.op("act", lambda f=f, gbk=gbk: nc.scalar.activation(out=sl[f % 2][:], in_=self.banks[gbk][:, :],
                                                                      func=AF.Silu),
                     reads=[self.Bbank[gbk]], writes=[Bsl[f % 2]])
                S.op("dve", lambda f=f, ubk=ubk: nc.vector.tensor_tensor(out=ffT[:, f * 512:(f + 1) * 512], in0=sl[f % 2][:],
                                                                         in1=self.banks[ubk][:, :], op=ALU.mult),
                     reads=[Bsl[f % 2], self.Bbank[ubk]], writes=[Bff[f]])
            for half in range(2):
                for sub in range(4):
                    bk = nb()
                    extra = [(self.banks[bk][:, (c % 4) * 128:(c % 4) * 128 + 128],
                              x1T[:, c * 512 + sub * 128:c * 512 + sub * 128 + 128], self.ident[:])
                             for c in range(half * 4, half * 4 + 4)]
                    self.mm_group(bk, 512, [(ffT[:, f * 512 + sub * 128:f * 512 + sub * 128 + 128],
                                             wfd[:, f * 1024 + half * 512:f * 1024 + half * 512 + 512]) for f in range(NF)],
                                  [Bwfd, self.Bconst] + Bff + Bx1[half * 4:half * 4 + 4], extra=extra)
                    k = sub % 2
                    oh = ot[:, k * 512:(k + 1) * 512]
                    if k == 0:
                        S.op("act", lambda bk=bk, oh=oh: nc.scalar.copy(out=oh, in_=self.banks[bk][:, :]),
                             reads=[self.Bbank[bk]], writes=[Both[k]])
                    else:
                        S.op("dve", lambda bk=bk, oh=oh: nc.vector.tensor_copy(out=oh, in_=self.banks[bk][:, :]),
                             reads=[self.Bbank[bk]], writes=[Both[k]])
                    r0 = j * 512 + sub * 128
                    S.dma(lambda r0=r0, oh=oh, half=half: nc.sync.dma_start(
                        out=self.out[r0:r0 + 128, half * 512:(half + 1) * 512], in_=oh), reads=[Both[k]])


def own_tiles(role, NS):
    tiles = []
    for j in range(NS):
        early = (j % 2 == 0) if role == 0 else (j % 2 == 1)
        tiles.append(2 * j if early else 2 * j + 1)
    return tiles


def host_constants(role, NT):
    NS = NT // 2
    SEQ = NT * 512
    OWN = NS * 512
    tiles = own_tiles(role, NS)
    half = HD // 2
    inv_freq = (np.float32(10000.0) ** (-(np.arange(half, dtype=np.float32) * np.float32(2.0) / np.float32(HD)))).astype(np.float32)
    pos = np.arange(SEQ, dtype=np.float32)
    ang = (pos[:, None] * inv_freq[None, :]).astype(np.float32)
    cos = np.cos(ang).astype(np.float32)
    sin = np.sin(ang).astype(np.float32)
    fidx = np.arange(128) % 32
    cosK = np.ascontiguousarray(cos[:, fidx].T)
    sinK = np.ascontiguousarray(sin[:, fidx].T)
    own_pos = np.concatenate([np.arange(t * 512, (t + 1) * 512) for t in tiles])
    cosQ = np.ascontiguousarray(cosK[:, own_pos])
    sinQ = np.ascontiguousarray(sinK[:, own_pos])
    qblk = own_pos // 256
    n = np.arange(32)
    pb = np.where(n[None, :] < qblk[:, None], 0.0,
                  np.where(n[None, :] == qblk[:, None], 1e30, -1e30)).astype(np.float32)
    def lay(a):
        a = np.tile(a[:, None, :], (1, 6, 1)).reshape(OWN // 128, 128, 192)
        return np.ascontiguousarray(a.transpose(1, 0, 2).reshape(128, -1))
    pastb = lay(pb)
    sbmask = np.zeros((2, 128, 8, 512), np.float32)
    momask = np.zeros((2, 128, 8, 512), np.float32)
    s_idx = np.arange(128)[:, None]
    q_idx = np.arange(512)[None, :]
    for par in range(2):
        early = (par == 0) if role == 0 else (par == 1)
        own_off = 0 if early else 512
        for wi in range(8):
            kpos = wi * 128 + s_idx
            qpos = own_off + q_idx
            sbmask[par, :, wi, :] = np.where(kpos < qpos, 0.0, NEG)
            same_blk = (kpos // 256) == (qpos // 256)
            momask[par, :, wi, :] = np.where(same_blk & (kpos > qpos), NEG, 0.0)
    bf = ml_dtypes.bfloat16
    return dict(cosK=cosK, sinK=sinK, cosQ=cosQ, sinQ=sinQ, pastb=pastb,
                sbmask=sbmask.reshape(2, 128, 4096).astype(bf), momask=momask.reshape(2, 128, 4096).astype(bf))


_CACHE = {}


STOP_AFTER = 4
NSTAGE = 4
MO_STOP = 5
VAR = set()
STRESS = 0


def _program(NT, debug=False):
    key = (NT, debug, STOP_AFTER, MO_STOP, tuple(sorted(VAR)), STRESS)
    if key not in _CACHE:
        b = Builder(NT, debug, STOP_AFTER)
        nc = b.build()
        _CACHE[key] = nc
    return _CACHE[key]


WNAMES = ["w_in", "w_mem_kv", "w_up_sb", "w_up_moba", "w_up_mem", "w_out", "w_ffn_in", "w_ffn_down",
          "mix_norm_g", "mem_norm_g", "ffn_norm_g", "moba_q_norm_g", "moba_k_norm_g", "mem_q_norm_g", "mem_k_norm_g"]


def run(inputs, debug=False, trace=False):
    x = np.asarray(inputs["x"], dtype=np.float32)
    mem = np.asarray(inputs["mem"], dtype=np.float32)
    B, SEQ, _ = x.shape
    NT = SEQ // 512
    NS = NT // 2
    assert B * 2 == N_CORES
    nc = _program(NT, debug)
    wmaps = {k: np.ascontiguousarray(np.asarray(inputs[k], dtype=np.float32)[0]) for k in WNAMES}
    consts = [host_constants(r, NT) for r in range(2)]
    in_maps = []
    for core in range(N_CORES):
        b, role = core // 2, core % 2
        tiles = own_tiles(role, NS)
        xown = np.concatenate([x[b, t * 512:(t + 1) * 512] for t in tiles], axis=0)
        m = dict(wmaps)
        m.update(consts[role])
        m["xseq"] = np.ascontiguousarray(x[b])
        m["xown"] = np.ascontiguousarray(xown)
        m["mem"] = np.ascontiguousarray(mem[b])
        in_maps.append(m)
    res = run_bass_kernel_spmd(nc, in_maps, core_ids=list(range(N_CORES)), **({"trace": True} if trace else {}))
    out = np.empty((B, SEQ, D), np.float32)
    for core in range(N_CORES):
        b, role = core // 2, core % 2
        tiles = own_tiles(role, NS)
        o = np.asarray(res.results[core]["out"])
        for j, t in enumerate(tiles):
            out[b, t * 512:(t + 1) * 512] = o[j * 512:(j + 1) * 512]
    return out, res


def kernel(**inputs):
    out, _ = run(inputs)
    return out
```
